# Optimizing a Trainium2 kernel written in Bass

```python
import math
import jax, jax.numpy as jnp
from jax import lax
import numpy as np

D_MODEL = 1024
BATCH = 2
SEQ = 16384
DEPTH = 1
DEC_BATCH = 1
DEC_SEQ = 16384
PAST_LEN = 128

GRID_W = 64
EPS = 1e-6
SSD_EXPAND = 2
D_INNER = SSD_EXPAND * D_MODEL
SSD_HEAD_DIM = 64
SSD_HEADS = D_INNER // SSD_HEAD_DIM
SSD_GROUPS = 4
SSD_STATE = 128
CONV_W = 5
CHUNK = 128
CONV_DIM = D_INNER + 2 * SSD_GROUPS * SSD_STATE
NA_HEADS = 16
NA_HEAD_DIM = 64
NA_WIDTH = NA_HEADS * NA_HEAD_DIM
WIN_ROWS_MAX = 8
WIN_COLS = 16
PEER_HEADS = 8
PEER_N_KEYS = 128
PEER_N_EXPERTS = PEER_N_KEYS * PEER_N_KEYS
PEER_QUERY_DIM = 256
PEER_TOPK = 16
PEER_BLOCK = 128
OFF_XBC = D_INNER
OFF_DT = OFF_XBC + CONV_DIM
OFF_QKV = OFF_DT + 2 * SSD_HEADS
OFF_GATE = OFF_QKV + 3 * NA_WIDTH
D_IN_PROJ = OFF_GATE + 2 * D_MODEL
MIX_WIDTH = D_INNER + NA_WIDTH

kernel_name = "hybrid_ssd_natten_peer_encoder"


def rms_norm(x, g):
    xf = x.astype(jnp.float32)
    y = xf * lax.rsqrt(jnp.mean(xf * xf, axis=-1, keepdims=True) + EPS)
    return (y * g.astype(jnp.float32)).astype(x.dtype)


def centred_dwconv(u, w, b):
    out = lax.conv_general_dilated(
        u, w[:, None, :].astype(u.dtype), window_strides=(1,),
        padding=[(CONV_W // 2, CONV_W // 2)],
        dimension_numbers=('NWC', 'WIO', 'NWC'),
        feature_group_count=u.shape[-1])
    return out + b.astype(u.dtype)


def ssd_chunked(xs, dt, a, bmat, cmat):
    f32 = jnp.float32
    bsz, L, H, P = xs.shape
    nc = L // CHUNK
    R = H // SSD_GROUPS
    xdt = (xs.astype(f32) * dt[..., None]).reshape(bsz, nc, CHUNK, SSD_GROUPS, R, P)
    bc = bmat.astype(f32).reshape(bsz, nc, CHUNK, SSD_GROUPS, SSD_STATE)
    cc = cmat.astype(f32).reshape(bsz, nc, CHUNK, SSD_GROUPS, SSD_STATE)
    a_cum = jnp.cumsum((dt * a).reshape(bsz, nc, CHUNK, SSD_GROUPS, R), axis=2)
    tril = jnp.tril(jnp.ones((CHUNK, CHUNK), dtype=bool))[:, :, None, None]
    seg = a_cum[:, :, :, None] - a_cum[:, :, None, :]
    decay = jnp.exp(jnp.where(tril, seg, -jnp.inf))
    cb = jnp.einsum('bclgn,bcsgn->bclsg', cc, bc)
    y_diag = jnp.einsum('bclsg,bclsgr,bcsgrp->bclgrp', cb, decay, xdt)
    decay_states = jnp.exp(a_cum[:, :, -1:] - a_cum)
    states = jnp.einsum('bclgn,bclgr,bclgrp->bcgrpn', bc, decay_states, xdt)
    chunk_decay = jnp.exp(a_cum[:, :, -1])

    def step(h, inp):
        s_c, d_c = inp
        return h * d_c[..., None, None] + s_c, h

    h0 = jnp.zeros((bsz, SSD_GROUPS, R, P, SSD_STATE), f32)
    _, prev = lax.scan(step, h0, (jnp.moveaxis(states, 1, 0), jnp.moveaxis(chunk_decay, 1, 0)))
    prev = jnp.moveaxis(prev, 0, 1)
    y_off = jnp.einsum('bclgn,bcgrpn,bclgr->bclgrp', cc, prev, jnp.exp(a_cum))
    return (y_diag + y_off).reshape(bsz, L, H, P)


def ssd_branch(z, xbc, dt_raw, conv_w, conv_b, dt_bias, a_log, d_skip, norm_g):
    f32 = jnp.float32
    bsz, L, _ = z.shape
    xbc = jax.nn.silu(centred_dwconv(xbc, conv_w, conv_b))
    xs, bm, cm = jnp.split(xbc, [D_INNER, D_INNER + SSD_GROUPS * SSD_STATE], axis=-1)
    xs = xs.reshape(bsz, L, SSD_HEADS, SSD_HEAD_DIM)
    bm = bm.reshape(bsz, L, SSD_GROUPS, SSD_STATE)
    cm = cm.reshape(bsz, L, SSD_GROUPS, SSD_STATE)
    dt = jax.nn.softplus(dt_raw.astype(f32).reshape(bsz, L, 2, SSD_HEADS) + dt_bias.astype(f32))
    a = -jnp.exp(a_log.astype(f32))
    flip = lambda t: jnp.flip(t, axis=1)
    y_fwd = ssd_chunked(xs, dt[:, :, 0], a[0], bm, cm)
    y_bwd = flip(ssd_chunked(flip(xs), flip(dt[:, :, 1]), a[1], flip(bm), flip(cm)))
    y = y_fwd + y_bwd + d_skip.astype(f32)[:, None] * xs.astype(f32)
    y = y.reshape(bsz, L, D_INNER) * jax.nn.silu(z.astype(f32))
    return rms_norm(y, norm_g).astype(z.dtype)


def na_branch(q, k, v, q_norm_g, k_norm_g, rpb):
    bsz, L, _ = q.shape
    rows = L // GRID_W
    win_r = min(WIN_ROWS_MAX, rows)
    heads = lambda t: t.reshape(bsz, rows, GRID_W, NA_HEADS, NA_HEAD_DIM)
    q = rms_norm(heads(q), q_norm_g) * (NA_HEAD_DIM ** -0.5)
    k = rms_norm(heads(k), k_norm_g)
    v = heads(v)
    cols = np.arange(GRID_W)
    c0 = np.clip(cols - WIN_COLS // 2, 0, GRID_W - WIN_COLS)
    col_idx = c0[:, None] + np.arange(WIN_COLS)[None, :]
    col_off = col_idx - cols[:, None] + (WIN_COLS - 1)
    rpb_cols = jnp.transpose(rpb[:, :, col_off], (0, 2, 1, 3))

    def one_row(r):
        r0 = jnp.clip(r - win_r // 2, 0, rows - win_r)
        kr = lax.dynamic_slice_in_dim(k, r0, win_r, axis=1)
        vr = lax.dynamic_slice_in_dim(v, r0, win_r, axis=1)
        kw = kr[:, :, col_idx]
        vw = vr[:, :, col_idx]
        qr = lax.dynamic_index_in_dim(q, r, axis=1, keepdims=False)
        s = jnp.einsum('bchd,bicjhd->bhcij', qr, kw, preferred_element_type=jnp.float32)
        row_off = r0 + jnp.arange(win_r) - r + (WIN_ROWS_MAX - 1)
        s = s + jnp.take(rpb_cols, row_off, axis=2).astype(jnp.float32)[None]
        p = jax.nn.softmax(s, axis=(-2, -1))
        return jnp.einsum('bhcij,bicjhd->bchd', p.astype(v.dtype), vw)

    out = lax.map(one_row, jnp.arange(rows))
    return jnp.transpose(out, (1, 0, 2, 3, 4)).reshape(bsz, L, NA_WIDTH)


def peer_ffn(h, w_pq, sub_keys, expert_u, expert_v):
    f32 = jnp.float32
    bsz, L, D = h.shape
    t = h.reshape(-1, D)
    nt = t.shape[0]
    q = (t @ w_pq).reshape(nt, PEER_HEADS, 2, PEER_QUERY_DIM // 2)
    s = jnp.einsum('thkd,knd->thkn', q, sub_keys).astype(f32)
    top_s, top_i = lax.top_k(s, PEER_TOPK)
    cand = top_s[:, :, 0, :, None] + top_s[:, :, 1, None, :]
    cand_idx = top_i[:, :, 0, :, None] * PEER_N_KEYS + top_i[:, :, 1, None, :]
    best_s, best_pos = lax.top_k(cand.reshape(nt, PEER_HEADS, -1), PEER_TOPK)
    idx = jnp.take_along_axis(cand_idx.reshape(nt, PEER_HEADS, -1), best_pos, axis=-1)
    g = jax.nn.softmax(best_s, axis=-1)
    nb = nt // PEER_BLOCK

    def block(args):
        tb, ib, gb = args
        u = expert_u[ib]
        vv = expert_v[ib]
        act = jax.nn.gelu(jnp.einsum('td,thkd->thk', tb, u).astype(f32), approximate=False)
        return jnp.einsum('thk,thkd->td', (gb * act).astype(vv.dtype), vv)

    out = lax.map(block, (t.reshape(nb, PEER_BLOCK, D),
                          idx.reshape(nb, PEER_BLOCK, PEER_HEADS, PEER_TOPK),
                          g.reshape(nb, PEER_BLOCK, PEER_HEADS, PEER_TOPK)))
    return out.reshape(bsz, L, D).astype(h.dtype)


def encoder_layer(x, g_mix, w_in, conv_w, conv_b, dt_bias, a_log, d_skip, ssd_norm_g,
                  q_norm_g, k_norm_g, rpb, w_out, g_ffn, w_pq, sub_keys, expert_u, expert_v):
    h = rms_norm(x, g_mix)
    proj = h @ w_in
    z, xbc, dt_raw, qkv, gates = jnp.split(proj, [OFF_XBC, OFF_DT, OFF_QKV, OFF_GATE], axis=-1)
    q, k, v = jnp.split(qkv, 3, axis=-1)
    y_ssd = ssd_branch(z, xbc, dt_raw, conv_w, conv_b, dt_bias, a_log, d_skip, ssd_norm_g)
    y_na = na_branch(q, k, v, q_norm_g, k_norm_g, rpb)
    gate_ssd, gate_na = jnp.split(jax.nn.sigmoid(gates.astype(jnp.float32)), 2, axis=-1)
    br_ssd = y_ssd @ w_out[:D_INNER]
    br_na = y_na @ w_out[D_INNER:]
    x = x + (gate_ssd * br_ssd + gate_na * br_na).astype(x.dtype)
    x = x + peer_ffn(rms_norm(x, g_ffn), w_pq, sub_keys, expert_u, expert_v)
    return x


def setup_inputs(seed: int = 0) -> dict:
    key = jax.random.key(seed)
    ks = jax.random.split(key, 20)
    f32 = jnp.float32
    nrm = lambda k, shape, scale: scale * jax.random.normal(k, shape, f32)
    dt0 = jnp.exp(jax.random.uniform(ks[6], (DEPTH, 2, SSD_HEADS), f32, math.log(1e-3), math.log(1e-1)))
    return {
        "x_prompt": jax.random.normal(ks[0], (BATCH, SEQ, D_MODEL), f32),
        "x_sample": jax.random.normal(ks[1], (DEC_BATCH, DEC_SEQ, D_MODEL), f32),
        "g_mix": 1.0 + nrm(ks[2], (DEPTH, D_MODEL), 0.02),
        "w_in": nrm(ks[3], (DEPTH, D_MODEL, D_IN_PROJ), D_MODEL ** -0.5),
        "conv_w": nrm(ks[4], (DEPTH, CONV_W, CONV_DIM), CONV_W ** -0.5),
        "conv_b": nrm(ks[5], (DEPTH, CONV_DIM), 0.01),
        "dt_bias": dt0 + jnp.log(-jnp.expm1(-dt0)),
        "a_log": jnp.log(jax.random.uniform(ks[7], (DEPTH, 2, SSD_HEADS), f32, 1.0, 16.0)),
        "d_skip": 1.0 + nrm(ks[8], (DEPTH, SSD_HEADS), 0.02),
        "ssd_norm_g": 1.0 + nrm(ks[9], (DEPTH, D_INNER), 0.02),
        "q_norm_g": 1.0 + nrm(ks[10], (DEPTH, NA_HEAD_DIM), 0.02),
        "k_norm_g": 1.0 + nrm(ks[11], (DEPTH, NA_HEAD_DIM), 0.02),
        "rpb": nrm(ks[12], (DEPTH, NA_HEADS, 2 * WIN_ROWS_MAX - 1, 2 * WIN_COLS - 1), 0.02),
        "w_out": nrm(ks[13], (DEPTH, MIX_WIDTH, D_MODEL), MIX_WIDTH ** -0.5),
        "g_ffn": 1.0 + nrm(ks[14], (DEPTH, D_MODEL), 0.02),
        "w_pq": nrm(ks[15], (DEPTH, D_MODEL, PEER_HEADS * PEER_QUERY_DIM), D_MODEL ** -0.5),
        "sub_keys": nrm(ks[16], (DEPTH, 2, PEER_N_KEYS, PEER_QUERY_DIM // 2), (PEER_QUERY_DIM // 2) ** -0.5),
        "expert_u": nrm(ks[17], (DEPTH, PEER_N_EXPERTS, D_MODEL), D_MODEL ** -0.5),
        "expert_v": nrm(ks[18], (DEPTH, PEER_N_EXPERTS, D_MODEL), (PEER_HEADS * PEER_TOPK) ** -0.5),
    }


def reference(x_prompt, x_sample, g_mix, w_in, conv_w, conv_b, dt_bias, a_log, d_skip, ssd_norm_g,
              q_norm_g, k_norm_g, rpb, w_out, g_ffn, w_pq, sub_keys, expert_u, expert_v):
    y_prompt = x_prompt
    y_sample = x_sample
    for i in range(DEPTH):
        layer_params = (g_mix[i], w_in[i], conv_w[i], conv_b[i], dt_bias[i], a_log[i], d_skip[i],
                        ssd_norm_g[i], q_norm_g[i], k_norm_g[i], rpb[i], w_out[i], g_ffn[i],
                        w_pq[i], sub_keys[i], expert_u[i], expert_v[i])
        y_prompt = encoder_layer(y_prompt, *layer_params)
        y_sample = encoder_layer(y_sample, *layer_params)
    return (y_prompt, y_sample)
```

```python
import contextlib
import numpy as np
import concourse.bass as bass
import concourse.mybir as mybir
from concourse.bass_utils import run_bass_kernel_spmd

F32 = mybir.dt.float32
BF16 = mybir.dt.bfloat16
I32 = mybir.dt.int32
U32 = mybir.dt.uint32
AF = mybir.ActivationFunctionType
ALU = mybir.AluOpType

ENGS = ("pe", "act", "dve", "pool", "sp")
NDMASEM = 12


class Buf:
    __slots__ = ("name", "w", "r")

    def __init__(self, name=""):
        self.name = name
        self.w = None
        self.r = []


class Sched:
    def __init__(self, nc):
        self.nc = nc
        self.q = {e: [] for e in ENGS}
        self.nop = {e: 0 for e in ENGS}
        self.opent = {e: [] for e in ENGS}
        self.seen = {e: {} for e in ENGS}
        self.dma_cnt = {(q, j): 0 for q in ("sp", "pool", "act") for j in range(NDMASEM)}
        self.dma_rr = {"sp": 0, "pool": 0, "act": 0}

    def _deps(self, eng, reads, writes):
        need = {}

        def add(tok):
            if tok is None:
                return
            k, v = tok
            if need.get(k, 0) < v:
                need[k] = v
        for b in reads:
            add(b.w)
        for b in writes:
            add(b.w)
            for t in b.r:
                add(t)
        waits = []
        seen = self.seen[eng]
        for k, v in need.items():
            if seen.get(k, 0) < v:
                seen[k] = v
                waits.append((k, v))
                if k[0] == "e":
                    self.opent[k[1]][v - 1][3] = True
        return waits

    def _commit(self, tok, reads, writes):
        for b in reads:
            b.r.append(tok)
            if len(b.r) > 16:
                m = {}
                for k, v in b.r:
                    if m.get(k, 0) < v:
                        m[k] = v
                b.r = list(m.items())
        for b in writes:
            b.w = tok
            b.r = []

    def op(self, eng, fn, reads=(), writes=()):
        waits = self._deps(eng, reads, writes)
        self.nop[eng] += 1
        tok = (("e", eng), self.nop[eng])
        ent = [waits, fn, "e", False]
        self.q[eng].append(ent)
        self.opent[eng].append(ent)
        self._commit(tok, reads, writes)
        return tok

    def dma(self, eng, fn, reads=(), writes=()):
        waits = self._deps(eng, reads, writes)
        j = self.dma_rr[eng]
        self.dma_rr[eng] = (j + 1) % NDMASEM
        self.dma_cnt[(eng, j)] += 16
        tok = (("d", eng, j), self.dma_cnt[(eng, j)])
        self.q[eng].append([waits, fn, ("d", eng, j), True])
        self._commit(tok, reads, writes)
        return tok

    def barrier(self):
        allw = [(("e", e), self.nop[e]) for e in ENGS if self.nop[e] > 0]
        allw += [(("d", q, j), v) for (q, j), v in self.dma_cnt.items() if v > 0]
        for e in ENGS:
            waits = []
            for k, v in allw:
                if self.seen[e].get(k, 0) < v:
                    self.seen[e][k] = v
                    waits.append((k, v))
                    if k[0] == "e":
                        self.opent[k[1]][v - 1][3] = True
            self.q[e].append([waits, None, None, False])

    def emit(self):
        nc = self.nc
        real = {}
        for e in ENGS:
            c = 0
            arr = [0]
            for ent in self.opent[e]:
                if ent[3]:
                    c += 1
                arr.append(c)
            real[e] = arr
        with contextlib.ExitStack() as st:
            sems = {}
            for e in ENGS:
                sems[("e", e)] = st.enter_context(nc.semaphore("s_" + e))
            for (q, j), v in self.dma_cnt.items():
                if v > 0:
                    sems[("d", q, j)] = st.enter_context(nc.semaphore("d_%s_%d" % (q, j)))
            block = st.enter_context(nc.Block())

            def run(ename):
                def body(eng):
                    for waits, fn, kind, marked in self.q[ename]:
                        for k, v in waits:
                            if k[0] == "e":
                                eng.wait_ge(sems[k], real[k[1]][v])
                            else:
                                eng.wait_ge(sems[k], v)
                        if fn is not None:
                            ins = fn(eng)
                            if kind == "e":
                                if marked:
                                    ins.then_inc(sems[("e", ename)], 1)
                            else:
                                ins.then_inc(sems[kind], 16)
                return body
            block.tensor(run("pe"))
            block.scalar(run("act"))
            block.vector(run("dve"))
            block.gpsimd(run("pool"))
            block.sync(run("sp"))


D = 1024
NF = 40
NTC = 5184
NEXP = 16384


def build(L, dbg=False, NSH=None):
    NT = L // 128
    if NSH is None:
        NSH = (NT + 1) // 2
    ROWS = L // 64
    nc = bass.Bass("TRN2", target_bir_lowering=False)
    dt_in = lambda n, s, d=F32: nc.dram_tensor(n, s, d, kind="ExternalInput")
    x_d = dt_in("x", [L, D])
    gmix_d = dt_in("gmix", [128, 8])
    wf_d = dt_in("wf", [NF, 128, 8, 128])
    wt_d = dt_in("wt", [128, 8, NTC])
    convw_d = dt_in("convw", [128, 120])
    convbT_d = dt_in("convbT", [128, 24])
    convbrow_d = dt_in("convbrow", [1, 3072])
    dtb_d = dt_in("dtb", [128, 64])
    alog_d = dt_in("alog", [128, 64])
    dskip_d = dt_in("dskip", [128, 2048])
    sng_d = dt_in("sng", [128, 2048])
    qg_d = dt_in("qg", [128, 1])
    kg_d = dt_in("kg", [128, 1])
    bt_d = dt_in("bt", [5, 128, 5 * 16 * 128])
    wo_d = dt_in("wo", [128, 24, 1024])
    gffn_d = dt_in("gffn", [128, 1024])
    wpq_d = dt_in("wpq", [16, 128, 8, 128])
    skT_d = dt_in("skT", [128, 2, 128])
    eu_d = dt_in("eu", [NEXP, D])
    ev_d = dt_in("ev", [NEXP, D])
    rowidx_d = nc.dram_tensor("rowidx", [128, NSH], I32, kind="ExternalInput")
    out_d = nc.dram_tensor("out", [NSH * 128, D], F32, kind="ExternalOutput")
    sk = "ExternalOutput" if dbg else "Internal"
    PF = nc.dram_tensor("PF", [NF * 128, L], BF16, kind=sk)
    PTz = nc.dram_tensor("PTz", [L, 2048], BF16, kind=sk)
    PTdt = nc.dram_tensor("PTdt", [L, 64], F32, kind=sk)
    PTv = nc.dram_tensor("PTv", [L, 1024], BF16, kind=sk)
    PTg = nc.dram_tensor("PTg", [L, 2048], BF16, kind=sk)
    XC = nc.dram_tensor("XC", [L, 2048], BF16, kind=sk)
    BC = nc.dram_tensor("BC", [L, 512], BF16, kind=sk)
    BCT = nc.dram_tensor("BCT", [512, L], BF16, kind=sk)
    CCT = nc.dram_tensor("CCT", [512, L], BF16, kind=sk)
    YF = nc.dram_tensor("YF", [L, 2048], F32, kind=sk)
    YTOK = nc.dram_tensor("YTOK", [L, 3072], BF16, kind=sk)
    EUb = nc.dram_tensor("EUb", [NEXP, D], BF16)
    EVb = nc.dram_tensor("EVb", [NEXP, D], BF16)
    XM = nc.dram_tensor("XM", [NSH * 128, D], F32, kind=sk)
    DBGF = nc.dram_tensor("DBGF", [128, 4096], F32, kind=sk)
    QN = nc.dram_tensor("QN", [1024, L], BF16, kind=sk)
    KN = nc.dram_tensor("KN", [1024, L], BF16, kind=sk)
    DBGB = nc.dram_tensor("DBGB", [128, 4096], BF16, kind=sk)

    S = Sched(nc)
    with contextlib.ExitStack() as st:
        sb = lambda n, s, d: st.enter_context(nc.sbuf_tensor(n, s, d))
        pst = lambda n, s, d: st.enter_context(nc.psum_tensor(n, s, d))
        identB = sb("identB", [128, 128], BF16)
        identF = sb("identF", [128, 128], F32)
        triU = sb("triU", [128, 128], F32)
        triLo = sb("triLo", [128, 128], F32)
        triLs = sb("triLs", [128, 128], F32)
        triUs = sb("triUs", [128, 128], F32)
        onesF = sb("onesF", [128, 128], F32)
        onesB = sb("onesB", [128, 128], BF16)
        gmix = sb("gmix_s", [128, 8], F32)
        wbuf = sb("wbuf", [128, 41472], BF16)
        fbuf = sb("fbuf", [128, 16384], F32)
        scr = sb("scr", [128, 6144], F32)
        scrB = scr[:, :].bitcast(BF16)
        psF = pst("psF", [128, 6, 512], F32)
        psB = pst("psB", [128, 2, 1024], BF16)
        B_const = Buf("const")
        B_w = Buf("wbuf")
        B_f = Buf("fbuf")
        B_ps = [Buf("ps%d" % i) for i in range(6)]
        B_psb = [Buf("psb%d" % i) for i in range(2)]

        def tri(t, pattern, cm, cmp):
            S.op("pool", lambda e: e.memset(t[:], 1.0), writes=[B_const])
            S.op("pool", lambda e: e.affine_select(out=t[:], in_=t[:], pattern=pattern, compare_op=cmp,
                                                   fill=0.0, base=0, channel_multiplier=cm),
                 reads=[B_const], writes=[B_const])
        S.op("pool", lambda e: e.memset(identF[:], 0.0), writes=[B_const])
        S.op("pool", lambda e: e.affine_select(out=identF[:], in_=identF[:], pattern=[[-1, 128]],
                                               compare_op=ALU.not_equal, fill=1.0, base=0, channel_multiplier=1),
             reads=[B_const], writes=[B_const])
        S.op("pool", lambda e: e.tensor_copy(out=identB[:], in_=identF[:]), reads=[B_const], writes=[B_const])
        tri(triU, [[1, 128]], -1, ALU.is_ge)
        tri(triLo, [[-1, 128]], 1, ALU.is_ge)
        tri(triLs, [[-1, 128]], 1, ALU.is_gt)
        tri(triUs, [[1, 128]], -1, ALU.is_gt)
        S.op("pool", lambda e: e.memset(onesF[:], 1.0), writes=[B_const])
        S.op("pool", lambda e: e.memset(onesB[:], 1.0), writes=[B_const])
        S.dma("sp", lambda e: e.dma_start(out=gmix[:], in_=gmix_d[:, :]), writes=[B_const])
        S.barrier()

        xt = [fbuf[:, i * 1024:(i + 1) * 1024] for i in range(2)]
        B_xt = [Buf() for _ in range(2)]
        sq = fbuf[:, 2048:4096]
        B_sq = Buf()
        ss = sb("ss", [128, 2], F32)
        B_ss = Buf()
        xs = scrB[:, 4096:5120]
        B_xs = Buf()
        hT = scrB[:, 0:4096].rearrange("p (k t) -> p k t", k=8)
        B_hT = Buf()
        fe_n = [0]

        def rms_rstd(src, Bsrc, width):
            sc = 1.0 / np.sqrt(width)
            S.op("act", lambda e: e.activation(out=sq[:, 0:width], in_=src, func=AF.Square, scale=float(sc),
                                               accum_out=ss[:, 0:1]),
                 reads=[Bsrc], writes=[B_sq, B_ss])
            S.op("dve", lambda e: e.tensor_scalar_add(out=ss[:, 0:1], in0=ss[:, 0:1], scalar1=1e-6),
                 reads=[B_ss], writes=[B_ss])
            S.op("act", lambda e: e.activation(out=ss[:, 0:1], in_=ss[:, 0:1], func=AF.Ln), reads=[B_ss], writes=[B_ss])
            S.op("act", lambda e: e.activation(out=ss[:, 0:1], in_=ss[:, 0:1], func=AF.Exp, scale=-0.5),
                 reads=[B_ss], writes=[B_ss])

        def front_end(i, tt):
            j = fe_n[0] % 2
            fe_n[0] += 1
            S.dma("sp", lambda e: e.dma_start(out=xt[j], in_=x_d[i * 128:(i + 1) * 128, :]), writes=[B_xt[j]])
            rms_rstd(xt[j], B_xt[j], D)
            S.op("dve", lambda e: e.tensor_scalar(out=xs, in0=xt[j], scalar1=ss[:, 0:1], scalar2=None,
                                                  op0=ALU.mult),
                 reads=[B_xt[j], B_ss], writes=[B_xs])
            for k in range(8):
                S.op("pe", lambda e, k=k: e.transpose(out=psB[:, 0, k * 128:(k + 1) * 128],
                                                      in_=xs[:, k * 128:(k + 1) * 128], identity=identB[:]),
                     reads=[B_xs, B_const], writes=[B_psb[0]])
            for k in range(8):
                if k % 2 == 0:
                    S.op("act", lambda e, k=k: e.activation(out=hT[:, k, tt * 128:(tt + 1) * 128],
                                                            in_=psB[:, 0, k * 128:(k + 1) * 128],
                                                            func=AF.Copy, scale=gmix[:, k:k + 1]),
                         reads=[B_psb[0], B_const], writes=[B_hT])
                else:
                    S.op("dve", lambda e, k=k: e.tensor_scalar(out=hT[:, k, tt * 128:(tt + 1) * 128],
                                                               in0=psB[:, 0, k * 128:(k + 1) * 128],
                                                               scalar1=gmix[:, k:k + 1], scalar2=None, op0=ALU.mult),
                         reads=[B_psb[0], B_const], writes=[B_hT])

        wst = [fbuf[:, 4096 + i * 1024:4096 + (i + 1) * 1024] for i in range(2)]
        B_wst = [Buf() for _ in range(2)]
        WF = wbuf[:, 0:NF * 1024].rearrange("p (c k n) -> p c k n", c=NF, k=8)
        for ct in range(NF):
            j = ct % 2
            S.dma("sp", lambda e, ct=ct, j=j: e.dma_start(out=wst[j], in_=wf_d[ct].rearrange("p k n -> p (k n)")),
                  writes=[B_wst[j]])
            eng = "dve" if ct % 2 == 0 else "pool"
            S.op(eng, lambda e, ct=ct, j=j: e.tensor_copy(out=wbuf[:, ct * 1024:(ct + 1) * 1024], in_=wst[j]),
                 reads=[B_wst[j]], writes=[B_w])
        stg = [scrB[:, 5120 + i * 512:5120 + (i + 1) * 512] for i in range(4)]
        B_stg = [Buf() for _ in range(4)]
        n_ev = 0
        for b in range(L // 512):
            for tt in range(4):
                front_end(b * 4 + tt, tt)
            for ct in range(NF):
                bank = ct % 4
                for k in range(8):
                    S.op("pe", lambda e, ct=ct, k=k, bank=bank: e.matmul(psF[:, bank, :], lhsT=WF[:, ct, k, :],
                                                                          rhs=hT[:, k, :], start=(k == 0), stop=(k == 7)),
                         reads=[B_w, B_hT], writes=[B_ps[bank]])
                sj = n_ev % 4
                n_ev += 1
                if sj % 2 == 0:
                    S.op("act", lambda e, bank=bank, sj=sj: e.copy(out=stg[sj], in_=psF[:, bank, :]),
                         reads=[B_ps[bank]], writes=[B_stg[sj]])
                else:
                    S.op("dve", lambda e, bank=bank, sj=sj: e.tensor_copy(out=stg[sj], in_=psF[:, bank, :]),
                         reads=[B_ps[bank]], writes=[B_stg[sj]])
                S.dma("sp", lambda e, ct=ct, b=b, sj=sj: e.dma_start(
                    out=PF[ct * 128:(ct + 1) * 128, b * 512:(b + 1) * 512], in_=stg[sj]),
                    reads=[B_stg[sj]])
        S.barrier()

        WT = wbuf[:, 0:8 * NTC].rearrange("p (k n) -> p k n", k=8)
        for k in range(8):
            for c0 in range(0, NTC, 1024):
                c1 = min(NTC, c0 + 1024)
                j = (c0 // 1024) % 2
                S.dma("sp", lambda e, k=k, c0=c0, c1=c1, j=j: e.dma_start(out=wst[j][:, 0:c1 - c0], in_=wt_d[:, k, c0:c1]),
                      writes=[B_wst[j]])
                S.op("dve", lambda e, k=k, c0=c0, c1=c1, j=j: e.tensor_copy(out=WT[:, k, c0:c1], in_=wst[j][:, 0:c1 - c0]),
                     reads=[B_wst[j]], writes=[B_w])
        stz = scrB[:, 7168:9216]
        stv = scrB[:, 9216:10240]
        stgt = scrB[:, 10240:12288]
        stdt = sb("stdt", [128, 64], F32)
        B_stz, B_stv, B_stgt, B_stdt = Buf(), Buf(), Buf(), Buf()
        nb = 0
        for i in range(NT):
            front_end(i, 0)

            def tgroup(c0, n, func, dst, Bdst, dcol):
                nonlocal nb
                bank = nb % 4
                nb += 1
                for k in range(8):
                    S.op("pe", lambda e, k=k: e.matmul(psF[:, bank, 0:n], lhsT=hT[:, k, 0:128], rhs=WT[:, k, c0:c0 + n],
                                                       start=(k == 0), stop=(k == 7)),
                         reads=[B_w, B_hT], writes=[B_ps[bank]])
                S.op("act", lambda e: e.activation(out=dst[:, dcol:dcol + n], in_=psF[:, bank, 0:n], func=func),
                     reads=[B_ps[bank]], writes=[Bdst])
            for q in range(4):
                tgroup(q * 512, 512, AF.Silu, stz, B_stz, q * 512)
            tgroup(2048, 64, AF.Copy, stdt, B_stdt, 0)
            for q in range(2):
                tgroup(2112 + q * 512, 512, AF.Copy, stv, B_stv, q * 512)
            for q in range(4):
                tgroup(3136 + q * 512, 512, AF.Sigmoid, stgt, B_stgt, q * 512)
            r0, r1 = i * 128, (i + 1) * 128
            S.dma("sp", lambda e, r0=r0, r1=r1: e.dma_start(out=PTz[r0:r1, :], in_=stz), reads=[B_stz])
            S.dma("sp", lambda e, r0=r0, r1=r1: e.dma_start(out=PTdt[r0:r1, :], in_=stdt[:]), reads=[B_stdt])
            S.dma("sp", lambda e, r0=r0, r1=r1: e.dma_start(out=PTv[r0:r1, :], in_=stv), reads=[B_stv])
            S.dma("sp", lambda e, r0=r0, r1=r1: e.dma_start(out=PTg[r0:r1, :], in_=stgt), reads=[B_stgt])
        S.barrier()

        convw_s = sb("convw_s", [128, 120], F32)
        convbT_s = sb("convbT_s", [128, 24], F32)
        convbrow_s = sb("convbrow_s", [1, 3072], F32)
        dtb_s = sb("dtb_s", [128, 64], F32)
        aneg = sb("aneg", [128, 64], F32)
        dskip_s = sb("dskip_s", [128, 2048], F32)
        sng_s = sb("sng_s", [128, 2048], F32)
        for dst, src in ((convw_s, convw_d), (convbT_s, convbT_d), (convbrow_s, convbrow_d), (dtb_s, dtb_d),
                         (aneg, alog_d), (dskip_s, dskip_d), (sng_s, sng_d)):
            S.dma("sp", lambda e, dst=dst, src=src: e.dma_start(out=dst[:], in_=src[:, :]), writes=[B_const])
        S.op("act", lambda e: e.activation(out=aneg[:], in_=aneg[:], func=AF.Exp), reads=[B_const], writes=[B_const])
        S.op("dve", lambda e: e.tensor_scalar_mul(out=aneg[:], in0=aneg[:], scalar1=-1.0), reads=[B_const], writes=[B_const])
        DG = wbuf[:, 0:120 * 128].rearrange("p (c j n) -> p c j n", c=24, j=5)
        for i in range(120):
            S.op("dve" if i % 2 == 0 else "pool",
                 lambda e, i=i: e.tensor_scalar(out=DG[:, i // 5, i % 5, :], in0=identF[:], scalar1=convw_s[:, i:i + 1],
                                                scalar2=None, op0=ALU.mult),
                 reads=[B_const], writes=[B_w])
        WB0 = 120 * 128

        def wv(off, n):
            return wbuf[:, WB0 + off:WB0 + off + n]
        xin = wv(0, 24 * 132).rearrange("p (c t) -> p c t", c=24)
        xcst = wv(3200, 2048)
        bcst = wv(5248, 512)
        tst = wv(5760, 1024).rearrange("p (g t) -> p g t", g=8)
        B_xin, B_xcst, B_bcst, B_tst = Buf(), Buf(), Buf(), Buf()
        nb = 0
        for c in range(NT):
            lo, hi = c * 128 - 2, c * 128 + 130
            s0, s1 = max(lo, 0), min(hi, L)
            if lo < 0 or hi > L:
                S.op("pool", lambda e: e.memset(xin, 0.0), writes=[B_xin])
            S.dma("sp", lambda e, s0=s0, s1=s1, lo=lo: e.dma_start(
                out=xin[:, :, s0 - lo:s1 - lo], in_=PF[0:3072, s0:s1].rearrange("(c p) t -> p c t", p=128)),
                writes=[B_xin])

            def conv_T(ct, bank, slot):
                o = psF[:, bank, slot * 128:(slot + 1) * 128]
                for j in range(5):
                    S.op("pe", lambda e, j=j: e.matmul(o, lhsT=xin[:, ct, j:j + 128], rhs=DG[:, ct, j, :],
                                                       start=(j == 0), stop=False),
                         reads=[B_xin, B_w], writes=[B_ps[bank]])
                S.op("pe", lambda e: e.matmul(o, lhsT=onesF[0:1, 0:128], rhs=convbrow_s[0:1, ct * 128:(ct + 1) * 128],
                                              start=False, stop=True),
                     reads=[B_const], writes=[B_ps[bank]])
            for cg in range(5):
                bank = nb % 4
                nb += 1
                for slot in range(4):
                    conv_T(cg * 4 + slot, bank, slot)
                if cg < 4:
                    S.op("act", lambda e, bank=bank, cg=cg: e.activation(out=xcst[:, cg * 512:(cg + 1) * 512],
                                                                         in_=psF[:, bank, :], func=AF.Silu),
                         reads=[B_ps[bank]], writes=[B_xcst])
                else:
                    S.op("act", lambda e, bank=bank: e.activation(out=bcst, in_=psF[:, bank, :], func=AF.Silu),
                         reads=[B_ps[bank]], writes=[B_bcst])
            S.dma("sp", lambda e, c=c: e.dma_start(out=XC[c * 128:(c + 1) * 128, :], in_=xcst), reads=[B_xcst])
            S.dma("sp", lambda e, c=c: e.dma_start(out=BC[c * 128:(c + 1) * 128, :], in_=bcst), reads=[B_bcst])
            for half in range(2):
                bank = nb % 4
                nb += 1
                for slot in range(4):
                    ct = 16 + half * 4 + slot
                    o = psF[:, bank, slot * 128:(slot + 1) * 128]
                    for j in range(5):
                        S.op("pe", lambda e, j=j, ct=ct, o=o: e.matmul(o, lhsT=DG[:, ct, j, :], rhs=xin[:, ct, j:j + 128],
                                                                        start=(j == 0), stop=(j == 4)),
                             reads=[B_xin, B_w], writes=[B_ps[bank]])
                    S.op("act", lambda e, ct=ct, o=o: e.activation(out=tst[:, ct - 16, :], in_=o, func=AF.Silu,
                                                                   bias=convbT_s[:, ct:ct + 1]),
                         reads=[B_ps[bank], B_const], writes=[B_tst])
            S.dma("sp", lambda e, c=c: e.dma_start(out=BCT[:, c * 128:(c + 1) * 128].rearrange("(g p) t -> p g t", p=128),
                                                   in_=tst[:, 0:4, :]), reads=[B_tst])
            S.dma("sp", lambda e, c=c: e.dma_start(out=CCT[:, c * 128:(c + 1) * 128].rearrange("(g p) t -> p g t", p=128),
                                                   in_=tst[:, 4:8, :]), reads=[B_tst])
        S.barrier()

        def fv(off, n):
            return fbuf[:, off:off + n]
        LA = fv(0, 4096).rearrange("p (h s) -> p h s", h=32)
        yacc = fv(4096, 2048)
        Hs = fv(6144, 2048)
        yf = fv(8192, 2048)
        yz = fv(10240, 2048)
        Eg = fv(12288, 1024)
        tmpf = fv(13312, 512)
        MTm = fv(13824, 512).rearrange("p (g l) -> p g l", g=4)
        sm = fv(14336, 256)
        sm2 = fv(14592, 96)
        sqw = fv(14700, 1600)
        xc = wv(0, 2048)
        bc = wv(2048, 512)
        bct = wv(2560, 512).rearrange("p (g t) -> p g t", g=4)
        cct = wv(3072, 512).rearrange("p (g t) -> p g t", g=4)
        xdt = wv(3584, 2048).rearrange("p (h d) -> p h d", h=32)
        MD = wv(5632, 1024).rearrange("p (h l) -> p h l", h=8)
        Hb = wv(6656, 2048)
        xdd = wv(8704, 512).rearrange("p (h d) -> p h d", h=8)
        zs = wv(9216, 2048)
        yn = wv(11264, 2048)
        ynT = wv(13312, 2048).rearrange("p (c t) -> p c t", c=16)
        dtr = sb("dtr", [128, 64], F32)
        Bn = {n: Buf(n) for n in "LA yacc Hs yf yz Eg tmpf MTm sm sm2 xc bc bct cct xdt MD Hb xdd zs yn ynT dtr".split()}
        def sweep(d):
            S.op("pool", lambda e: e.memset(Hs, 0.0), writes=[Bn["Hs"]])
            S.op("pool", lambda e: e.memset(Hb, 0.0), writes=[Bn["Hb"]])
            T1 = triLs if d == 0 else triUs
            T2 = triU if d == 0 else triLo
            Tds = triLs if d == 0 else triUs
            order = range(NT) if d == 0 else range(NT - 1, -1, -1)
            for c in order:
                r0, r1 = c * 128, (c + 1) * 128
                S.dma("sp", lambda e, r0=r0, r1=r1: e.dma_start(out=xc, in_=XC[r0:r1, :]), writes=[Bn["xc"]])
                S.dma("sp", lambda e, r0=r0, r1=r1: e.dma_start(out=bc, in_=BC[r0:r1, :]), writes=[Bn["bc"]])
                S.dma("sp", lambda e, r0=r0, r1=r1: e.dma_start(out=bct, in_=BCT[:, r0:r1].rearrange("(g p) t -> p g t", p=128)),
                      writes=[Bn["bct"]])
                S.dma("sp", lambda e, r0=r0, r1=r1: e.dma_start(out=cct, in_=CCT[:, r0:r1].rearrange("(g p) t -> p g t", p=128)),
                      writes=[Bn["cct"]])
                S.dma("sp", lambda e, r0=r0, r1=r1: e.dma_start(out=dtr[:], in_=PTdt[r0:r1, :]), writes=[Bn["dtr"]])
                if d == 1:
                    S.dma("sp", lambda e, r0=r0, r1=r1: e.dma_start(out=yf, in_=YF[r0:r1, :]), writes=[Bn["yf"]])
                    S.dma("sp", lambda e, r0=r0, r1=r1: e.dma_start(out=zs, in_=PTz[r0:r1, :]), writes=[Bn["zs"]])
                o32 = slice(d * 32, d * 32 + 32)
                smB = [Bn["sm"]]
                S.op("dve", lambda e: e.tensor_tensor(out=sm[:, 0:32], in0=dtr[:, o32], in1=dtb_s[:, o32], op=ALU.add),
                     reads=[Bn["dtr"], B_const], writes=smB)
                S.op("act", lambda e: e.activation(out=sm[:, 32:64], in_=sm[:, 0:32], func=AF.Abs),
                     reads=smB, writes=smB)
                S.op("act", lambda e: e.activation(out=sm[:, 64:96], in_=sm[:, 32:64], func=AF.Exp, scale=-1.0),
                     reads=smB, writes=smB)
                S.op("act", lambda e: e.activation(out=sm[:, 96:128], in_=sm[:, 64:96], func=AF.Ln, bias=1.0),
                     reads=smB, writes=smB)
                S.op("dve", lambda e: e.tensor_scalar_max(out=sm[:, 128:160], in0=sm[:, 0:32], scalar1=0.0),
                     reads=smB, writes=smB)
                S.op("dve", lambda e: e.tensor_tensor(out=sm[:, 160:192], in0=sm[:, 128:160], in1=sm[:, 96:128], op=ALU.add),
                     reads=smB, writes=smB)
                S.op("dve", lambda e: e.tensor_tensor(out=sm[:, 192:224], in0=sm[:, 160:192], in1=aneg[:, o32], op=ALU.mult),
                     reads=smB + [B_const], writes=smB)
                dtv = sm[:, 160:192]
                adt = sm[:, 192:224]
                S.op("dve", lambda e: e.tensor_tensor(out=xdt, in0=xc.rearrange("p (h d) -> p h d", h=32),
                                                      in1=dtv.unsqueeze(2).to_broadcast([128, 32, 64]), op=ALU.mult),
                     reads=[Bn["xc"]] + smB, writes=[Bn["xdt"]])
                for g in range(4):
                    S.op("pe", lambda e, g=g: e.matmul(psF[:, 0, g * 128:(g + 1) * 128], lhsT=bct[:, g, :], rhs=cct[:, g, :],
                                                       start=True, stop=True),
                         reads=[Bn["bct"], Bn["cct"]], writes=[B_ps[0]])
                msk = triU if d == 0 else triLo
                S.op("dve", lambda e: e.tensor_tensor(out=MTm, in0=psF[:, 0, :].rearrange("p (g l) -> p g l", g=4),
                                                      in1=msk[:].unsqueeze(1).to_broadcast([128, 4, 128]), op=ALU.mult),
                     reads=[B_ps[0], B_const], writes=[Bn["MTm"]])
                S.op("pool", lambda e: e.tensor_tensor(out=LA, in0=T1[:].unsqueeze(1).to_broadcast([128, 32, 128]),
                                                       in1=adt.unsqueeze(2).to_broadcast([128, 32, 128]), op=ALU.mult),
                     reads=smB + [B_const], writes=[Bn["LA"]])
                S.op("pe", lambda e: e.matmul(psF[:, 0, 0:32], lhsT=T2[:], rhs=adt, start=True, stop=True),
                     reads=smB + [B_const, Bn["MTm"]], writes=[B_ps[0]])
                S.op("pe", lambda e: e.matmul(psF[:, 0, 32:64], lhsT=Tds[:], rhs=adt, start=True, stop=True),
                     reads=smB + [B_const], writes=[B_ps[0]])
                S.op("pe", lambda e: e.matmul(psF[:, 0, 64:96], lhsT=onesF[:], rhs=adt, start=True, stop=True),
                     reads=smB + [B_const], writes=[B_ps[0]])
                S.op("act", lambda e: e.activation(out=sm2, in_=psF[:, 0, 0:96], func=AF.Exp),
                     reads=[B_ps[0]], writes=[Bn["sm2"]])
                EA, dsv, cdv = sm2[:, 0:32], sm2[:, 32:64], sm2[:, 64:96]
                for g in range(4):
                    for hh in range(8):
                        h = g * 8 + hh
                        bk = 1 + hh // 4
                        S.op("pe", lambda e, h=h, hh=hh, bk=bk: e.matmul(psF[:, bk, (hh % 4) * 128:(hh % 4 + 1) * 128],
                                                                          lhsT=LA[:, h, :], rhs=T2[:], start=True, stop=True),
                             reads=[Bn["LA"], B_const], writes=[B_ps[bk]])
                    S.op("act", lambda e: e.activation(out=Eg, in_=psF[:, 1:3, :].rearrange("p b n -> p (b n)"), func=AF.Exp),
                         reads=[B_ps[1], B_ps[2]], writes=[Bn["Eg"]])
                    S.op("dve", lambda e, g=g: e.tensor_tensor(out=MD, in0=Eg.rearrange("p (h l) -> p h l", h=8),
                                                               in1=MTm[:, g, :].unsqueeze(1).to_broadcast([128, 8, 128]),
                                                               op=ALU.mult),
                         reads=[Bn["Eg"], Bn["MTm"]], writes=[Bn["MD"]])
                    if dbg and d == 0 and c == 0 and g == 0:
                        S.dma("sp", lambda e: e.dma_start(out=DBGF[:, 0:256], in_=sm), reads=[Bn["sm"]])
                        S.dma("sp", lambda e: e.dma_start(out=DBGF[:, 256:352], in_=sm2), reads=[Bn["sm2"]])
                        S.dma("sp", lambda e: e.dma_start(out=DBGF[:, 512:1024], in_=MTm.rearrange("p g l -> p (g l)")), reads=[Bn["MTm"]])
                        S.dma("sp", lambda e: e.dma_start(out=DBGF[:, 1024:2048], in_=Eg), reads=[Bn["Eg"]])
                        S.dma("sp", lambda e: e.dma_start(out=DBGB[:, 0:2048], in_=xdt.rearrange("p h d -> p (h d)")), reads=[Bn["xdt"]])
                        S.dma("sp", lambda e: e.dma_start(out=DBGB[:, 2048:3072], in_=MD.rearrange("p h l -> p (h l)")), reads=[Bn["MD"]])
                    for hh in range(8):
                        h = g * 8 + hh
                        S.op("pe", lambda e, h=h, hh=hh: e.matmul(psF[:, 3, hh * 64:(hh + 1) * 64], lhsT=MD[:, hh, :],
                                                                  rhs=xdt[:, h, :], start=True, stop=True),
                             reads=[Bn["MD"], Bn["xdt"]], writes=[B_ps[3]])
                    S.op("pe", lambda e, g=g: e.matmul(psF[:, 4, :], lhsT=cct[:, g, :], rhs=Hb[:, g * 512:(g + 1) * 512],
                                                       start=True, stop=True),
                         reads=[Bn["cct"], Bn["Hb"]], writes=[B_ps[4]])
                    S.op("dve", lambda e, g=g: e.tensor_tensor(out=tmpf.rearrange("p (h d) -> p h d", h=8),
                                                               in0=psF[:, 4, :].rearrange("p (h d) -> p h d", h=8),
                                                               in1=EA[:, g * 8:(g + 1) * 8].unsqueeze(2).to_broadcast([128, 8, 64]),
                                                               op=ALU.mult),
                         reads=[B_ps[4], Bn["sm2"]], writes=[Bn["tmpf"]])
                    S.op("dve", lambda e, g=g: e.tensor_tensor(out=yacc[:, g * 512:(g + 1) * 512], in0=psF[:, 3, :], in1=tmpf,
                                                               op=ALU.add),
                         reads=[B_ps[3], Bn["tmpf"]], writes=[Bn["yacc"]])
                    S.op("pool", lambda e, g=g: e.tensor_tensor(out=xdd, in0=xdt[:, g * 8:(g + 1) * 8, :],
                                                                in1=dsv[:, g * 8:(g + 1) * 8].unsqueeze(2).to_broadcast([128, 8, 64]),
                                                                op=ALU.mult),
                         reads=[Bn["xdt"], Bn["sm2"]], writes=[Bn["xdd"]])
                    S.op("pe", lambda e, g=g: e.matmul(psF[:, 5, :], lhsT=bc[:, g * 128:(g + 1) * 128],
                                                       rhs=xdd.rearrange("p h d -> p (h d)"), start=True, stop=True),
                         reads=[Bn["bc"], Bn["xdd"]], writes=[B_ps[5]])
                    Hg = Hs[:, g * 512:(g + 1) * 512]
                    S.op("dve", lambda e, g=g, Hg=Hg: e.tensor_tensor(out=Hg.rearrange("p (h d) -> p h d", h=8),
                                                                      in0=Hg.rearrange("p (h d) -> p h d", h=8),
                                                                      in1=cdv[:, g * 8:(g + 1) * 8].unsqueeze(2).to_broadcast([128, 8, 64]),
                                                                      op=ALU.mult),
                         reads=[Bn["sm2"], Bn["Hs"]], writes=[Bn["Hs"]])
                    S.op("dve", lambda e, Hg=Hg: e.tensor_tensor(out=Hg, in0=psF[:, 5, :], in1=Hg, op=ALU.add),
                         reads=[B_ps[5], Bn["Hs"]], writes=[Bn["Hs"]])
                    S.op("act", lambda e, g=g, Hg=Hg: e.copy(out=Hb[:, g * 512:(g + 1) * 512], in_=Hg),
                         reads=[Bn["Hs"]], writes=[Bn["Hb"]])
                if d == 0:
                    S.dma("sp", lambda e, r0=r0, r1=r1: e.dma_start(out=YF[r0:r1, :], in_=yacc), reads=[Bn["yacc"]])
                else:
                    S.op("pool", lambda e: e.tensor_tensor(out=yacc, in0=yacc, in1=yf, op=ALU.add),
                         reads=[Bn["yf"], Bn["yacc"]], writes=[Bn["yacc"]])
                    S.op("dve", lambda e: e.tensor_tensor(out=yf, in0=xc, in1=dskip_s[:], op=ALU.mult),
                         reads=[Bn["xc"], B_const, Bn["yacc"]], writes=[Bn["yf"]])
                    S.op("pool", lambda e: e.tensor_tensor(out=yacc, in0=yacc, in1=yf, op=ALU.add),
                         reads=[Bn["yf"], Bn["yacc"]], writes=[Bn["yacc"]])
                    S.op("dve", lambda e: e.tensor_tensor(out=yz, in0=yacc, in1=zs, op=ALU.mult),
                         reads=[Bn["yacc"], Bn["zs"]], writes=[Bn["yz"]])
                    S.op("act", lambda e: e.activation(out=yf, in_=yz, func=AF.Square, scale=float(1.0 / np.sqrt(2048.0)),
                                                       accum_out=ss[:, 0:1]),
                         reads=[Bn["yz"]], writes=[Bn["yf"], B_ss])
                    S.op("dve", lambda e: e.tensor_scalar_add(out=ss[:, 0:1], in0=ss[:, 0:1], scalar1=1e-6),
                         reads=[B_ss], writes=[B_ss])
                    S.op("act", lambda e: e.activation(out=ss[:, 0:1], in_=ss[:, 0:1], func=AF.Ln), reads=[B_ss], writes=[B_ss])
                    S.op("act", lambda e: e.activation(out=ss[:, 0:1], in_=ss[:, 0:1], func=AF.Exp, scale=-0.5),
                         reads=[B_ss], writes=[B_ss])
                    S.op("dve", lambda e: e.scalar_tensor_tensor(out=yn, in0=yz, scalar=ss[:, 0:1], in1=sng_s[:],
                                                                 op0=ALU.mult, op1=ALU.mult),
                         reads=[Bn["yz"], B_ss, B_const], writes=[Bn["yn"]])
                    S.dma("sp", lambda e, r0=r0, r1=r1: e.dma_start(out=YTOK[r0:r1, 0:2048], in_=yn), reads=[Bn["yn"]])
        sweep(0)
        sweep(1)
        S.barrier()

        qkg = sb("qkg", [128, 2], F32)
        S.dma("sp", lambda e: e.dma_start(out=qkg[:, 0:1], in_=qg_d[:, :]), writes=[B_const])
        S.dma("sp", lambda e: e.dma_start(out=qkg[:, 1:2], in_=kg_d[:, :]), writes=[B_const])
        BD = sb("BD", [128, 128], BF16)
        S.op("pool", lambda e: e.memset(BD[:], 0.0), writes=[B_const])
        S.op("pool", lambda e: e.memset(BD[0:64, 0:64], 1.0), reads=[B_const], writes=[B_const])
        S.op("pool", lambda e: e.memset(BD[64:128, 64:128], 1.0), reads=[B_const], writes=[B_const])
        qraw = wbuf[:, 0:512]
        qsq = wbuf[:, 512:1024]
        qno = wbuf[:, 1024:1536]
        rstd_q = fbuf[:, 0:512]
        B_qraw, B_qsq, B_qno, B_rq = Buf(), Buf(), Buf(), Buf()
        for b in range(L // 512):
            for ct in range(16):
                c0, c1 = b * 512, (b + 1) * 512
                S.dma("sp", lambda e, ct=ct, c0=c0, c1=c1: e.dma_start(out=qraw, in_=PF[(24 + ct) * 128:(25 + ct) * 128, c0:c1]),
                      writes=[B_qraw])
                S.op("dve", lambda e: e.tensor_tensor(out=qsq, in0=qraw, in1=qraw, op=ALU.mult), reads=[B_qraw], writes=[B_qsq])
                bank = ct % 2
                S.op("pe", lambda e, bank=bank: e.matmul(psF[:, bank, :], lhsT=BD[:], rhs=qsq, start=True, stop=True),
                     reads=[B_qsq, B_const], writes=[B_ps[bank]])
                S.op("act", lambda e, bank=bank: e.activation(out=rstd_q, in_=psF[:, bank, :], func=AF.Ln, scale=1.0 / 64, bias=1e-6),
                     reads=[B_ps[bank]], writes=[B_rq])
                S.op("act", lambda e: e.activation(out=rstd_q, in_=rstd_q, func=AF.Exp, scale=-0.5), reads=[B_rq], writes=[B_rq])
                gi = 0 if ct < 8 else 1
                S.op("dve", lambda e, gi=gi: e.scalar_tensor_tensor(out=qno, in0=qraw, scalar=qkg[:, gi:gi + 1], in1=rstd_q,
                                                                   op0=ALU.mult, op1=ALU.mult),
                     reads=[B_qraw, B_rq, B_const], writes=[B_qno])
                dstT = QN if ct < 8 else KN
                cc = ct % 8
                S.dma("sp", lambda e, dstT=dstT, cc=cc, c0=c0, c1=c1: e.dma_start(out=dstT[cc * 128:(cc + 1) * 128, c0:c1], in_=qno),
                      reads=[B_qno])
        S.barrier()
        NE = 5 * 16 * 128
        Egen = wbuf[:, 0:NE].rearrange("p (k h q) -> p k h q", k=5, h=16)
        Espec = wbuf[:, NE:2 * NE].rearrange("p (k h q) -> p k h q", k=5, h=16)
        o0 = 2 * NE
        qn = wbuf[:, o0:o0 + 1024].rearrange("p (c t) -> p c t", c=8)
        kn = wbuf[:, o0 + 1024:o0 + 1024 + 5120].rearrange("p (c t) -> p c t", c=8)
        o1 = o0 + 6144
        vaug = wbuf[:, o1:o1 + 5200].rearrange("p (k h d) -> p k h d", k=5, h=16)
        o2 = o1 + 5200
        pbuf = [wbuf[:, o2 + i * 512:o2 + (i + 1) * 512] for i in range(2)]
        pm = [[wbuf[:, o2 + 1024 + (i * 5 + k) * 512:o2 + 1024 + (i * 5 + k + 1) * 512] for k in range(5)] for i in range(2)]
        yna = wbuf[:, o2 + 6144:o2 + 7168]
        ynaT = wbuf[:, o2 + 7168:o2 + 8192].rearrange("p (c t) -> p c t", c=8)
        tstage = fbuf[:, 0:NE]
        den = fbuf[:, NE:NE + 16]
        B_E, B_Es, B_qn, B_kn, B_va, B_yna, B_ynaT, B_ts, B_den = (Buf() for _ in range(9))
        B_pb = [Buf(), Buf()]
        B_pm = [Buf(), Buf()]

        def load_table(pat, dst, Bdst):
            S.dma("sp", lambda e: e.dma_start(out=tstage, in_=bt_d[pat]), writes=[B_ts])
            S.op("act", lambda e: e.activation(out=dst.rearrange("p k h q -> p (k h q)"), in_=tstage, func=AF.Exp),
                 reads=[B_ts], writes=[Bdst])
        load_table(2, Egen, B_E)
        S.op("pool", lambda e: e.memset(vaug.rearrange("p k h d -> p (k h d)"), 1.0), writes=[B_va])
        npp = 0
        for t in range(NT):
            kt0 = min(max(t - 2, 0), NT - 5)
            rel = t - kt0
            if rel == 2:
                Et, BEt = Egen, B_E
            else:
                load_table(rel, Espec, B_Es)
                Et, BEt = Espec, B_Es
            r0, r1 = t * 128, (t + 1) * 128
            k0, k1 = kt0 * 128, (kt0 + 5) * 128
            S.dma("sp", lambda e, r0=r0, r1=r1: e.dma_start(out=qn, in_=QN[:, r0:r1].rearrange("(c p) t -> p c t", p=128)),
                  writes=[B_qn])
            S.dma("sp", lambda e, k0=k0, k1=k1: e.dma_start(out=kn, in_=KN[:, k0:k1].rearrange("(c p) t -> p c t", p=128)),
                  writes=[B_kn])
            for kb in range(5):
                S.dma("sp", lambda e, kb=kb, k0=k0: e.dma_start(
                    out=vaug[:, kb, :, 0:64], in_=PTv[k0 + kb * 128:k0 + (kb + 1) * 128, :].rearrange("p (h d) -> p h d", h=16)),
                    writes=[B_va])
            for hq in range(4):
                jq = hq % 2
                for kb in range(5):
                    j = npp % 2
                    npp += 1
                    bank = j
                    for hh in range(4):
                        h = hq * 4 + hh
                        ct, po = h // 2, (h % 2) * 64
                        S.op("pe", lambda e, ct=ct, po=po, hh=hh, kb=kb, bank=bank: e.matmul(
                            psF[:, bank, hh * 128:(hh + 1) * 128], lhsT=kn[po:po + 64, ct, kb * 128:(kb + 1) * 128],
                            rhs=qn[po:po + 64, ct, :], start=True, stop=True),
                            reads=[B_kn, B_qn], writes=[B_ps[bank]])
                    S.op("act", lambda e, bank=bank, j=j: e.activation(out=pbuf[j], in_=psF[:, bank, :], func=AF.Exp, scale=0.125),
                         reads=[B_ps[bank]], writes=[B_pb[j]])
                    S.op("pool" if j == 0 else "dve", lambda e, j=j, kb=kb, hq=hq, Et=Et, jq=jq: e.tensor_tensor(
                        out=pm[jq][kb], in0=pbuf[j], in1=Et[:, kb, hq * 4:(hq + 1) * 4, :].rearrange("p h q -> p (h q)"), op=ALU.mult),
                        reads=[B_pb[j], BEt], writes=[B_pm[jq]])
                for hh in range(4):
                    h = hq * 4 + hh
                    ob, oc = 2 + h // 7, (h % 7) * 65
                    for kb in range(5):
                        S.op("pe", lambda e, h=h, hh=hh, kb=kb, jq=jq, ob=ob, oc=oc: e.matmul(
                            psF[:, ob, oc:oc + 65], lhsT=pm[jq][kb][:, hh * 128:(hh + 1) * 128], rhs=vaug[:, kb, h, :],
                            start=(kb == 0), stop=(kb == 4)),
                            reads=[B_pm[jq], B_va], writes=[B_ps[ob]])
            for ob, h0, nh in ((2, 0, 7), (3, 7, 7), (4, 14, 2)):
                pv = psF[:, ob, 0:nh * 65].rearrange("p (h d) -> p h d", h=nh)
                S.op("dve", lambda e, pv=pv, h0=h0, nh=nh: e.tensor_copy(out=den[:, h0:h0 + nh].unsqueeze(2), in_=pv[:, :, 64:65]),
                     reads=[B_ps[ob]], writes=[B_den])
            S.op("dve", lambda e: e.reciprocal(out=den, in_=den), reads=[B_den], writes=[B_den])
            for ob, h0, nh in ((2, 0, 7), (3, 7, 7), (4, 14, 2)):
                pv = psF[:, ob, 0:nh * 65].rearrange("p (h d) -> p h d", h=nh)
                S.op("dve", lambda e, pv=pv, h0=h0, nh=nh: e.tensor_tensor(
                    out=yna[:, h0 * 64:(h0 + nh) * 64].rearrange("p (h d) -> p h d", h=nh), in0=pv[:, :, 0:64],
                    in1=den[:, h0:h0 + nh].unsqueeze(2).to_broadcast([128, nh, 64]), op=ALU.mult),
                    reads=[B_ps[ob], B_den], writes=[B_yna])
            S.dma("sp", lambda e, r0=r0, r1=r1: e.dma_start(out=YTOK[r0:r1, 2048:3072], in_=yna), reads=[B_yna])
        S.barrier()

        WO = wbuf[:, 0:24 * 1024].rearrange("p (c n) -> p c n", c=24)
        for ct in range(24):
            j = ct % 2
            S.dma("sp", lambda e, ct=ct, j=j: e.dma_start(out=wst[j], in_=wo_d[:, ct, :]), writes=[B_wst[j]])
            S.op("dve" if j == 0 else "pool", lambda e, ct=ct, j=j: e.tensor_copy(out=WO[:, ct, :], in_=wst[j]),
                 reads=[B_wst[j]], writes=[B_w])
        _o = [0]

        def sv(n, shape=None, dt=None):
            v = scr[:, _o[0]:_o[0] + n]
            _o[0] += n
            if dt is not None:
                v = v.bitcast(dt)
            return v
        gffn_s = sv(1024)
        skT_s = sv(256).rearrange("p (k n) -> p k n", k=2)
        iota_i = sv(256, dt=I32)
        iota_f = sv(256)
        S.dma("sp", lambda e: e.dma_start(out=gffn_s, in_=gffn_d[:, :]), writes=[B_const])
        S.dma("sp", lambda e: e.dma_start(out=skT_s, in_=skT_d[:, :, :]), writes=[B_const])
        S.op("pool", lambda e: e.iota(iota_i, pattern=[[1, 256]], base=0, channel_multiplier=0), writes=[B_const])
        S.op("pool", lambda e: e.tensor_copy(out=iota_f, in_=iota_i), reads=[B_const], writes=[B_const])
        yT_s = wbuf[:, 24576:24576 + 3072].rearrange("p (c t) -> p c t", c=24)
        g12 = wbuf[:, 27648:27648 + 2048]
        yrow = wbuf[:, 29696:29696 + 3072]
        gbb = [wbuf[:, 32768 + i * 1024:32768 + (i + 1) * 1024] for i in range(4)]
        cst = [wbuf[:, 36864 + i * 1024:36864 + (i + 1) * 1024] for i in range(2)]
        B_cst = [Buf(), Buf()]
        B_yrow = Buf()
        rowidx_s = sv(NSH, dt=I32)
        S.dma("sp", lambda e: e.dma_start(out=rowidx_s, in_=rowidx_d[:, :]), writes=[B_const])
        nce = 0
        for src, dstE in ((eu_d, EUb), (ev_d, EVb)):
            for rt in range(NEXP // 128):
                j = nce % 2
                nce += 1
                S.dma("sp", lambda e, src=src, rt=rt, j=j: e.dma_start(out=wst[j], in_=src[rt * 128:(rt + 1) * 128, :]), writes=[B_wst[j]])
                if nce % 3 == 0:
                    S.op("act", lambda e, j=j: e.copy(out=cst[j], in_=wst[j]), reads=[B_wst[j]], writes=[B_cst[j]])
                elif nce % 3 == 1:
                    S.op("dve", lambda e, j=j: e.tensor_copy(out=cst[j], in_=wst[j]), reads=[B_wst[j]], writes=[B_cst[j]])
                else:
                    S.op("pool", lambda e, j=j: e.tensor_copy(out=cst[j], in_=wst[j]), reads=[B_wst[j]], writes=[B_cst[j]])
                S.dma("sp", lambda e, dstE=dstE, rt=rt, j=j: e.dma_start(out=dstE[rt * 128:(rt + 1) * 128, :], in_=cst[j]), reads=[B_cst[j]])
        S.barrier()
        xin_f = fbuf[:, 0:1024]
        xm = fbuf[:, 1024:2048]
        tmpo = fbuf[:, 2048:3072]
        tt = fbuf[:, 3072:4096]
        tT_s = fbuf[:, 4096:5120]
        qT_s = fbuf[:, 5120:7168]
        s_s = fbuf[:, 7168:9216].rearrange("p (k n) -> p k n", k=16)
        gbuf = [fbuf[:, 9216 + i * 1024:9216 + (i + 1) * 1024] for i in range(4)]
        junk = fbuf[:, 13312:14336]
        wpst = [fbuf[:, 14336 + i * 1024:14336 + (i + 1) * 1024] for i in range(2)]
        top = sv(256).rearrange("p (a b) -> p a b", a=16)
        idxu = sv(256, dt=U32).rearrange("p (a b) -> p a b", a=16)
        idxf = sv(256).rearrange("p (a b) -> p a b", a=16)
        work = sv(128)
        cand = sv(256).rearrange("p (a b) -> p a b", a=16)
        cidx = sv(256).rearrange("p (a b) -> p a b", a=16)
        work2 = sv(256)
        best = sv(16)
        posu = sv(16, dt=U32)
        posf = sv(16)
        ef = sv(128)
        ei = sv(128, dt=I32)
        gts = sv(128)
        actv = sv(128)
        wgt = sv(128)
        smz = sv(4)
        Bq = {n: Buf(n) for n in "yT g12 xin xm tmpo tt tT qT s junk top idxu idxf work cand cidx work2 best posu posf ef ei gts actv wgt smz".split()}
        B_gb = [Buf() for _ in range(4)]
        B_wp = [Buf(), Buf()]
        tt2 = fbuf[:, 9216:10240]
        xm2 = fbuf[:, 10240:11264]
        junk2 = fbuf[:, 11264:12288]
        gts2 = sv(128)
        ei2 = sv(128, dt=I32)
        gring = gbb + cst
        B_gr = [Buf() for _ in gring]
        NG = len(gring)
        for _n in ("tt", "xm", "ei", "gts"):
            Bq[_n + "0"] = Bq[_n]
            Bq[_n + "1"] = Buf(_n + "1")
        Bq["junk2"] = Buf("junk2")
        PARV = {"tt": (tt, tt2), "xm": (xm, xm2), "ei": (ei, ei2), "gts": (gts, gts2)}
        ngbc = [0]

        def stage1(i):
            par = i % 2
            tt_p, xm_p, ei_p, gts_p = PARV["tt"][par], PARV["xm"][par], PARV["ei"][par], PARV["gts"][par]
            r0, r1 = i * 128, (i + 1) * 128
            ridx = rowidx_s[:, i:i + 1]
            S.dma("pool", lambda e, ridx=ridx: e.indirect_dma_start(out=yrow, out_offset=None, in_=YTOK[:, :],
                                                                    in_offset=bass.IndirectOffsetOnAxis(ap=ridx, axis=0)),
                  reads=[B_const], writes=[B_yrow])
            S.dma("pool", lambda e, ridx=ridx: e.indirect_dma_start(out=g12, out_offset=None, in_=PTg[:, :],
                                                                    in_offset=bass.IndirectOffsetOnAxis(ap=ridx, axis=0)),
                  reads=[B_const], writes=[Bq["g12"]])
            S.dma("pool", lambda e, ridx=ridx: e.indirect_dma_start(out=xin_f, out_offset=None, in_=x_d[:, :],
                                                                    in_offset=bass.IndirectOffsetOnAxis(ap=ridx, axis=0)),
                  reads=[B_const], writes=[Bq["xin"]])
            for rnd, (c0, nct) in enumerate(((0, 16), (16, 8))):
                for cc in range(nct):
                    ct = c0 + cc
                    S.op("pe", lambda e, ct=ct, cc=cc: e.transpose(out=psB[:, cc // 8, (cc % 8) * 128:(cc % 8 + 1) * 128],
                                                                  in_=yrow[:, ct * 128:(ct + 1) * 128], identity=identB[:]),
                         reads=[B_yrow, B_const], writes=[B_psb[cc // 8]])
                S.op("act", lambda e, c0=c0: e.copy(out=yT_s[:, c0:c0 + 8, :].rearrange("p c t -> p (c t)"), in_=psB[:, 0, :]),
                     reads=[B_psb[0]], writes=[Bq["yT"]])
                if nct == 16:
                    S.op("dve", lambda e: e.tensor_copy(out=yT_s[:, 8:16, :].rearrange("p c t -> p (c t)"), in_=psB[:, 1, :]),
                         reads=[B_psb[1]], writes=[Bq["yT"]])
            for half in range(2):
                yield
                hs = slice(half * 512, (half + 1) * 512)
                hs2 = slice(1024 + half * 512, 1024 + (half + 1) * 512)
                for ct in range(16):
                    S.op("pe", lambda e, ct=ct, hs=hs: e.matmul(psF[:, 0, :], lhsT=yT_s[:, ct, :], rhs=WO[:, ct, hs],
                                                                start=(ct == 0), stop=(ct == 15)),
                         reads=[Bq["yT"], B_w], writes=[B_ps[0]])
                for ct in range(16, 24):
                    S.op("pe", lambda e, ct=ct, hs=hs: e.matmul(psF[:, 1, :], lhsT=yT_s[:, ct, :], rhs=WO[:, ct, hs],
                                                                start=(ct == 16), stop=(ct == 23)),
                         reads=[Bq["yT"], B_w], writes=[B_ps[1]])
                S.op("dve", lambda e, hs=hs: e.tensor_tensor(out=tmpo[:, hs], in0=psF[:, 0, :], in1=g12[:, hs], op=ALU.mult),
                     reads=[B_ps[0], Bq["g12"]], writes=[Bq["tmpo"]])
                S.op("dve", lambda e, hs=hs, hs2=hs2: e.tensor_tensor(out=junk[:, hs], in0=psF[:, 1, :], in1=g12[:, hs2], op=ALU.mult),
                     reads=[B_ps[1], Bq["g12"]], writes=[Bq["junk"]])
                S.op("pool", lambda e, hs=hs: e.tensor_tensor(out=tmpo[:, hs], in0=tmpo[:, hs], in1=junk[:, hs], op=ALU.add),
                     reads=[Bq["junk"], Bq["tmpo"]], writes=[Bq["tmpo"]])
                S.op("pool", lambda e, hs=hs: e.tensor_tensor(out=xm_p[:, hs], in0=tmpo[:, hs], in1=xin_f[:, hs], op=ALU.add),
                     reads=[Bq["tmpo"], Bq["xin"]], writes=[Bq["xm%d" % par]])
            if dbg:
                S.dma("sp", lambda e, r0=r0, r1=r1: e.dma_start(out=XM[r0:r1, :], in_=xm_p), reads=[Bq["xm%d" % par]])
            S.op("act", lambda e: e.activation(out=junk, in_=xm_p, func=AF.Square, scale=1.0 / 32, accum_out=ss[:, 0:1]),
                 reads=[Bq["xm%d" % par]], writes=[Bq["junk"], B_ss])
            S.op("dve", lambda e: e.tensor_scalar_add(out=ss[:, 0:1], in0=ss[:, 0:1], scalar1=1e-6), reads=[B_ss], writes=[B_ss])
            S.op("act", lambda e: e.activation(out=ss[:, 0:1], in_=ss[:, 0:1], func=AF.Ln), reads=[B_ss], writes=[B_ss])
            S.op("act", lambda e: e.activation(out=ss[:, 0:1], in_=ss[:, 0:1], func=AF.Exp, scale=-0.5), reads=[B_ss], writes=[B_ss])
            S.op("dve", lambda e: e.scalar_tensor_tensor(out=tt_p, in0=xm_p, scalar=ss[:, 0:1], in1=gffn_s, op0=ALU.mult, op1=ALU.mult),
                 reads=[Bq["xm%d" % par], B_ss, B_const], writes=[Bq["tt%d" % par]])
            for k in range(8):
                S.op("pe", lambda e, k=k: e.transpose(out=psF[:, 2 + k // 4, (k % 4) * 128:(k % 4 + 1) * 128],
                                                      in_=tt_p[:, k * 128:(k + 1) * 128], identity=identF[:]),
                     reads=[Bq["tt%d" % par], B_const], writes=[B_ps[2 + k // 4]])
            S.op("act", lambda e: e.copy(out=tT_s[:, 0:512], in_=psF[:, 2, :]), reads=[B_ps[2]], writes=[Bq["tT"]])
            S.op("dve", lambda e: e.tensor_copy(out=tT_s[:, 512:1024], in_=psF[:, 3, :]), reads=[B_ps[3]], writes=[Bq["tT"]])
            for cq in range(16):
                yield
                j = cq % 2
                bank = 4 + (cq // 4) % 2
                S.dma("sp", lambda e, cq=cq, j=j: e.dma_start(out=wpst[j], in_=wpq_d[cq].rearrange("p k n -> p (k n)")), writes=[B_wp[j]])
                for k in range(8):
                    S.op("pe", lambda e, cq=cq, k=k, j=j, bank=bank: e.matmul(
                        psF[:, bank, (cq % 4) * 128:(cq % 4 + 1) * 128], lhsT=wpst[j][:, k * 128:(k + 1) * 128],
                        rhs=tT_s[:, k * 128:(k + 1) * 128], start=(k == 0), stop=(k == 7)),
                        reads=[B_wp[j], Bq["tT"]], writes=[B_ps[bank]])
                if cq % 4 == 3:
                    q0 = (cq // 4) * 512
                    S.op("act" if (cq // 4) % 2 == 0 else "dve",
                         (lambda e, bank=bank, q0=q0: e.copy(out=qT_s[:, q0:q0 + 512], in_=psF[:, bank, :])) if (cq // 4) % 2 == 0 else
                         (lambda e, bank=bank, q0=q0: e.tensor_copy(out=qT_s[:, q0:q0 + 512], in_=psF[:, bank, :])),
                         reads=[B_ps[bank]], writes=[Bq["qT"]])
            for hk in range(16):
                yield
                bank = 2 + (hk // 4) % 2
                S.op("pe", lambda e, hk=hk, bank=bank: e.matmul(psF[:, bank, (hk % 4) * 128:(hk % 4 + 1) * 128],
                                                                lhsT=qT_s[:, hk * 128:(hk + 1) * 128], rhs=skT_s[:, hk % 2, :],
                                                                start=True, stop=True),
                     reads=[Bq["qT"], B_const], writes=[B_ps[bank]])
                if hk % 4 == 3:
                    k0 = hk - 3
                    S.op("act", lambda e, bank=bank, k0=k0: e.copy(out=s_s[:, k0:k0 + 4, :].rearrange("p k n -> p (k n)"), in_=psF[:, bank, :]),
                         reads=[B_ps[bank]], writes=[Bq["s"]])
            for hk in range(16):
                yield
                S.op("dve", lambda e, hk=hk: e.max(out=top[:, hk, 0:8], in_=s_s[:, hk, :]), reads=[Bq["s"]], writes=[Bq["top"]])
                S.op("dve", lambda e, hk=hk: e.max_index(out=idxu[:, hk, 0:8], in_max=top[:, hk, 0:8], in_values=s_s[:, hk, :]),
                     reads=[Bq["s"], Bq["top"]], writes=[Bq["idxu"]])
                S.op("dve", lambda e, hk=hk: e.match_replace(out=work, in_to_replace=top[:, hk, 0:8], in_values=s_s[:, hk, :],
                                                             imm_value=-1e30),
                     reads=[Bq["s"], Bq["top"]], writes=[Bq["work"]])
                S.op("dve", lambda e, hk=hk: e.max(out=top[:, hk, 8:16], in_=work), reads=[Bq["work"]], writes=[Bq["top"]])
                S.op("dve", lambda e, hk=hk: e.max_index(out=idxu[:, hk, 8:16], in_max=top[:, hk, 8:16], in_values=work),
                     reads=[Bq["work"], Bq["top"]], writes=[Bq["idxu"]])
            S.op("dve", lambda e: e.tensor_copy(out=idxf, in_=idxu), reads=[Bq["idxu"]], writes=[Bq["idxf"]])
            for h in range(8):
                yield
                S.op("dve", lambda e, h=h: e.tensor_tensor(out=cand, in0=top[:, 2 * h, :].unsqueeze(2).to_broadcast([128, 16, 16]),
                                                           in1=top[:, 2 * h + 1, :].unsqueeze(1).to_broadcast([128, 16, 16]), op=ALU.add),
                     reads=[Bq["top"]], writes=[Bq["cand"]])
                S.op("pool", lambda e, h=h: e.tensor_scalar_mul(out=posf, in0=idxf[:, 2 * h, :], scalar1=128.0),
                     reads=[Bq["idxf"]], writes=[Bq["posf"]])
                S.op("pool", lambda e, h=h: e.tensor_tensor(out=cidx, in0=posf.unsqueeze(2).to_broadcast([128, 16, 16]),
                                                            in1=idxf[:, 2 * h + 1, :].unsqueeze(1).to_broadcast([128, 16, 16]), op=ALU.add),
                     reads=[Bq["idxf"], Bq["posf"]], writes=[Bq["cidx"]])
                candf = cand.rearrange("p a b -> p (a b)")
                cidxf = cidx.rearrange("p a b -> p (a b)")
                S.op("dve", lambda e: e.max(out=best[:, 0:8], in_=candf), reads=[Bq["cand"]], writes=[Bq["best"]])
                S.op("dve", lambda e: e.max_index(out=posu[:, 0:8], in_max=best[:, 0:8], in_values=candf),
                     reads=[Bq["cand"], Bq["best"]], writes=[Bq["posu"]])
                S.op("dve", lambda e: e.match_replace(out=work2, in_to_replace=best[:, 0:8], in_values=candf, imm_value=-1e30),
                     reads=[Bq["cand"], Bq["best"]], writes=[Bq["work2"]])
                S.op("dve", lambda e: e.max(out=best[:, 8:16], in_=work2), reads=[Bq["work2"]], writes=[Bq["best"]])
                S.op("dve", lambda e: e.max_index(out=posu[:, 8:16], in_max=best[:, 8:16], in_values=work2),
                     reads=[Bq["work2"], Bq["best"]], writes=[Bq["posu"]])
                S.op("dve", lambda e: e.tensor_copy(out=posf, in_=posu), reads=[Bq["posu"], Bq["cidx"]], writes=[Bq["posf"]])
                for k in range(16):
                    yield
                    S.op("dve", lambda e, h=h, k=k: e.scalar_tensor_tensor(out=work2, in0=iota_f, scalar=posf[:, k:k + 1],
                                                                         in1=cidxf, op0=ALU.is_equal, op1=ALU.mult,
                                                                         accum_out=ef[:, h * 16 + k:h * 16 + k + 1]),
                         reads=[Bq["posf"], Bq["cidx"], B_const, Bq["work2"]], writes=[Bq["work2"], Bq["ef"]])
                S.op("dve", lambda e: e.tensor_scalar_mul(out=smz[:, 0:1], in0=best[:, 0:1], scalar1=-1.0),
                     reads=[Bq["best"]], writes=[Bq["smz"]])
                S.op("act", lambda e, h=h: e.activation(out=gts_p[:, h * 16:(h + 1) * 16], in_=best, func=AF.Exp, bias=smz[:, 0:1],
                                                        accum_out=smz[:, 1:2]),
                     reads=[Bq["best"], Bq["smz"]], writes=[Bq["gts%d" % par], Bq["smz"]])
                S.op("dve", lambda e: e.reciprocal(out=smz[:, 2:3], in_=smz[:, 1:2]), reads=[Bq["smz"]], writes=[Bq["smz"]])
                S.op("dve", lambda e, h=h: e.tensor_scalar(out=gts_p[:, h * 16:(h + 1) * 16], in0=gts_p[:, h * 16:(h + 1) * 16],
                                                           scalar1=smz[:, 2:3], scalar2=None, op0=ALU.mult),
                     reads=[Bq["gts%d" % par], Bq["smz"]], writes=[Bq["gts%d" % par]])
            S.op("dve", lambda e: e.tensor_copy(out=ei_p, in_=ef), reads=[Bq["ef"]], writes=[Bq["ei%d" % par]])
            yield

        def stage2(i):
            par = i % 2
            r0, r1 = i * 128, (i + 1) * 128
            tt_p, xm_p, ei_p, gts_p = PARV["tt"][par], PARV["xm"][par], PARV["ei"][par], PARV["gts"][par]
            for m in range(128):
                j = ngbc[0] % NG
                ngbc[0] += 1
                yield
                S.dma("pool", lambda e, m=m, j=j: e.indirect_dma_start(
                    out=gring[j], out_offset=None, in_=EUb[:, :],
                    in_offset=bass.IndirectOffsetOnAxis(ap=ei_p[:, m:m + 1], axis=0)),
                    reads=[Bq["ei%d" % par]], writes=[B_gr[j]])
                S.op("dve", lambda e, m=m, j=j: e.scalar_tensor_tensor(out=junk2, in0=gring[j], scalar=1.0, in1=tt_p, op0=ALU.mult,
                                                                      op1=ALU.mult, accum_out=actv[:, m:m + 1]),
                     reads=[B_gr[j], Bq["tt%d" % par]], writes=[Bq["junk2"], Bq["actv"]])
            S.op("act", lambda e: e.activation(out=wgt, in_=actv, func=AF.Gelu), reads=[Bq["actv"]], writes=[Bq["wgt"]])
            S.op("dve", lambda e: e.tensor_tensor(out=wgt, in0=wgt, in1=gts_p, op=ALU.mult),
                 reads=[Bq["wgt"], Bq["gts%d" % par]], writes=[Bq["wgt"]])
            for m in range(128):
                j = ngbc[0] % NG
                ngbc[0] += 1
                yield
                S.dma("pool", lambda e, m=m, j=j: e.indirect_dma_start(
                    out=gring[j], out_offset=None, in_=EVb[:, :],
                    in_offset=bass.IndirectOffsetOnAxis(ap=ei_p[:, m:m + 1], axis=0)),
                    reads=[Bq["ei%d" % par]], writes=[B_gr[j]])
                S.op("dve", lambda e, m=m, j=j: e.scalar_tensor_tensor(out=xm_p, in0=gring[j], scalar=wgt[:, m:m + 1], in1=xm_p,
                                                                      op0=ALU.mult, op1=ALU.add),
                     reads=[B_gr[j], Bq["wgt"], Bq["xm%d" % par]], writes=[Bq["xm%d" % par]])
            S.dma("sp", lambda e, r0=r0, r1=r1: e.dma_start(out=out_d[r0:r1, :], in_=xm_p), reads=[Bq["xm%d" % par]])
            yield

        g1 = stage1(0)
        for _ in g1:
            pass
        for i in range(NSH):
            g2 = stage2(i)
            g1 = stage1(i + 1) if i + 1 < NSH else None
            while True:
                alive = False
                for _k in range(4):
                    try:
                        next(g2)
                        alive = True
                    except StopIteration:
                        break
                for _k in range(3):
                    if g1 is None:
                        break
                    try:
                        next(g1)
                        alive = True
                    except StopIteration:
                        g1 = None
                if not alive:
                    break
        S.barrier()
        S.emit()
    return nc


OFF_XBC, OFF_DT, OFF_QKV, OFF_GATE = 2048, 5120, 5184, 8256


def na_bias_tables(rpb, rows):
    NTl = rows // 2
    pats = [0, 1, 2, NTl - 2, NTl - 1]
    out = np.full((5, 128, 5, 16, 128), -30000.0, np.float32)
    for pi, t in enumerate(pats):
        kt0 = min(max(t - 2, 0), NTl - 5)
        for qi in range(128):
            r = 2 * t + qi // 64
            c = qi % 64
            r0 = min(max(r - 4, 0), rows - 8)
            c0 = min(max(c - 8, 0), 64 - 16)
            for kr in range(r0, r0 + 8):
                kb = kr // 2 - kt0
                assert 0 <= kb < 5
                kp0 = (kr % 2) * 64
                cols = np.arange(c0, c0 + 16)
                out[pi, kp0 + cols, kb, :, qi] = rpb[:, kr - r + 7, cols - c + 15].T
    return out.reshape(5, 128, 5 * 16 * 128)


def prep_common(inp, L):
    w_in = inp["w_in"][0]
    m = {}
    m["gmix"] = np.ascontiguousarray(inp["g_mix"][0].reshape(8, 128).T)
    colsF = np.concatenate([np.arange(OFF_XBC, OFF_XBC + 3072), np.arange(OFF_QKV, OFF_QKV + 2048)])
    wf = w_in[:, colsF].reshape(8, 128, NF, 128)
    m["wf"] = np.ascontiguousarray(wf.transpose(2, 1, 0, 3))
    colsT = np.concatenate([np.arange(0, 2048), np.arange(OFF_DT, OFF_DT + 64),
                            np.arange(OFF_QKV + 2048, OFF_QKV + 3072), np.arange(OFF_GATE, OFF_GATE + 2048)])
    wt = w_in[:, colsT].reshape(8, 128, NTC)
    m["wt"] = np.ascontiguousarray(wt.transpose(1, 0, 2))
    cw = inp["conv_w"][0]
    m["convw"] = np.ascontiguousarray(cw.reshape(5, 24, 128).transpose(2, 1, 0).reshape(128, 120))
    cb = inp["conv_b"][0]
    m["convbT"] = np.ascontiguousarray(cb.reshape(24, 128).T)
    m["convbrow"] = np.ascontiguousarray(cb.reshape(1, 3072))
    m["dtb"] = np.ascontiguousarray(np.broadcast_to(inp["dt_bias"][0].reshape(1, 64), (128, 64)))
    m["alog"] = np.ascontiguousarray(np.broadcast_to(inp["a_log"][0].reshape(1, 64), (128, 64)))
    m["dskip"] = np.ascontiguousarray(np.broadcast_to(np.repeat(inp["d_skip"][0], 64).reshape(1, 2048), (128, 2048)))
    m["sng"] = np.ascontiguousarray(np.broadcast_to(inp["ssd_norm_g"][0].reshape(1, 2048), (128, 2048)))
    m["qg"] = np.ascontiguousarray(np.tile(inp["q_norm_g"][0], 2).reshape(128, 1))
    m["kg"] = np.ascontiguousarray(np.tile(inp["k_norm_g"][0], 2).reshape(128, 1))
    m["bt"] = na_bias_tables(inp["rpb"][0], L // 64)
    m["wo"] = np.ascontiguousarray(inp["w_out"][0].reshape(24, 128, 1024).transpose(1, 0, 2))
    m["gffn"] = np.ascontiguousarray(np.broadcast_to(inp["g_ffn"][0].reshape(1, 1024), (128, 1024)))
    m["wpq"] = np.ascontiguousarray(inp["w_pq"][0].reshape(8, 128, 16, 128).transpose(2, 1, 0, 3))
    m["skT"] = np.ascontiguousarray(inp["sub_keys"][0].transpose(2, 0, 1))
    m["eu"] = np.ascontiguousarray(inp["expert_u"][0])
    m["ev"] = np.ascontiguousarray(inp["expert_v"][0])
    return {k: np.asarray(v, np.float32) for k, v in m.items()}


_NC_CACHE = {}


def core_shares(NT):
    groups = {0: [0, 3, 6], 1: [1, 4, 7], 2: [2, 5]}
    shares = {}
    for sq, cores in groups.items():
        parts = np.array_split(np.arange(NT), len(cores))
        for c, p in zip(cores, parts):
            shares[c] = [int(v) for v in p]
    NSH = (NT + 1) // 2
    return shares, NSH


def share_rowidx(tiles, NSH):
    tl = list(tiles) + [tiles[-1]] * (NSH - len(tiles))
    idx = np.array(tl, np.int32)[None, :] * 128 + np.arange(128, dtype=np.int32)[:, None]
    return np.ascontiguousarray(idx.astype(np.int32))


def kernel(**inp):
    L = inp["x_prompt"].shape[1]
    seqs = [inp["x_prompt"][0], inp["x_prompt"][1], inp["x_sample"][0]]
    common = prep_common(inp, L)
    if L not in _NC_CACHE:
        _NC_CACHE[L] = build(L)
    nc = _NC_CACHE[L]
    shares, NSH = core_shares(L // 128)
    in_maps = []
    for c in range(8):
        m = dict(common)
        m["x"] = np.ascontiguousarray(seqs[c % 3], dtype=np.float32)
        m["rowidx"] = share_rowidx(shares[c], NSH)
        in_maps.append(m)
    res = run_bass_kernel_spmd(nc, in_maps, core_ids=list(range(8)))
    outs = [np.empty((L, D), np.float32) for _ in range(3)]
    for c in range(8):
        o = np.asarray(res.results[c]["out"], np.float32)
        for i, t in enumerate(shares[c]):
            outs[c % 3][t * 128:(t + 1) * 128] = o[i * 128:(i + 1) * 128]
    return (np.stack(outs[0:2], 0), outs[2][None])
```

```python
import contextlib
import numpy as np
import concourse.bass as bass
import concourse.mybir as mybir
from concourse.bass_utils import run_bass_kernel_spmd

F32 = mybir.dt.float32
BF16 = mybir.dt.bfloat16
I32 = mybir.dt.int32
U32 = mybir.dt.uint32
AF = mybir.ActivationFunctionType
ALU = mybir.AluOpType

ENGS = ("pe", "act", "dve", "pool", "sp")
NDMASEM = 12


class Buf:
    __slots__ = ("name", "w", "r")

    def __init__(self, name=""):
        self.name = name
        self.w = None
        self.r = []


class Sched:
    def __init__(self, nc):
        self.nc = nc
        self.q = {e: [] for e in ENGS}
        self.nop = {e: 0 for e in ENGS}
        self.opent = {e: [] for e in ENGS}
        self.seen = {e: {} for e in ENGS}
        self.dma_cnt = {(q, j): 0 for q in ("sp", "pool", "act") for j in range(NDMASEM)}
        self.dma_rr = {"sp": 0, "pool": 0, "act": 0}

    def _deps(self, eng, reads, writes):
        need = {}

        def add(tok):
            if tok is None:
                return
            k, v = tok
            if need.get(k, 0) < v:
                need[k] = v
        for b in reads:
            add(b.w)
        for b in writes:
            add(b.w)
            for t in b.r:
                add(t)
        waits = []
        seen = self.seen[eng]
        for k, v in need.items():
            if seen.get(k, 0) < v:
                seen[k] = v
                waits.append((k, v))
                if k[0] == "e":
                    self.opent[k[1]][v - 1][3] = True
        return waits

    def _commit(self, tok, reads, writes):
        for b in reads:
            b.r.append(tok)
            if len(b.r) > 16:
                m = {}
                for k, v in b.r:
                    if m.get(k, 0) < v:
                        m[k] = v
                b.r = list(m.items())
        for b in writes:
            b.w = tok
            b.r = []

    def op(self, eng, fn, reads=(), writes=()):
        waits = self._deps(eng, reads, writes)
        self.nop[eng] += 1
        tok = (("e", eng), self.nop[eng])
        ent = [waits, fn, "e", False]
        self.q[eng].append(ent)
        self.opent[eng].append(ent)
        self._commit(tok, reads, writes)
        return tok

    def dma(self, eng, fn, reads=(), writes=()):
        waits = self._deps(eng, reads, writes)
        j = self.dma_rr[eng]
        self.dma_rr[eng] = (j + 1) % NDMASEM
        self.dma_cnt[(eng, j)] += 16
        tok = (("d", eng, j), self.dma_cnt[(eng, j)])
        self.q[eng].append([waits, fn, ("d", eng, j), True])
        self._commit(tok, reads, writes)
        return tok

    def barrier(self):
        allw = [(("e", e), self.nop[e]) for e in ENGS if self.nop[e] > 0]
        allw += [(("d", q, j), v) for (q, j), v in self.dma_cnt.items() if v > 0]
        for e in ENGS:
            waits = []
            for k, v in allw:
                if self.seen[e].get(k, 0) < v:
                    self.seen[e][k] = v
                    waits.append((k, v))
                    if k[0] == "e":
                        self.opent[k[1]][v - 1][3] = True
            self.q[e].append([waits, None, None, False])

    def emit(self):
        nc = self.nc
        real = {}
        for e in ENGS:
            c = 0
            arr = [0]
            for ent in self.opent[e]:
                if ent[3]:
                    c += 1
                arr.append(c)
            real[e] = arr
        with contextlib.ExitStack() as st:
            sems = {}
            for e in ENGS:
                sems[("e", e)] = st.enter_context(nc.semaphore("s_" + e))
            for (q, j), v in self.dma_cnt.items():
                if v > 0:
                    sems[("d", q, j)] = st.enter_context(nc.semaphore("d_%s_%d" % (q, j)))
            block = st.enter_context(nc.Block())

            def run(ename):
                def body(eng):
                    for waits, fn, kind, marked in self.q[ename]:
                        for k, v in waits:
                            if k[0] == "e":
                                eng.wait_ge(sems[k], real[k[1]][v])
                            else:
                                eng.wait_ge(sems[k], v)
                        if fn is not None:
                            ins = fn(eng)
                            if kind == "e":
                                if marked:
                                    ins.then_inc(sems[("e", ename)], 1)
                            else:
                                ins.then_inc(sems[kind], 16)
                return body
            block.tensor(run("pe"))
            block.scalar(run("act"))
            block.vector(run("dve"))
            block.gpsimd(run("pool"))
            block.sync(run("sp"))


D = 1024
NF = 40
NTC = 5184
NEXP = 16384


def build(L, dbg=False, NSH=None):
    NT = L // 128
    if NSH is None:
        NSH = (NT + 1) // 2
    ROWS = L // 64
    nc = bass.Bass("TRN2", target_bir_lowering=False)
    dt_in = lambda n, s, d=F32: nc.dram_tensor(n, s, d, kind="ExternalInput")
    x_d = dt_in("x", [L, D])
    gmix_d = dt_in("gmix", [128, 8])
    wf_d = dt_in("wf", [NF, 128, 8, 128])
    wt_d = dt_in("wt", [128, 8, NTC])
    convw_d = dt_in("convw", [128, 120])
    convbT_d = dt_in("convbT", [128, 24])
    convbrow_d = dt_in("convbrow", [1, 3072])
    dtb_d = dt_in("dtb", [128, 64])
    alog_d = dt_in("alog", [128, 64])
    dskip_d = dt_in("dskip", [128, 2048])
    sng_d = dt_in("sng", [128, 2048])
    qg_d = dt_in("qg", [128, 1])
    kg_d = dt_in("kg", [128, 1])
    bt_d = dt_in("bt", [5, 128, 5 * 16 * 128])
    wo_d = dt_in("wo", [128, 24, 1024])
    gffn_d = dt_in("gffn", [128, 1024])
    wpq_d = dt_in("wpq", [16, 128, 8, 128])
    skT_d = dt_in("skT", [128, 2, 128])
    eu_d = dt_in("eu", [NEXP, D])
    ev_d = dt_in("ev", [NEXP, D])
    rowidx_d = nc.dram_tensor("rowidx", [128, NSH], I32, kind="ExternalInput")
    out_d = nc.dram_tensor("out", [NSH * 128, D], F32, kind="ExternalOutput")
    sk = "ExternalOutput" if dbg else "Internal"
    PF = nc.dram_tensor("PF", [NF * 128, L], BF16, kind=sk)
    PTz = nc.dram_tensor("PTz", [L, 2048], BF16, kind=sk)
    PTdt = nc.dram_tensor("PTdt", [L, 64], F32, kind=sk)
    PTv = nc.dram_tensor("PTv", [L, 1024], BF16, kind=sk)
    PTg = nc.dram_tensor("PTg", [L, 2048], BF16, kind=sk)
    XC = nc.dram_tensor("XC", [L, 2048], BF16, kind=sk)
    BC = nc.dram_tensor("BC", [L, 512], BF16, kind=sk)
    BCT = nc.dram_tensor("BCT", [512, L], BF16, kind=sk)
    CCT = nc.dram_tensor("CCT", [512, L], BF16, kind=sk)
    YF = nc.dram_tensor("YF", [L, 2048], F32, kind=sk)
    YTOK = nc.dram_tensor("YTOK", [L, 3072], BF16, kind=sk)
    EUb = nc.dram_tensor("EUb", [NEXP, D], BF16)
    EVb = nc.dram_tensor("EVb", [NEXP, D], BF16)
    XM = nc.dram_tensor("XM", [NSH * 128, D], F32, kind=sk)
    DBGF = nc.dram_tensor("DBGF", [128, 4096], F32, kind=sk)
    QN = nc.dram_tensor("QN", [1024, L], BF16, kind=sk)
    KN = nc.dram_tensor("KN", [1024, L], BF16, kind=sk)
    DBGB = nc.dram_tensor("DBGB", [128, 4096], BF16, kind=sk)

    S = Sched(nc)
    with contextlib.ExitStack() as st:
        sb = lambda n, s, d: st.enter_context(nc.sbuf_tensor(n, s, d))
        pst = lambda n, s, d: st.enter_context(nc.psum_tensor(n, s, d))
        identB = sb("identB", [128, 128], BF16)
        identF = sb("identF", [128, 128], F32)
        triU = sb("triU", [128, 128], F32)
        triLo = sb("triLo", [128, 128], F32)
        triLs = sb("triLs", [128, 128], F32)
        triUs = sb("triUs", [128, 128], F32)
        onesF = sb("onesF", [128, 128], F32)
        onesB = sb("onesB", [128, 128], BF16)
        gmix = sb("gmix_s", [128, 8], F32)
        wbuf = sb("wbuf", [128, 41472], BF16)
        fbuf = sb("fbuf", [128, 16384], F32)
        scr = sb("scr", [128, 6144], F32)
        scrB = scr[:, :].bitcast(BF16)
        psF = pst("psF", [128, 6, 512], F32)
        psB = pst("psB", [128, 2, 1024], BF16)
        B_const = Buf("const")
        B_w = Buf("wbuf")
        B_f = Buf("fbuf")
        B_ps = [Buf("ps%d" % i) for i in range(6)]
        B_psb = [Buf("psb%d" % i) for i in range(2)]

        def tri(t, pattern, cm, cmp):
            S.op("pool", lambda e: e.memset(t[:], 1.0), writes=[B_const])
            S.op("pool", lambda e: e.affine_select(out=t[:], in_=t[:], pattern=pattern, compare_op=cmp,
                                                   fill=0.0, base=0, channel_multiplier=cm),
                 reads=[B_const], writes=[B_const])
        S.op("pool", lambda e: e.memset(identF[:], 0.0), writes=[B_const])
        S.op("pool", lambda e: e.affine_select(out=identF[:], in_=identF[:], pattern=[[-1, 128]],
                                               compare_op=ALU.not_equal, fill=1.0, base=0, channel_multiplier=1),
             reads=[B_const], writes=[B_const])
        S.op("pool", lambda e: e.tensor_copy(out=identB[:], in_=identF[:]), reads=[B_const], writes=[B_const])
        tri(triU, [[1, 128]], -1, ALU.is_ge)
        tri(triLo, [[-1, 128]], 1, ALU.is_ge)
        tri(triLs, [[-1, 128]], 1, ALU.is_gt)
        tri(triUs, [[1, 128]], -1, ALU.is_gt)
        S.op("pool", lambda e: e.memset(onesF[:], 1.0), writes=[B_const])
        S.op("pool", lambda e: e.memset(onesB[:], 1.0), writes=[B_const])
        S.dma("sp", lambda e: e.dma_start(out=gmix[:], in_=gmix_d[:, :]), writes=[B_const])
        S.barrier()

        xt = [fbuf[:, i * 1024:(i + 1) * 1024] for i in range(2)]
        B_xt = [Buf() for _ in range(2)]
        sq = fbuf[:, 2048:4096]
        B_sq = Buf()
        ss = sb("ss", [128, 2], F32)
        B_ss = Buf()
        xs = scrB[:, 4096:5120]
        B_xs = Buf()
        hT = scrB[:, 0:4096].rearrange("p (k t) -> p k t", k=8)
        B_hT = Buf()
        fe_n = [0]

        def rms_rstd(src, Bsrc, width):
            sc = 1.0 / np.sqrt(width)
            S.op("act", lambda e: e.activation(out=sq[:, 0:width], in_=src, func=AF.Square, scale=float(sc),
                                               accum_out=ss[:, 0:1]),
                 reads=[Bsrc], writes=[B_sq, B_ss])
            S.op("dve", lambda e: e.tensor_scalar_add(out=ss[:, 0:1], in0=ss[:, 0:1], scalar1=1e-6),
                 reads=[B_ss], writes=[B_ss])
            S.op("act", lambda e: e.activation(out=ss[:, 0:1], in_=ss[:, 0:1], func=AF.Ln), reads=[B_ss], writes=[B_ss])
            S.op("act", lambda e: e.activation(out=ss[:, 0:1], in_=ss[:, 0:1], func=AF.Exp, scale=-0.5),
                 reads=[B_ss], writes=[B_ss])

        def front_end(i, tt):
            j = fe_n[0] % 2
            fe_n[0] += 1
            S.dma("sp", lambda e: e.dma_start(out=xt[j], in_=x_d[i * 128:(i + 1) * 128, :]), writes=[B_xt[j]])
            rms_rstd(xt[j], B_xt[j], D)
            S.op("dve", lambda e: e.tensor_scalar(out=xs, in0=xt[j], scalar1=ss[:, 0:1], scalar2=None,
                                                  op0=ALU.mult),
                 reads=[B_xt[j], B_ss], writes=[B_xs])
            for k in range(8):
                S.op("pe", lambda e, k=k: e.transpose(out=psB[:, 0, k * 128:(k + 1) * 128],
                                                      in_=xs[:, k * 128:(k + 1) * 128], identity=identB[:]),
                     reads=[B_xs, B_const], writes=[B_psb[0]])
            for k in range(8):
                if k % 2 == 0:
                    S.op("act", lambda e, k=k: e.activation(out=hT[:, k, tt * 128:(tt + 1) * 128],
                                                            in_=psB[:, 0, k * 128:(k + 1) * 128],
                                                            func=AF.Copy, scale=gmix[:, k:k + 1]),
                         reads=[B_psb[0], B_const], writes=[B_hT])
                else:
                    S.op("dve", lambda e, k=k: e.tensor_scalar(out=hT[:, k, tt * 128:(tt + 1) * 128],
                                                               in0=psB[:, 0, k * 128:(k + 1) * 128],
                                                               scalar1=gmix[:, k:k + 1], scalar2=None, op0=ALU.mult),
                         reads=[B_psb[0], B_const], writes=[B_hT])

        wst = [fbuf[:, 4096 + i * 1024:4096 + (i + 1) * 1024] for i in range(2)]
        B_wst = [Buf() for _ in range(2)]
        WF = wbuf[:, 0:NF * 1024].rearrange("p (c k n) -> p c k n", c=NF, k=8)
        for ct in range(NF):
            j = ct % 2
            S.dma("sp", lambda e, ct=ct, j=j: e.dma_start(out=wst[j], in_=wf_d[ct].rearrange("p k n -> p (k n)")),
                  writes=[B_wst[j]])
            eng = "dve" if ct % 2 == 0 else "pool"
            S.op(eng, lambda e, ct=ct, j=j: e.tensor_copy(out=wbuf[:, ct * 1024:(ct + 1) * 1024], in_=wst[j]),
                 reads=[B_wst[j]], writes=[B_w])
        stg = [scrB[:, 5120 + i * 512:5120 + (i + 1) * 512] for i in range(4)]
        B_stg = [Buf() for _ in range(4)]
        n_ev = 0
        for b in range(L // 512):
            for tt in range(4):
                front_end(b * 4 + tt, tt)
            for ct in range(NF):
                bank = ct % 4
                for k in range(8):
                    S.op("pe", lambda e, ct=ct, k=k, bank=bank: e.matmul(psF[:, bank, :], lhsT=WF[:, ct, k, :],
                                                                          rhs=hT[:, k, :], start=(k == 0), stop=(k == 7)),
                         reads=[B_w, B_hT], writes=[B_ps[bank]])
                sj = n_ev % 4
                n_ev += 1
                if sj % 2 == 0:
                    S.op("act", lambda e, bank=bank, sj=sj: e.copy(out=stg[sj], in_=psF[:, bank, :]),
                         reads=[B_ps[bank]], writes=[B_stg[sj]])
                else:
                    S.op("dve", lambda e, bank=bank, sj=sj: e.tensor_copy(out=stg[sj], in_=psF[:, bank, :]),
                         reads=[B_ps[bank]], writes=[B_stg[sj]])
                S.dma("sp", lambda e, ct=ct, b=b, sj=sj: e.dma_start(
                    out=PF[ct * 128:(ct + 1) * 128, b * 512:(b + 1) * 512], in_=stg[sj]),
                    reads=[B_stg[sj]])
        S.barrier()

        WT = wbuf[:, 0:8 * NTC].rearrange("p (k n) -> p k n", k=8)
        for k in range(8):
            for c0 in range(0, NTC, 1024):
                c1 = min(NTC, c0 + 1024)
                j = (c0 // 1024) % 2
                S.dma("sp", lambda e, k=k, c0=c0, c1=c1, j=j: e.dma_start(out=wst[j][:, 0:c1 - c0], in_=wt_d[:, k, c0:c1]),
                      writes=[B_wst[j]])
                S.op("dve", lambda e, k=k, c0=c0, c1=c1, j=j: e.tensor_copy(out=WT[:, k, c0:c1], in_=wst[j][:, 0:c1 - c0]),
                     reads=[B_wst[j]], writes=[B_w])
        stz = scrB[:, 7168:9216]
        stv = scrB[:, 9216:10240]
        stgt = scrB[:, 10240:12288]
        stdt = sb("stdt", [128, 64], F32)
        B_stz, B_stv, B_stgt, B_stdt = Buf(), Buf(), Buf(), Buf()
        nb = 0
        for i in range(NT):
            front_end(i, 0)

            def tgroup(c0, n, func, dst, Bdst, dcol):
                nonlocal nb
                bank = nb % 4
                nb += 1
                for k in range(8):
                    S.op("pe", lambda e, k=k: e.matmul(psF[:, bank, 0:n], lhsT=hT[:, k, 0:128], rhs=WT[:, k, c0:c0 + n],
                                                       start=(k == 0), stop=(k == 7)),
                         reads=[B_w, B_hT], writes=[B_ps[bank]])
                S.op("act", lambda e: e.activation(out=dst[:, dcol:dcol + n], in_=psF[:, bank, 0:n], func=func),
                     reads=[B_ps[bank]], writes=[Bdst])
            for q in range(4):
                tgroup(q * 512, 512, AF.Silu, stz, B_stz, q * 512)
            tgroup(2048, 64, AF.Copy, stdt, B_stdt, 0)
            for q in range(2):
                tgroup(2112 + q * 512, 512, AF.Copy, stv, B_stv, q * 512)
            for q in range(4):
                tgroup(3136 + q * 512, 512, AF.Sigmoid, stgt, B_stgt, q * 512)
            r0, r1 = i * 128, (i + 1) * 128
            S.dma("sp", lambda e, r0=r0, r1=r1: e.dma_start(out=PTz[r0:r1, :], in_=stz), reads=[B_stz])
            S.dma("sp", lambda e, r0=r0, r1=r1: e.dma_start(out=PTdt[r0:r1, :], in_=stdt[:]), reads=[B_stdt])
            S.dma("sp", lambda e, r0=r0, r1=r1: e.dma_start(out=PTv[r0:r1, :], in_=stv), reads=[B_stv])
            S.dma("sp", lambda e, r0=r0, r1=r1: e.dma_start(out=PTg[r0:r1, :], in_=stgt), reads=[B_stgt])
        S.barrier()

        convw_s = sb("convw_s", [128, 120], F32)
        convbT_s = sb("convbT_s", [128, 24], F32)
        convbrow_s = sb("convbrow_s", [1, 3072], F32)
        dtb_s = sb("dtb_s", [128, 64], F32)
        aneg = sb("aneg", [128, 64], F32)
        dskip_s = sb("dskip_s", [128, 2048], F32)
        sng_s = sb("sng_s", [128, 2048], F32)
        for dst, src in ((convw_s, convw_d), (convbT_s, convbT_d), (convbrow_s, convbrow_d), (dtb_s, dtb_d),
                         (aneg, alog_d), (dskip_s, dskip_d), (sng_s, sng_d)):
            S.dma("sp", lambda e, dst=dst, src=src: e.dma_start(out=dst[:], in_=src[:, :]), writes=[B_const])
        S.op("act", lambda e: e.activation(out=aneg[:], in_=aneg[:], func=AF.Exp), reads=[B_const], writes=[B_const])
        S.op("dve", lambda e: e.tensor_scalar_mul(out=aneg[:], in0=aneg[:], scalar1=-1.0), reads=[B_const], writes=[B_const])
        DG = wbuf[:, 0:120 * 128].rearrange("p (c j n) -> p c j n", c=24, j=5)
        for i in range(120):
            S.op("dve" if i % 2 == 0 else "pool",
                 lambda e, i=i: e.tensor_scalar(out=DG[:, i // 5, i % 5, :], in0=identF[:], scalar1=convw_s[:, i:i + 1],
                                                scalar2=None, op0=ALU.mult),
                 reads=[B_const], writes=[B_w])
        WB0 = 120 * 128

        def wv(off, n):
            return wbuf[:, WB0 + off:WB0 + off + n]
        xin = wv(0, 24 * 132).rearrange("p (c t) -> p c t", c=24)
        xcst = wv(3200, 2048)
        bcst = wv(5248, 512)
        tst = wv(5760, 1024).rearrange("p (g t) -> p g t", g=8)
        B_xin, B_xcst, B_bcst, B_tst = Buf(), Buf(), Buf(), Buf()
        nb = 0
        for c in range(NT):
            lo, hi = c * 128 - 2, c * 128 + 130
            s0, s1 = max(lo, 0), min(hi, L)
            if lo < 0 or hi > L:
                S.op("pool", lambda e: e.memset(xin, 0.0), writes=[B_xin])
            S.dma("sp", lambda e, s0=s0, s1=s1, lo=lo: e.dma_start(
                out=xin[:, :, s0 - lo:s1 - lo], in_=PF[0:3072, s0:s1].rearrange("(c p) t -> p c t", p=128)),
                writes=[B_xin])

            def conv_T(ct, bank, slot):
                o = psF[:, bank, slot * 128:(slot + 1) * 128]
                for j in range(5):
                    S.op("pe", lambda e, j=j: e.matmul(o, lhsT=xin[:, ct, j:j + 128], rhs=DG[:, ct, j, :],
                                                       start=(j == 0), stop=False),
                         reads=[B_xin, B_w], writes=[B_ps[bank]])
                S.op("pe", lambda e: e.matmul(o, lhsT=onesF[0:1, 0:128], rhs=convbrow_s[0:1, ct * 128:(ct + 1) * 128],
                                              start=False, stop=True),
                     reads=[B_const], writes=[B_ps[bank]])
            for cg in range(5):
                bank = nb % 4
                nb += 1
                for slot in range(4):
                    conv_T(cg * 4 + slot, bank, slot)
                if cg < 4:
                    S.op("act", lambda e, bank=bank, cg=cg: e.activation(out=xcst[:, cg * 512:(cg + 1) * 512],
                                                                         in_=psF[:, bank, :], func=AF.Silu),
                         reads=[B_ps[bank]], writes=[B_xcst])
                else:
                    S.op("act", lambda e, bank=bank: e.activation(out=bcst, in_=psF[:, bank, :], func=AF.Silu),
                         reads=[B_ps[bank]], writes=[B_bcst])
            S.dma("sp", lambda e, c=c: e.dma_start(out=XC[c * 128:(c + 1) * 128, :], in_=xcst), reads=[B_xcst])
            S.dma("sp", lambda e, c=c: e.dma_start(out=BC[c * 128:(c + 1) * 128, :], in_=bcst), reads=[B_bcst])
            for half in range(2):
                bank = nb % 4
                nb += 1
                for slot in range(4):
                    ct = 16 + half * 4 + slot
                    o = psF[:, bank, slot * 128:(slot + 1) * 128]
                    for j in range(5):
                        S.op("pe", lambda e, j=j, ct=ct, o=o: e.matmul(o, lhsT=DG[:, ct, j, :], rhs=xin[:, ct, j:j + 128],
                                                                        start=(j == 0), stop=(j == 4)),
                             reads=[B_xin, B_w], writes=[B_ps[bank]])
                    S.op("act", lambda e, ct=ct, o=o: e.activation(out=tst[:, ct - 16, :], in_=o, func=AF.Silu,
                                                                   bias=convbT_s[:, ct:ct + 1]),
                         reads=[B_ps[bank], B_const], writes=[B_tst])
            S.dma("sp", lambda e, c=c: e.dma_start(out=BCT[:, c * 128:(c + 1) * 128].rearrange("(g p) t -> p g t", p=128),
                                                   in_=tst[:, 0:4, :]), reads=[B_tst])
            S.dma("sp", lambda e, c=c: e.dma_start(out=CCT[:, c * 128:(c + 1) * 128].rearrange("(g p) t -> p g t", p=128),
                                                   in_=tst[:, 4:8, :]), reads=[B_tst])
        S.barrier()

        def fv(off, n):
            return fbuf[:, off:off + n]
        LA = fv(0, 4096).rearrange("p (h s) -> p h s", h=32)
        yacc = fv(4096, 2048)
        Hs = fv(6144, 2048)
        yf = fv(8192, 2048)
        yz = fv(10240, 2048)
        Eg = fv(12288, 1024)
        tmpf = fv(13312, 512)
        MTm = fv(13824, 512).rearrange("p (g l) -> p g l", g=4)
        sm = fv(14336, 256)
        sm2 = fv(14592, 96)
        sqw = fv(14700, 1600)
        xc = wv(0, 2048)
        bc = wv(2048, 512)
        bct = wv(2560, 512).rearrange("p (g t) -> p g t", g=4)
        cct = wv(3072, 512).rearrange("p (g t) -> p g t", g=4)
        xdt = wv(3584, 2048).rearrange("p (h d) -> p h d", h=32)
        MD = wv(5632, 1024).rearrange("p (h l) -> p h l", h=8)
        Hb = wv(6656, 2048)
        xdd = wv(8704, 512).rearrange("p (h d) -> p h d", h=8)
        zs = wv(9216, 2048)
        yn = wv(11264, 2048)
        ynT = wv(13312, 2048).rearrange("p (c t) -> p c t", c=16)
        dtr = sb("dtr", [128, 64], F32)
        Bn = {n: Buf(n) for n in "LA yacc Hs yf yz Eg tmpf MTm sm sm2 xc bc bct cct xdt MD Hb xdd zs yn ynT dtr".split()}
        def sweep(d):
            S.op("pool", lambda e: e.memset(Hs, 0.0), writes=[Bn["Hs"]])
            S.op("pool", lambda e: e.memset(Hb, 0.0), writes=[Bn["Hb"]])
            T1 = triLs if d == 0 else triUs
            T2 = triU if d == 0 else triLo
            Tds = triLs if d == 0 else triUs
            order = range(NT) if d == 0 else range(NT - 1, -1, -1)
            for c in order:
                r0, r1 = c * 128, (c + 1) * 128
                S.dma("sp", lambda e, r0=r0, r1=r1: e.dma_start(out=xc, in_=XC[r0:r1, :]), writes=[Bn["xc"]])
                S.dma("sp", lambda e, r0=r0, r1=r1: e.dma_start(out=bc, in_=BC[r0:r1, :]), writes=[Bn["bc"]])
                S.dma("sp", lambda e, r0=r0, r1=r1: e.dma_start(out=bct, in_=BCT[:, r0:r1].rearrange("(g p) t -> p g t", p=128)),
                      writes=[Bn["bct"]])
                S.dma("sp", lambda e, r0=r0, r1=r1: e.dma_start(out=cct, in_=CCT[:, r0:r1].rearrange("(g p) t -> p g t", p=128)),
                      writes=[Bn["cct"]])
                S.dma("sp", lambda e, r0=r0, r1=r1: e.dma_start(out=dtr[:], in_=PTdt[r0:r1, :]), writes=[Bn["dtr"]])
                if d == 1:
                    S.dma("sp", lambda e, r0=r0, r1=r1: e.dma_start(out=yf, in_=YF[r0:r1, :]), writes=[Bn["yf"]])
                    S.dma("sp", lambda e, r0=r0, r1=r1: e.dma_start(out=zs, in_=PTz[r0:r1, :]), writes=[Bn["zs"]])
                o32 = slice(d * 32, d * 32 + 32)
                smB = [Bn["sm"]]
                S.op("dve", lambda e: e.tensor_tensor(out=sm[:, 0:32], in0=dtr[:, o32], in1=dtb_s[:, o32], op=ALU.add),
                     reads=[Bn["dtr"], B_const], writes=smB)
                S.op("act", lambda e: e.activation(out=sm[:, 32:64], in_=sm[:, 0:32], func=AF.Abs),
                     reads=smB, writes=smB)
                S.op("act", lambda e: e.activation(out=sm[:, 64:96], in_=sm[:, 32:64], func=AF.Exp, scale=-1.0),
                     reads=smB, writes=smB)
                S.op("act", lambda e: e.activation(out=sm[:, 96:128], in_=sm[:, 64:96], func=AF.Ln, bias=1.0),
                     reads=smB, writes=smB)
                S.op("dve", lambda e: e.tensor_scalar_max(out=sm[:, 128:160], in0=sm[:, 0:32], scalar1=0.0),
                     reads=smB, writes=smB)
                S.op("dve", lambda e: e.tensor_tensor(out=sm[:, 160:192], in0=sm[:, 128:160], in1=sm[:, 96:128], op=ALU.add),
                     reads=smB, writes=smB)
                S.op("dve", lambda e: e.tensor_tensor(out=sm[:, 192:224], in0=sm[:, 160:192], in1=aneg[:, o32], op=ALU.mult),
                     reads=smB + [B_const], writes=smB)
                dtv = sm[:, 160:192]
                adt = sm[:, 192:224]
                S.op("dve", lambda e: e.tensor_tensor(out=xdt, in0=xc.rearrange("p (h d) -> p h d", h=32),
                                                      in1=dtv.unsqueeze(2).to_broadcast([128, 32, 64]), op=ALU.mult),
                     reads=[Bn["xc"]] + smB, writes=[Bn["xdt"]])
                for g in range(4):
                    S.op("pe", lambda e, g=g: e.matmul(psF[:, 0, g * 128:(g + 1) * 128], lhsT=bct[:, g, :], rhs=cct[:, g, :],
                                                       start=True, stop=True),
                         reads=[Bn["bct"], Bn["cct"]], writes=[B_ps[0]])
                msk = triU if d == 0 else triLo
                S.op("dve", lambda e: e.tensor_tensor(out=MTm, in0=psF[:, 0, :].rearrange("p (g l) -> p g l", g=4),
                                                      in1=msk[:].unsqueeze(1).to_broadcast([128, 4, 128]), op=ALU.mult),
                     reads=[B_ps[0], B_const], writes=[Bn["MTm"]])
                S.op("pool", lambda e: e.tensor_tensor(out=LA, in0=T1[:].unsqueeze(1).to_broadcast([128, 32, 128]),
                                                       in1=adt.unsqueeze(2).to_broadcast([128, 32, 128]), op=ALU.mult),
                     reads=smB + [B_const], writes=[Bn["LA"]])
                S.op("pe", lambda e: e.matmul(psF[:, 0, 0:32], lhsT=T2[:], rhs=adt, start=True, stop=True),
                     reads=smB + [B_const, Bn["MTm"]], writes=[B_ps[0]])
                S.op("pe", lambda e: e.matmul(psF[:, 0, 32:64], lhsT=Tds[:], rhs=adt, start=True, stop=True),
                     reads=smB + [B_const], writes=[B_ps[0]])
                S.op("pe", lambda e: e.matmul(psF[:, 0, 64:96], lhsT=onesF[:], rhs=adt, start=True, stop=True),
                     reads=smB + [B_const], writes=[B_ps[0]])
                S.op("act", lambda e: e.activation(out=sm2, in_=psF[:, 0, 0:96], func=AF.Exp),
                     reads=[B_ps[0]], writes=[Bn["sm2"]])
                EA, dsv, cdv = sm2[:, 0:32], sm2[:, 32:64], sm2[:, 64:96]
                for g in range(4):
                    for hh in range(8):
                        h = g * 8 + hh
                        bk = 1 + hh // 4
                        S.op("pe", lambda e, h=h, hh=hh, bk=bk: e.matmul(psF[:, bk, (hh % 4) * 128:(hh % 4 + 1) * 128],
                                                                          lhsT=LA[:, h, :], rhs=T2[:], start=True, stop=True),
                             reads=[Bn["LA"], B_const], writes=[B_ps[bk]])
                    S.op("act", lambda e: e.activation(out=Eg, in_=psF[:, 1:3, :].rearrange("p b n -> p (b n)"), func=AF.Exp),
                         reads=[B_ps[1], B_ps[2]], writes=[Bn["Eg"]])
                    S.op("dve", lambda e, g=g: e.tensor_tensor(out=MD, in0=Eg.rearrange("p (h l) -> p h l", h=8),
                                                               in1=MTm[:, g, :].unsqueeze(1).to_broadcast([128, 8, 128]),
                                                               op=ALU.mult),
                         reads=[Bn["Eg"], Bn["MTm"]], writes=[Bn["MD"]])
                    if dbg and d == 0 and c == 0 and g == 0:
                        S.dma("sp", lambda e: e.dma_start(out=DBGF[:, 0:256], in_=sm), reads=[Bn["sm"]])
                        S.dma("sp", lambda e: e.dma_start(out=DBGF[:, 256:352], in_=sm2), reads=[Bn["sm2"]])
                        S.dma("sp", lambda e: e.dma_start(out=DBGF[:, 512:1024], in_=MTm.rearrange("p g l -> p (g l)")), reads=[Bn["MTm"]])
                        S.dma("sp", lambda e: e.dma_start(out=DBGF[:, 1024:2048], in_=Eg), reads=[Bn["Eg"]])
                        S.dma("sp", lambda e: e.dma_start(out=DBGB[:, 0:2048], in_=xdt.rearrange("p h d -> p (h d)")), reads=[Bn["xdt"]])
                        S.dma("sp", lambda e: e.dma_start(out=DBGB[:, 2048:3072], in_=MD.rearrange("p h l -> p (h l)")), reads=[Bn["MD"]])
                    for hh in range(8):
                        h = g * 8 + hh
                        S.op("pe", lambda e, h=h, hh=hh: e.matmul(psF[:, 3, hh * 64:(hh + 1) * 64], lhsT=MD[:, hh, :],
                                                                  rhs=xdt[:, h, :], start=True, stop=True),
                             reads=[Bn["MD"], Bn["xdt"]], writes=[B_ps[3]])
                    S.op("pe", lambda e, g=g: e.matmul(psF[:, 4, :], lhsT=cct[:, g, :], rhs=Hb[:, g * 512:(g + 1) * 512],
                                                       start=True, stop=True),
                         reads=[Bn["cct"], Bn["Hb"]], writes=[B_ps[4]])
                    S.op("dve", lambda e, g=g: e.tensor_tensor(out=tmpf.rearrange("p (h d) -> p h d", h=8),
                                                               in0=psF[:, 4, :].rearrange("p (h d) -> p h d", h=8),
                                                               in1=EA[:, g * 8:(g + 1) * 8].unsqueeze(2).to_broadcast([128, 8, 64]),
                                                               op=ALU.mult),
                         reads=[B_ps[4], Bn["sm2"]], writes=[Bn["tmpf"]])
                    S.op("dve", lambda e, g=g: e.tensor_tensor(out=yacc[:, g * 512:(g + 1) * 512], in0=psF[:, 3, :], in1=tmpf,
                                                               op=ALU.add),
                         reads=[B_ps[3], Bn["tmpf"]], writes=[Bn["yacc"]])
                    S.op("pool", lambda e, g=g: e.tensor_tensor(out=xdd, in0=xdt[:, g * 8:(g + 1) * 8, :],
                                                                in1=dsv[:, g * 8:(g + 1) * 8].unsqueeze(2).to_broadcast([128, 8, 64]),
                                                                op=ALU.mult),
                         reads=[Bn["xdt"], Bn["sm2"]], writes=[Bn["xdd"]])
                    S.op("pe", lambda e, g=g: e.matmul(psF[:, 5, :], lhsT=bc[:, g * 128:(g + 1) * 128],
                                                       rhs=xdd.rearrange("p h d -> p (h d)"), start=True, stop=True),
                         reads=[Bn["bc"], Bn["xdd"]], writes=[B_ps[5]])
                    Hg = Hs[:, g * 512:(g + 1) * 512]
                    S.op("dve", lambda e, g=g, Hg=Hg: e.tensor_tensor(out=Hg.rearrange("p (h d) -> p h d", h=8),
                                                                      in0=Hg.rearrange("p (h d) -> p h d", h=8),
                                                                      in1=cdv[:, g * 8:(g + 1) * 8].unsqueeze(2).to_broadcast([128, 8, 64]),
                                                                      op=ALU.mult),
                         reads=[Bn["sm2"], Bn["Hs"]], writes=[Bn["Hs"]])
                    S.op("dve", lambda e, Hg=Hg: e.tensor_tensor(out=Hg, in0=psF[:, 5, :], in1=Hg, op=ALU.add),
                         reads=[B_ps[5], Bn["Hs"]], writes=[Bn["Hs"]])
                    S.op("act", lambda e, g=g, Hg=Hg: e.copy(out=Hb[:, g * 512:(g + 1) * 512], in_=Hg),
                         reads=[Bn["Hs"]], writes=[Bn["Hb"]])
                if d == 0:
                    S.dma("sp", lambda e, r0=r0, r1=r1: e.dma_start(out=YF[r0:r1, :], in_=yacc), reads=[Bn["yacc"]])
                else:
                    S.op("pool", lambda e: e.tensor_tensor(out=yacc, in0=yacc, in1=yf, op=ALU.add),
                         reads=[Bn["yf"], Bn["yacc"]], writes=[Bn["yacc"]])
                    S.op("dve", lambda e: e.tensor_tensor(out=yf, in0=xc, in1=dskip_s[:], op=ALU.mult),
                         reads=[Bn["xc"], B_const, Bn["yacc"]], writes=[Bn["yf"]])
                    S.op("pool", lambda e: e.tensor_tensor(out=yacc, in0=yacc, in1=yf, op=ALU.add),
                         reads=[Bn["yf"], Bn["yacc"]], writes=[Bn["yacc"]])
                    S.op("dve", lambda e: e.tensor_tensor(out=yz, in0=yacc, in1=zs, op=ALU.mult),
                         reads=[Bn["yacc"], Bn["zs"]], writes=[Bn["yz"]])
                    S.op("act", lambda e: e.activation(out=yf, in_=yz, func=AF.Square, scale=float(1.0 / np.sqrt(2048.0)),
                                                       accum_out=ss[:, 0:1]),
                         reads=[Bn["yz"]], writes=[Bn["yf"], B_ss])
                    S.op("dve", lambda e: e.tensor_scalar_add(out=ss[:, 0:1], in0=ss[:, 0:1], scalar1=1e-6),
                         reads=[B_ss], writes=[B_ss])
                    S.op("act", lambda e: e.activation(out=ss[:, 0:1], in_=ss[:, 0:1], func=AF.Ln), reads=[B_ss], writes=[B_ss])
                    S.op("act", lambda e: e.activation(out=ss[:, 0:1], in_=ss[:, 0:1], func=AF.Exp, scale=-0.5),
                         reads=[B_ss], writes=[B_ss])
                    S.op("dve", lambda e: e.scalar_tensor_tensor(out=yn, in0=yz, scalar=ss[:, 0:1], in1=sng_s[:],
                                                                 op0=ALU.mult, op1=ALU.mult),
                         reads=[Bn["yz"], B_ss, B_const], writes=[Bn["yn"]])
                    S.dma("sp", lambda e, r0=r0, r1=r1: e.dma_start(out=YTOK[r0:r1, 0:2048], in_=yn), reads=[Bn["yn"]])
        sweep(0)
        sweep(1)
        S.barrier()

        qkg = sb("qkg", [128, 2], F32)
        S.dma("sp", lambda e: e.dma_start(out=qkg[:, 0:1], in_=qg_d[:, :]), writes=[B_const])
        S.dma("sp", lambda e: e.dma_start(out=qkg[:, 1:2], in_=kg_d[:, :]), writes=[B_const])
        BD = sb("BD", [128, 128], BF16)
        S.op("pool", lambda e: e.memset(BD[:], 0.0), writes=[B_const])
        S.op("pool", lambda e: e.memset(BD[0:64, 0:64], 1.0), reads=[B_const], writes=[B_const])
        S.op("pool", lambda e: e.memset(BD[64:128, 64:128], 1.0), reads=[B_const], writes=[B_const])
        qraw = wbuf[:, 0:512]
        qsq = wbuf[:, 512:1024]
        qno = wbuf[:, 1024:1536]
        rstd_q = fbuf[:, 0:512]
        B_qraw, B_qsq, B_qno, B_rq = Buf(), Buf(), Buf(), Buf()
        for b in range(L // 512):
            for ct in range(16):
                c0, c1 = b * 512, (b + 1) * 512
                S.dma("sp", lambda e, ct=ct, c0=c0, c1=c1: e.dma_start(out=qraw, in_=PF[(24 + ct) * 128:(25 + ct) * 128, c0:c1]),
                      writes=[B_qraw])
                S.op("dve", lambda e: e.tensor_tensor(out=qsq, in0=qraw, in1=qraw, op=ALU.mult), reads=[B_qraw], writes=[B_qsq])
                bank = ct % 2
                S.op("pe", lambda e, bank=bank: e.matmul(psF[:, bank, :], lhsT=BD[:], rhs=qsq, start=True, stop=True),
                     reads=[B_qsq, B_const], writes=[B_ps[bank]])
                S.op("act", lambda e, bank=bank: e.activation(out=rstd_q, in_=psF[:, bank, :], func=AF.Ln, scale=1.0 / 64, bias=1e-6),
                     reads=[B_ps[bank]], writes=[B_rq])
                S.op("act", lambda e: e.activation(out=rstd_q, in_=rstd_q, func=AF.Exp, scale=-0.5), reads=[B_rq], writes=[B_rq])
                gi = 0 if ct < 8 else 1
                S.op("dve", lambda e, gi=gi: e.scalar_tensor_tensor(out=qno, in0=qraw, scalar=qkg[:, gi:gi + 1], in1=rstd_q,
                                                                   op0=ALU.mult, op1=ALU.mult),
                     reads=[B_qraw, B_rq, B_const], writes=[B_qno])
                dstT = QN if ct < 8 else KN
                cc = ct % 8
                S.dma("sp", lambda e, dstT=dstT, cc=cc, c0=c0, c1=c1: e.dma_start(out=dstT[cc * 128:(cc + 1) * 128, c0:c1], in_=qno),
                      reads=[B_qno])
        S.barrier()
        NE = 5 * 16 * 128
        Egen = wbuf[:, 0:NE].rearrange("p (k h q) -> p k h q", k=5, h=16)
        Espec = wbuf[:, NE:2 * NE].rearrange("p (k h q) -> p k h q", k=5, h=16)
        o0 = 2 * NE
        qn = wbuf[:, o0:o0 + 1024].rearrange("p (c t) -> p c t", c=8)
        kn = wbuf[:, o0 + 1024:o0 + 1024 + 5120].rearrange("p (c t) -> p c t", c=8)
        o1 = o0 + 6144
        vaug = wbuf[:, o1:o1 + 5200].rearrange("p (k h d) -> p k h d", k=5, h=16)
        o2 = o1 + 5200
        pbuf = [wbuf[:, o2 + i * 512:o2 + (i + 1) * 512] for i in range(2)]
        pm = [[wbuf[:, o2 + 1024 + (i * 5 + k) * 512:o2 + 1024 + (i * 5 + k + 1) * 512] for k in range(5)] for i in range(2)]
        yna = wbuf[:, o2 + 6144:o2 + 7168]
        ynaT = wbuf[:, o2 + 7168:o2 + 8192].rearrange("p (c t) -> p c t", c=8)
        tstage = fbuf[:, 0:NE]
        den = fbuf[:, NE:NE + 16]
        B_E, B_Es, B_qn, B_kn, B_va, B_yna, B_ynaT, B_ts, B_den = (Buf() for _ in range(9))
        B_pb = [Buf(), Buf()]
        B_pm = [Buf(), Buf()]

        def load_table(pat, dst, Bdst):
            S.dma("sp", lambda e: e.dma_start(out=tstage, in_=bt_d[pat]), writes=[B_ts])
            S.op("act", lambda e: e.activation(out=dst.rearrange("p k h q -> p (k h q)"), in_=tstage, func=AF.Exp),
                 reads=[B_ts], writes=[Bdst])
        load_table(2, Egen, B_E)
        S.op("pool", lambda e: e.memset(vaug.rearrange("p k h d -> p (k h d)"), 1.0), writes=[B_va])
        npp = 0
        for t in range(NT):
            kt0 = min(max(t - 2, 0), NT - 5)
            rel = t - kt0
            if rel == 2:
                Et, BEt = Egen, B_E
            else:
                load_table(rel, Espec, B_Es)
                Et, BEt = Espec, B_Es
            r0, r1 = t * 128, (t + 1) * 128
            k0, k1 = kt0 * 128, (kt0 + 5) * 128
            S.dma("sp", lambda e, r0=r0, r1=r1: e.dma_start(out=qn, in_=QN[:, r0:r1].rearrange("(c p) t -> p c t", p=128)),
                  writes=[B_qn])
            S.dma("sp", lambda e, k0=k0, k1=k1: e.dma_start(out=kn, in_=KN[:, k0:k1].rearrange("(c p) t -> p c t", p=128)),
                  writes=[B_kn])
            for kb in range(5):
                S.dma("sp", lambda e, kb=kb, k0=k0: e.dma_start(
                    out=vaug[:, kb, :, 0:64], in_=PTv[k0 + kb * 128:k0 + (kb + 1) * 128, :].rearrange("p (h d) -> p h d", h=16)),
                    writes=[B_va])
            for hq in range(4):
                jq = hq % 2
                for kb in range(5):
                    j = npp % 2
                    npp += 1
                    bank = j
                    for hh in range(4):
                        h = hq * 4 + hh
                        ct, po = h // 2, (h % 2) * 64
                        S.op("pe", lambda e, ct=ct, po=po, hh=hh, kb=kb, bank=bank: e.matmul(
                            psF[:, bank, hh * 128:(hh + 1) * 128], lhsT=kn[po:po + 64, ct, kb * 128:(kb + 1) * 128],
                            rhs=qn[po:po + 64, ct, :], start=True, stop=True),
                            reads=[B_kn, B_qn], writes=[B_ps[bank]])
                    S.op("act", lambda e, bank=bank, j=j: e.activation(out=pbuf[j], in_=psF[:, bank, :], func=AF.Exp, scale=0.125),
                         reads=[B_ps[bank]], writes=[B_pb[j]])
                    S.op("pool" if j == 0 else "dve", lambda e, j=j, kb=kb, hq=hq, Et=Et, jq=jq: e.tensor_tensor(
                        out=pm[jq][kb], in0=pbuf[j], in1=Et[:, kb, hq * 4:(hq + 1) * 4, :].rearrange("p h q -> p (h q)"), op=ALU.mult),
                        reads=[B_pb[j], BEt], writes=[B_pm[jq]])
                for hh in range(4):
                    h = hq * 4 + hh
                    ob, oc = 2 + h // 7, (h % 7) * 65
                    for kb in range(5):
                        S.op("pe", lambda e, h=h, hh=hh, kb=kb, jq=jq, ob=ob, oc=oc: e.matmul(
                            psF[:, ob, oc:oc + 65], lhsT=pm[jq][kb][:, hh * 128:(hh + 1) * 128], rhs=vaug[:, kb, h, :],
                            start=(kb == 0), stop=(kb == 4)),
                            reads=[B_pm[jq], B_va], writes=[B_ps[ob]])
            for ob, h0, nh in ((2, 0, 7), (3, 7, 7), (4, 14, 2)):
                pv = psF[:, ob, 0:nh * 65].rearrange("p (h d) -> p h d", h=nh)
                S.op("dve", lambda e, pv=pv, h0=h0, nh=nh: e.tensor_copy(out=den[:, h0:h0 + nh].unsqueeze(2), in_=pv[:, :, 64:65]),
                     reads=[B_ps[ob]], writes=[B_den])
            S.op("dve", lambda e: e.reciprocal(out=den, in_=den), reads=[B_den], writes=[B_den])
            for ob, h0, nh in ((2, 0, 7), (3, 7, 7), (4, 14, 2)):
                pv = psF[:, ob, 0:nh * 65].rearrange("p (h d) -> p h d", h=nh)
                S.op("dve", lambda e, pv=pv, h0=h0, nh=nh: e.tensor_tensor(
                    out=yna[:, h0 * 64:(h0 + nh) * 64].rearrange("p (h d) -> p h d", h=nh), in0=pv[:, :, 0:64],
                    in1=den[:, h0:h0 + nh].unsqueeze(2).to_broadcast([128, nh, 64]), op=ALU.mult),
                    reads=[B_ps[ob], B_den], writes=[B_yna])
            S.dma("sp", lambda e, r0=r0, r1=r1: e.dma_start(out=YTOK[r0:r1, 2048:3072], in_=yna), reads=[B_yna])
        S.barrier()

        WO = wbuf[:, 0:24 * 1024].rearrange("p (c n) -> p c n", c=24)
        for ct in range(24):
            j = ct % 2
            S.dma("sp", lambda e, ct=ct, j=j: e.dma_start(out=wst[j], in_=wo_d[:, ct, :]), writes=[B_wst[j]])
            S.op("dve" if j == 0 else "pool", lambda e, ct=ct, j=j: e.tensor_copy(out=WO[:, ct, :], in_=wst[j]),
                 reads=[B_wst[j]], writes=[B_w])
        _o = [0]

        def sv(n, shape=None, dt=None):
            v = scr[:, _o[0]:_o[0] + n]
            _o[0] += n
            if dt is not None:
                v = v.bitcast(dt)
            return v
        gffn_s = sv(1024)
        skT_s = sv(256).rearrange("p (k n) -> p k n", k=2)
        iota_i = sv(256, dt=I32)
        iota_f = sv(256)
        S.dma("sp", lambda e: e.dma_start(out=gffn_s, in_=gffn_d[:, :]), writes=[B_const])
        S.dma("sp", lambda e: e.dma_start(out=skT_s, in_=skT_d[:, :, :]), writes=[B_const])
        S.op("pool", lambda e: e.iota(iota_i, pattern=[[1, 256]], base=0, channel_multiplier=0), writes=[B_const])
        S.op("pool", lambda e: e.tensor_copy(out=iota_f, in_=iota_i), reads=[B_const], writes=[B_const])
        yT_s = wbuf[:, 24576:24576 + 3072].rearrange("p (c t) -> p c t", c=24)
        g12 = wbuf[:, 27648:27648 + 2048]
        yrow = wbuf[:, 29696:29696 + 3072]
        gbb = [wbuf[:, 32768 + i * 1024:32768 + (i + 1) * 1024] for i in range(4)]
        cst = [wbuf[:, 36864 + i * 1024:36864 + (i + 1) * 1024] for i in range(2)]
        B_cst = [Buf(), Buf()]
        B_yrow = Buf()
        rowidx_s = sv(NSH, dt=I32)
        S.dma("sp", lambda e: e.dma_start(out=rowidx_s, in_=rowidx_d[:, :]), writes=[B_const])
        nce = 0
        for src, dstE in ((eu_d, EUb), (ev_d, EVb)):
            for rt in range(NEXP // 128):
                j = nce % 2
                nce += 1
                S.dma("sp", lambda e, src=src, rt=rt, j=j: e.dma_start(out=wst[j], in_=src[rt * 128:(rt + 1) * 128, :]), writes=[B_wst[j]])
                if nce % 3 == 0:
                    S.op("act", lambda e, j=j: e.copy(out=cst[j], in_=wst[j]), reads=[B_wst[j]], writes=[B_cst[j]])
                elif nce % 3 == 1:
                    S.op("dve", lambda e, j=j: e.tensor_copy(out=cst[j], in_=wst[j]), reads=[B_wst[j]], writes=[B_cst[j]])
                else:
                    S.op("pool", lambda e, j=j: e.tensor_copy(out=cst[j], in_=wst[j]), reads=[B_wst[j]], writes=[B_cst[j]])
                S.dma("sp", lambda e, dstE=dstE, rt=rt, j=j: e.dma_start(out=dstE[rt * 128:(rt + 1) * 128, :], in_=cst[j]), reads=[B_cst[j]])
        S.barrier()
        xin_f = fbuf[:, 0:1024]
        xm = fbuf[:, 1024:2048]
        tmpo = fbuf[:, 2048:3072]
        tt = fbuf[:, 3072:4096]
        tT_s = fbuf[:, 4096:5120]
        qT_s = fbuf[:, 5120:7168]
        s_s = fbuf[:, 7168:9216].rearrange("p (k n) -> p k n", k=16)
        gbuf = [fbuf[:, 9216 + i * 1024:9216 + (i + 1) * 1024] for i in range(4)]
        junk = fbuf[:, 13312:14336]
        wpst = [fbuf[:, 14336 + i * 1024:14336 + (i + 1) * 1024] for i in range(2)]
        top = sv(256).rearrange("p (a b) -> p a b", a=16)
        idxu = sv(256, dt=U32).rearrange("p (a b) -> p a b", a=16)
        idxf = sv(256).rearrange("p (a b) -> p a b", a=16)
        work = sv(128)
        cand = sv(256).rearrange("p (a b) -> p a b", a=16)
        cidx = sv(256).rearrange("p (a b) -> p a b", a=16)
        work2 = sv(256)
        best = sv(16)
        posu = sv(16, dt=U32)
        posf = sv(16)
        ef = sv(128)
        ei = sv(128, dt=I32)
        gts = sv(128)
        actv = sv(128)
        wgt = sv(128)
        smz = sv(4)
        Bq = {n: Buf(n) for n in "yT g12 xin xm tmpo tt tT qT s junk top idxu idxf work cand cidx work2 best posu posf ef ei gts actv wgt smz".split()}
        B_gb = [Buf() for _ in range(4)]
        B_wp = [Buf(), Buf()]
        tt2 = fbuf[:, 9216:10240]
        xm2 = fbuf[:, 10240:11264]
        junk2 = fbuf[:, 11264:12288]
        gts2 = sv(128)
        ei2 = sv(128, dt=I32)
        gring = gbb
        B_gr = [Buf() for _ in gring]
        NG = len(gring)
        for _n in ("tt", "xm", "ei", "gts"):
            Bq[_n + "0"] = Bq[_n]
            Bq[_n + "1"] = Buf(_n + "1")
        Bq["junk2"] = Buf("junk2")
        PARV = {"tt": (tt, tt2), "xm": (xm, xm2), "ei": (ei, ei2), "gts": (gts, gts2)}
        ngbc = [0]
        dgs = [sv(64, dt=BF16) for _ in range(4)]
        B_dg = [Buf() for _ in range(4)]

        def stage1(i):
            par = i % 2
            tt_p, xm_p, ei_p, gts_p = PARV["tt"][par], PARV["xm"][par], PARV["ei"][par], PARV["gts"][par]
            r0, r1 = i * 128, (i + 1) * 128
            ridx = rowidx_s[:, i:i + 1]
            S.dma("pool", lambda e, ridx=ridx: e.indirect_dma_start(out=yrow, out_offset=None, in_=YTOK[:, :],
                                                                    in_offset=bass.IndirectOffsetOnAxis(ap=ridx, axis=0)),
                  reads=[B_const], writes=[B_yrow])
            S.dma("pool", lambda e, ridx=ridx: e.indirect_dma_start(out=g12, out_offset=None, in_=PTg[:, :],
                                                                    in_offset=bass.IndirectOffsetOnAxis(ap=ridx, axis=0)),
                  reads=[B_const], writes=[Bq["g12"]])
            S.dma("pool", lambda e, ridx=ridx: e.indirect_dma_start(out=xin_f, out_offset=None, in_=x_d[:, :],
                                                                    in_offset=bass.IndirectOffsetOnAxis(ap=ridx, axis=0)),
                  reads=[B_const], writes=[Bq["xin"]])
            for rnd, (c0, nct) in enumerate(((0, 16), (16, 8))):
                for cc in range(nct):
                    ct = c0 + cc
                    S.op("pe", lambda e, ct=ct, cc=cc: e.transpose(out=psB[:, cc // 8, (cc % 8) * 128:(cc % 8 + 1) * 128],
                                                                  in_=yrow[:, ct * 128:(ct + 1) * 128], identity=identB[:]),
                         reads=[B_yrow, B_const], writes=[B_psb[cc // 8]])
                S.op("act", lambda e, c0=c0: e.copy(out=yT_s[:, c0:c0 + 8, :].rearrange("p c t -> p (c t)"), in_=psB[:, 0, :]),
                     reads=[B_psb[0]], writes=[Bq["yT"]])
                if nct == 16:
                    S.op("dve", lambda e: e.tensor_copy(out=yT_s[:, 8:16, :].rearrange("p c t -> p (c t)"), in_=psB[:, 1, :]),
                         reads=[B_psb[1]], writes=[Bq["yT"]])
            for half in range(2):
                yield
                hs = slice(half * 512, (half + 1) * 512)
                hs2 = slice(1024 + half * 512, 1024 + (half + 1) * 512)
                for ct in range(16):
                    S.op("pe", lambda e, ct=ct, hs=hs: e.matmul(psF[:, 0, :], lhsT=yT_s[:, ct, :], rhs=WO[:, ct, hs],
                                                                start=(ct == 0), stop=(ct == 15)),
                         reads=[Bq["yT"], B_w], writes=[B_ps[0]])
                for ct in range(16, 24):
                    S.op("pe", lambda e, ct=ct, hs=hs: e.matmul(psF[:, 1, :], lhsT=yT_s[:, ct, :], rhs=WO[:, ct, hs],
                                                                start=(ct == 16), stop=(ct == 23)),
                         reads=[Bq["yT"], B_w], writes=[B_ps[1]])
                S.op("dve", lambda e, hs=hs: e.tensor_tensor(out=tmpo[:, hs], in0=psF[:, 0, :], in1=g12[:, hs], op=ALU.mult),
                     reads=[B_ps[0], Bq["g12"]], writes=[Bq["tmpo"]])
                S.op("dve", lambda e, hs=hs, hs2=hs2: e.tensor_tensor(out=junk[:, hs], in0=psF[:, 1, :], in1=g12[:, hs2], op=ALU.mult),
                     reads=[B_ps[1], Bq["g12"]], writes=[Bq["junk"]])
                S.op("pool", lambda e, hs=hs: e.tensor_tensor(out=tmpo[:, hs], in0=tmpo[:, hs], in1=junk[:, hs], op=ALU.add),
                     reads=[Bq["junk"], Bq["tmpo"]], writes=[Bq["tmpo"]])
                S.op("pool", lambda e, hs=hs: e.tensor_tensor(out=xm_p[:, hs], in0=tmpo[:, hs], in1=xin_f[:, hs], op=ALU.add),
                     reads=[Bq["tmpo"], Bq["xin"]], writes=[Bq["xm%d" % par]])
            if dbg:
                S.dma("sp", lambda e, r0=r0, r1=r1: e.dma_start(out=XM[r0:r1, :], in_=xm_p), reads=[Bq["xm%d" % par]])
            S.op("act", lambda e: e.activation(out=junk, in_=xm_p, func=AF.Square, scale=1.0 / 32, accum_out=ss[:, 0:1]),
                 reads=[Bq["xm%d" % par]], writes=[Bq["junk"], B_ss])
            S.op("dve", lambda e: e.tensor_scalar_add(out=ss[:, 0:1], in0=ss[:, 0:1], scalar1=1e-6), reads=[B_ss], writes=[B_ss])
            S.op("act", lambda e: e.activation(out=ss[:, 0:1], in_=ss[:, 0:1], func=AF.Ln), reads=[B_ss], writes=[B_ss])
            S.op("act", lambda e: e.activation(out=ss[:, 0:1], in_=ss[:, 0:1], func=AF.Exp, scale=-0.5), reads=[B_ss], writes=[B_ss])
            S.op("dve", lambda e: e.scalar_tensor_tensor(out=tt_p, in0=xm_p, scalar=ss[:, 0:1], in1=gffn_s, op0=ALU.mult, op1=ALU.mult),
                 reads=[Bq["xm%d" % par], B_ss, B_const], writes=[Bq["tt%d" % par]])
            for k in range(8):
                S.op("pe", lambda e, k=k: e.transpose(out=psF[:, 2 + k // 4, (k % 4) * 128:(k % 4 + 1) * 128],
                                                      in_=tt_p[:, k * 128:(k + 1) * 128], identity=identF[:]),
                     reads=[Bq["tt%d" % par], B_const], writes=[B_ps[2 + k // 4]])
            S.op("act", lambda e: e.copy(out=tT_s[:, 0:512], in_=psF[:, 2, :]), reads=[B_ps[2]], writes=[Bq["tT"]])
            S.op("dve", lambda e: e.tensor_copy(out=tT_s[:, 512:1024], in_=psF[:, 3, :]), reads=[B_ps[3]], writes=[Bq["tT"]])
            for cq in range(16):
                yield
                j = cq % 2
                bank = 2 + (cq // 4) % 2
                S.dma("sp", lambda e, cq=cq, j=j: e.dma_start(out=wpst[j], in_=wpq_d[cq].rearrange("p k n -> p (k n)")), writes=[B_wp[j]])
                for k in range(8):
                    S.op("pe", lambda e, cq=cq, k=k, j=j, bank=bank: e.matmul(
                        psF[:, bank, (cq % 4) * 128:(cq % 4 + 1) * 128], lhsT=wpst[j][:, k * 128:(k + 1) * 128],
                        rhs=tT_s[:, k * 128:(k + 1) * 128], start=(k == 0), stop=(k == 7)),
                        reads=[B_wp[j], Bq["tT"]], writes=[B_ps[bank]])
                if cq % 4 == 3:
                    q0 = (cq // 4) * 512
                    S.op("act" if (cq // 4) % 2 == 0 else "dve",
                         (lambda e, bank=bank, q0=q0: e.copy(out=qT_s[:, q0:q0 + 512], in_=psF[:, bank, :])) if (cq // 4) % 2 == 0 else
                         (lambda e, bank=bank, q0=q0: e.tensor_copy(out=qT_s[:, q0:q0 + 512], in_=psF[:, bank, :])),
                         reads=[B_ps[bank]], writes=[Bq["qT"]])
            for hk in range(16):
                yield
                bank = 2 + (hk // 4) % 2
                S.op("pe", lambda e, hk=hk, bank=bank: e.matmul(psF[:, bank, (hk % 4) * 128:(hk % 4 + 1) * 128],
                                                                lhsT=qT_s[:, hk * 128:(hk + 1) * 128], rhs=skT_s[:, hk % 2, :],
                                                                start=True, stop=True),
                     reads=[Bq["qT"], B_const], writes=[B_ps[bank]])
                if hk % 4 == 3:
                    k0 = hk - 3
                    S.op("act", lambda e, bank=bank, k0=k0: e.copy(out=s_s[:, k0:k0 + 4, :].rearrange("p k n -> p (k n)"), in_=psF[:, bank, :]),
                         reads=[B_ps[bank]], writes=[Bq["s"]])
            for hk in range(16):
                yield
                S.op("dve", lambda e, hk=hk: e.max(out=top[:, hk, 0:8], in_=s_s[:, hk, :]), reads=[Bq["s"]], writes=[Bq["top"]])
                S.op("dve", lambda e, hk=hk: e.max_index(out=idxu[:, hk, 0:8], in_max=top[:, hk, 0:8], in_values=s_s[:, hk, :]),
                     reads=[Bq["s"], Bq["top"]], writes=[Bq["idxu"]])
                S.op("dve", lambda e, hk=hk: e.match_replace(out=work, in_to_replace=top[:, hk, 0:8], in_values=s_s[:, hk, :],
                                                             imm_value=-1e30),
                     reads=[Bq["s"], Bq["top"]], writes=[Bq["work"]])
                S.op("dve", lambda e, hk=hk: e.max(out=top[:, hk, 8:16], in_=work), reads=[Bq["work"]], writes=[Bq["top"]])
                S.op("dve", lambda e, hk=hk: e.max_index(out=idxu[:, hk, 8:16], in_max=top[:, hk, 8:16], in_values=work),
                     reads=[Bq["work"], Bq["top"]], writes=[Bq["idxu"]])
            S.op("dve", lambda e: e.tensor_copy(out=idxf, in_=idxu), reads=[Bq["idxu"]], writes=[Bq["idxf"]])
            for h in range(8):
                yield
                S.op("dve", lambda e, h=h: e.tensor_tensor(out=cand, in0=top[:, 2 * h, :].unsqueeze(2).to_broadcast([128, 16, 16]),
                                                           in1=top[:, 2 * h + 1, :].unsqueeze(1).to_broadcast([128, 16, 16]), op=ALU.add),
                     reads=[Bq["top"]], writes=[Bq["cand"]])
                S.op("pool", lambda e, h=h: e.tensor_scalar_mul(out=posf, in0=idxf[:, 2 * h, :], scalar1=128.0),
                     reads=[Bq["idxf"]], writes=[Bq["posf"]])
                S.op("pool", lambda e, h=h: e.tensor_tensor(out=cidx, in0=posf.unsqueeze(2).to_broadcast([128, 16, 16]),
                                                            in1=idxf[:, 2 * h + 1, :].unsqueeze(1).to_broadcast([128, 16, 16]), op=ALU.add),
                     reads=[Bq["idxf"], Bq["posf"]], writes=[Bq["cidx"]])
                candf = cand.rearrange("p a b -> p (a b)")
                cidxf = cidx.rearrange("p a b -> p (a b)")
                S.op("dve", lambda e: e.max(out=best[:, 0:8], in_=candf), reads=[Bq["cand"]], writes=[Bq["best"]])
                S.op("dve", lambda e: e.max_index(out=posu[:, 0:8], in_max=best[:, 0:8], in_values=candf),
                     reads=[Bq["cand"], Bq["best"]], writes=[Bq["posu"]])
                S.op("dve", lambda e: e.match_replace(out=work2, in_to_replace=best[:, 0:8], in_values=candf, imm_value=-1e30),
                     reads=[Bq["cand"], Bq["best"]], writes=[Bq["work2"]])
                S.op("dve", lambda e: e.max(out=best[:, 8:16], in_=work2), reads=[Bq["work2"]], writes=[Bq["best"]])
                S.op("dve", lambda e: e.max_index(out=posu[:, 8:16], in_max=best[:, 8:16], in_values=work2),
                     reads=[Bq["work2"], Bq["best"]], writes=[Bq["posu"]])
                S.op("dve", lambda e: e.tensor_copy(out=posf, in_=posu), reads=[Bq["posu"], Bq["cidx"]], writes=[Bq["posf"]])
                for k in range(16):
                    yield
                    S.op("dve", lambda e, h=h, k=k: e.scalar_tensor_tensor(out=work2, in0=iota_f, scalar=posf[:, k:k + 1],
                                                                         in1=cidxf, op0=ALU.is_equal, op1=ALU.mult,
                                                                         accum_out=ef[:, h * 16 + k:h * 16 + k + 1]),
                         reads=[Bq["posf"], Bq["cidx"], B_const, Bq["work2"]], writes=[Bq["work2"], Bq["ef"]])
                S.op("dve", lambda e: e.tensor_scalar_mul(out=smz[:, 0:1], in0=best[:, 0:1], scalar1=-1.0),
                     reads=[Bq["best"]], writes=[Bq["smz"]])
                S.op("act", lambda e, h=h: e.activation(out=gts_p[:, h * 16:(h + 1) * 16], in_=best, func=AF.Exp, bias=smz[:, 0:1],
                                                        accum_out=smz[:, 1:2]),
                     reads=[Bq["best"], Bq["smz"]], writes=[Bq["gts%d" % par], Bq["smz"]])
                S.op("dve", lambda e: e.reciprocal(out=smz[:, 2:3], in_=smz[:, 1:2]), reads=[Bq["smz"]], writes=[Bq["smz"]])
                S.op("dve", lambda e, h=h: e.tensor_scalar(out=gts_p[:, h * 16:(h + 1) * 16], in0=gts_p[:, h * 16:(h + 1) * 16],
                                                           scalar1=smz[:, 2:3], scalar2=None, op0=ALU.mult),
                     reads=[Bq["gts%d" % par], Bq["smz"]], writes=[Bq["gts%d" % par]])
            S.op("dve", lambda e: e.tensor_copy(out=ei_p, in_=ef), reads=[Bq["ef"]], writes=[Bq["ei%d" % par]])
            yield

        def stage2(i):
            par = i % 2
            r0, r1 = i * 128, (i + 1) * 128
            tt_p, xm_p, ei_p, gts_p = PARV["tt"][par], PARV["xm"][par], PARV["ei"][par], PARV["gts"][par]
            for m in range(128):
                j = ngbc[0] % NG
                ngbc[0] += 1
                yield
                S.dma("pool", lambda e, m=m, j=j: e.indirect_dma_start(
                    out=gring[j], out_offset=None, in_=EUb[:, :],
                    in_offset=bass.IndirectOffsetOnAxis(ap=ei_p[:, m:m + 1], axis=0)),
                    reads=[Bq["ei%d" % par]], writes=[B_gr[j]])
                S.op("dve", lambda e, m=m, j=j: e.scalar_tensor_tensor(out=junk2, in0=gring[j], scalar=1.0, in1=tt_p, op0=ALU.mult,
                                                                      op1=ALU.mult, accum_out=actv[:, m:m + 1]),
                     reads=[B_gr[j], Bq["tt%d" % par]], writes=[Bq["junk2"], Bq["actv"]])
            S.op("act", lambda e: e.activation(out=wgt, in_=actv, func=AF.Gelu), reads=[Bq["actv"]], writes=[Bq["wgt"]])
            S.op("dve", lambda e: e.tensor_tensor(out=wgt, in0=wgt, in1=gts_p, op=ALU.mult),
                 reads=[Bq["wgt"], Bq["gts%d" % par]], writes=[Bq["wgt"]])
            for m in range(128):
                j = ngbc[0] % NG
                ngbc[0] += 1
                jd = m % 4
                yield
                S.dma("pool", lambda e, m=m, j=j: e.indirect_dma_start(
                    out=gring[j], out_offset=None, in_=EVb[:, :],
                    in_offset=bass.IndirectOffsetOnAxis(ap=ei_p[:, m:m + 1], axis=0)),
                    reads=[Bq["ei%d" % par]], writes=[B_gr[j]])
                S.op("act", lambda e, m=m, jd=jd: e.activation(out=dgs[jd], in_=identF[:], func=AF.Copy, scale=wgt[:, m:m + 1]),
                     reads=[Bq["wgt"], B_const], writes=[B_dg[jd]])
                for half in range(2):
                    S.op("pe", lambda e, m=m, j=j, jd=jd, half=half: e.matmul(
                        psF[:, 4 + half, :], lhsT=dgs[jd], rhs=gring[j][:, half * 512:(half + 1) * 512],
                        start=(m == 0), stop=(m == 127)),
                        reads=[B_dg[jd], B_gr[j]], writes=[B_ps[4 + half]])
            S.op("dve", lambda e: e.tensor_tensor(out=xm_p[:, 0:512], in0=psF[:, 4, :], in1=xm_p[:, 0:512], op=ALU.add),
                 reads=[B_ps[4], Bq["xm%d" % par]], writes=[Bq["xm%d" % par]])
            S.op("dve", lambda e: e.tensor_tensor(out=xm_p[:, 512:1024], in0=psF[:, 5, :], in1=xm_p[:, 512:1024], op=ALU.add),
                 reads=[B_ps[5], Bq["xm%d" % par]], writes=[Bq["xm%d" % par]])
            S.dma("sp", lambda e, r0=r0, r1=r1: e.dma_start(out=out_d[r0:r1, :], in_=xm_p), reads=[Bq["xm%d" % par]])
            yield

        g1 = stage1(0)
        for _ in g1:
            pass
        for i in range(NSH):
            g2 = stage2(i)
            g1 = stage1(i + 1) if i + 1 < NSH else None
            while True:
                alive = False
                for _k in range(3):
                    try:
                        next(g2)
                        alive = True
                    except StopIteration:
                        break
                if g1 is not None:
                    try:
                        next(g1)
                        alive = True
                    except StopIteration:
                        g1 = None
                if not alive:
                    break
        S.barrier()
        S.emit()
    return nc


OFF_XBC, OFF_DT, OFF_QKV, OFF_GATE = 2048, 5120, 5184, 8256


def na_bias_tables(rpb, rows):
    NTl = rows // 2
    pats = [0, 1, 2, NTl - 2, NTl - 1]
    out = np.full((5, 128, 5, 16, 128), -30000.0, np.float32)
    for pi, t in enumerate(pats):
        kt0 = min(max(t - 2, 0), NTl - 5)
        for qi in range(128):
            r = 2 * t + qi // 64
            c = qi % 64
            r0 = min(max(r - 4, 0), rows - 8)
            c0 = min(max(c - 8, 0), 64 - 16)
            for kr in range(r0, r0 + 8):
                kb = kr // 2 - kt0
                assert 0 <= kb < 5
                kp0 = (kr % 2) * 64
                cols = np.arange(c0, c0 + 16)
                out[pi, kp0 + cols, kb, :, qi] = rpb[:, kr - r + 7, cols - c + 15].T
    return out.reshape(5, 128, 5 * 16 * 128)


def prep_common(inp, L):
    w_in = inp["w_in"][0]
    m = {}
    m["gmix"] = np.ascontiguousarray(inp["g_mix"][0].reshape(8, 128).T)
    colsF = np.concatenate([np.arange(OFF_XBC, OFF_XBC + 3072), np.arange(OFF_QKV, OFF_QKV + 2048)])
    wf = w_in[:, colsF].reshape(8, 128, NF, 128)
    m["wf"] = np.ascontiguousarray(wf.transpose(2, 1, 0, 3))
    colsT = np.concatenate([np.arange(0, 2048), np.arange(OFF_DT, OFF_DT + 64),
                            np.arange(OFF_QKV + 2048, OFF_QKV + 3072), np.arange(OFF_GATE, OFF_GATE + 2048)])
    wt = w_in[:, colsT].reshape(8, 128, NTC)
    m["wt"] = np.ascontiguousarray(wt.transpose(1, 0, 2))
    cw = inp["conv_w"][0]
    m["convw"] = np.ascontiguousarray(cw.reshape(5, 24, 128).transpose(2, 1, 0).reshape(128, 120))
    cb = inp["conv_b"][0]
    m["convbT"] = np.ascontiguousarray(cb.reshape(24, 128).T)
    m["convbrow"] = np.ascontiguousarray(cb.reshape(1, 3072))
    m["dtb"] = np.ascontiguousarray(np.broadcast_to(inp["dt_bias"][0].reshape(1, 64), (128, 64)))
    m["alog"] = np.ascontiguousarray(np.broadcast_to(inp["a_log"][0].reshape(1, 64), (128, 64)))
    m["dskip"] = np.ascontiguousarray(np.broadcast_to(np.repeat(inp["d_skip"][0], 64).reshape(1, 2048), (128, 2048)))
    m["sng"] = np.ascontiguousarray(np.broadcast_to(inp["ssd_norm_g"][0].reshape(1, 2048), (128, 2048)))
    m["qg"] = np.ascontiguousarray(np.tile(inp["q_norm_g"][0], 2).reshape(128, 1))
    m["kg"] = np.ascontiguousarray(np.tile(inp["k_norm_g"][0], 2).reshape(128, 1))
    m["bt"] = na_bias_tables(inp["rpb"][0], L // 64)
    m["wo"] = np.ascontiguousarray(inp["w_out"][0].reshape(24, 128, 1024).transpose(1, 0, 2))
    m["gffn"] = np.ascontiguousarray(np.broadcast_to(inp["g_ffn"][0].reshape(1, 1024), (128, 1024)))
    m["wpq"] = np.ascontiguousarray(inp["w_pq"][0].reshape(8, 128, 16, 128).transpose(2, 1, 0, 3))
    m["skT"] = np.ascontiguousarray(inp["sub_keys"][0].transpose(2, 0, 1))
    m["eu"] = np.ascontiguousarray(inp["expert_u"][0])
    m["ev"] = np.ascontiguousarray(inp["expert_v"][0])
    return {k: np.asarray(v, np.float32) for k, v in m.items()}


_NC_CACHE = {}


def core_shares(NT):
    groups = {0: [0, 3, 6], 1: [1, 4, 7], 2: [2, 5]}
    shares = {}
    for sq, cores in groups.items():
        parts = np.array_split(np.arange(NT), len(cores))
        for c, p in zip(cores, parts):
            shares[c] = [int(v) for v in p]
    NSH = (NT + 1) // 2
    return shares, NSH


def share_rowidx(tiles, NSH):
    tl = list(tiles) + [tiles[-1]] * (NSH - len(tiles))
    idx = np.array(tl, np.int32)[None, :] * 128 + np.arange(128, dtype=np.int32)[:, None]
    return np.ascontiguousarray(idx.astype(np.int32))


def kernel(**inp):
    L = inp["x_prompt"].shape[1]
    seqs = [inp["x_prompt"][0], inp["x_prompt"][1], inp["x_sample"][0]]
    common = prep_common(inp, L)
    if L not in _NC_CACHE:
        _NC_CACHE[L] = build(L)
    nc = _NC_CACHE[L]
    shares, NSH = core_shares(L // 128)
    in_maps = []
    for c in range(8):
        m = dict(common)
        m["x"] = np.ascontiguousarray(seqs[c % 3], dtype=np.float32)
        m["rowidx"] = share_rowidx(shares[c], NSH)
        in_maps.append(m)
    res = run_bass_kernel_spmd(nc, in_maps, core_ids=list(range(8)))
    outs = [np.empty((L, D), np.float32) for _ in range(3)]
    for c in range(8):
        o = np.asarray(res.results[c]["out"], np.float32)
        for i, t in enumerate(shares[c]):
            outs[c % 3][t * 128:(t + 1) * 128] = o[i * 128:(i + 1) * 128]
    return (np.stack(outs[0:2], 0), outs[2][None])
```

```python
import contextlib
import numpy as np
import concourse.bass as bass
import concourse.mybir as mybir
from concourse.bass_utils import run_bass_kernel_spmd

F32 = mybir.dt.float32
BF16 = mybir.dt.bfloat16
I32 = mybir.dt.int32
U32 = mybir.dt.uint32
AF = mybir.ActivationFunctionType
ALU = mybir.AluOpType

ENGS = ("pe", "act", "dve", "pool", "sp")
NDMASEM = 12


class Buf:
    __slots__ = ("name", "w", "r")

    def __init__(self, name=""):
        self.name = name
        self.w = None
        self.r = []


class Sched:
    def __init__(self, nc):
        self.nc = nc
        self.q = {e: [] for e in ENGS}
        self.nop = {e: 0 for e in ENGS}
        self.opent = {e: [] for e in ENGS}
        self.seen = {e: {} for e in ENGS}
        self.dma_cnt = {(q, j): 0 for q in ("sp", "pool", "act") for j in range(NDMASEM)}
        self.dma_rr = {"sp": 0, "pool": 0, "act": 0}

    def _deps(self, eng, reads, writes):
        need = {}

        def add(tok):
            if tok is None:
                return
            k, v = tok
            if need.get(k, 0) < v:
                need[k] = v
        for b in reads:
            add(b.w)
        for b in writes:
            add(b.w)
            for t in b.r:
                add(t)
        waits = []
        seen = self.seen[eng]
        for k, v in need.items():
            if seen.get(k, 0) < v:
                seen[k] = v
                waits.append((k, v))
                if k[0] == "e":
                    self.opent[k[1]][v - 1][3] = True
        return waits

    def _commit(self, tok, reads, writes):
        for b in reads:
            b.r.append(tok)
            if len(b.r) > 16:
                m = {}
                for k, v in b.r:
                    if m.get(k, 0) < v:
                        m[k] = v
                b.r = list(m.items())
        for b in writes:
            b.w = tok
            b.r = []

    def op(self, eng, fn, reads=(), writes=()):
        waits = self._deps(eng, reads, writes)
        self.nop[eng] += 1
        tok = (("e", eng), self.nop[eng])
        ent = [waits, fn, "e", False]
        self.q[eng].append(ent)
        self.opent[eng].append(ent)
        self._commit(tok, reads, writes)
        return tok

    def dma(self, eng, fn, reads=(), writes=()):
        waits = self._deps(eng, reads, writes)
        j = self.dma_rr[eng]
        self.dma_rr[eng] = (j + 1) % NDMASEM
        self.dma_cnt[(eng, j)] += 16
        tok = (("d", eng, j), self.dma_cnt[(eng, j)])
        self.q[eng].append([waits, fn, ("d", eng, j), True])
        self._commit(tok, reads, writes)
        return tok

    def barrier(self):
        allw = [(("e", e), self.nop[e]) for e in ENGS if self.nop[e] > 0]
        allw += [(("d", q, j), v) for (q, j), v in self.dma_cnt.items() if v > 0]
        for e in ENGS:
            waits = []
            for k, v in allw:
                if self.seen[e].get(k, 0) < v:
                    self.seen[e][k] = v
                    waits.append((k, v))
                    if k[0] == "e":
                        self.opent[k[1]][v - 1][3] = True
            self.q[e].append([waits, None, None, False])

    def emit(self):
        nc = self.nc
        real = {}
        for e in ENGS:
            c = 0
            arr = [0]
            for ent in self.opent[e]:
                if ent[3]:
                    c += 1
                arr.append(c)
            real[e] = arr
        with contextlib.ExitStack() as st:
            sems = {}
            for e in ENGS:
                sems[("e", e)] = st.enter_context(nc.semaphore("s_" + e))
            for (q, j), v in self.dma_cnt.items():
                if v > 0:
                    sems[("d", q, j)] = st.enter_context(nc.semaphore("d_%s_%d" % (q, j)))
            block = st.enter_context(nc.Block())

            def run(ename):
                def body(eng):
                    for waits, fn, kind, marked in self.q[ename]:
                        for k, v in waits:
                            if k[0] == "e":
                                eng.wait_ge(sems[k], real[k[1]][v])
                            else:
                                eng.wait_ge(sems[k], v)
                        if fn is not None:
                            ins = fn(eng)
                            if kind == "e":
                                if marked:
                                    ins.then_inc(sems[("e", ename)], 1)
                            else:
                                ins.then_inc(sems[kind], 16)
                return body
            block.tensor(run("pe"))
            block.scalar(run("act"))
            block.vector(run("dve"))
            block.gpsimd(run("pool"))
            block.sync(run("sp"))


D = 1024
NF = 40
NTC = 5184
NEXP = 16384


def build(L, dbg=False, NSH=None):
    NT = L // 128
    if NSH is None:
        NSH = (NT + 1) // 2
    ROWS = L // 64
    nc = bass.Bass("TRN2", target_bir_lowering=False)
    dt_in = lambda n, s, d=F32: nc.dram_tensor(n, s, d, kind="ExternalInput")
    x_d = dt_in("x", [L, D])
    gmix_d = dt_in("gmix", [128, 8])
    wf_d = dt_in("wf", [NF, 128, 8, 128])
    wt_d = dt_in("wt", [128, 8, NTC])
    convw_d = dt_in("convw", [128, 120])
    convbT_d = dt_in("convbT", [128, 24])
    convbrow_d = dt_in("convbrow", [1, 3072])
    dtb_d = dt_in("dtb", [128, 64])
    alog_d = dt_in("alog", [128, 64])
    dskip_d = dt_in("dskip", [128, 2048])
    sng_d = dt_in("sng", [128, 2048])
    qg_d = dt_in("qg", [128, 1])
    kg_d = dt_in("kg", [128, 1])
    bt_d = dt_in("bt", [5, 128, 5 * 16 * 128])
    wo_d = dt_in("wo", [128, 24, 1024])
    gffn_d = dt_in("gffn", [128, 1024])
    wpq_d = dt_in("wpq", [16, 128, 8, 128])
    skT_d = dt_in("skT", [128, 2, 128])
    eu_d = dt_in("eu", [NEXP, D])
    ev_d = dt_in("ev", [NEXP, D])
    rowidx_d = nc.dram_tensor("rowidx", [128, NSH], I32, kind="ExternalInput")
    out_d = nc.dram_tensor("out", [NSH * 128, D], F32, kind="ExternalOutput")
    sk = "ExternalOutput" if dbg else "Internal"
    PF = nc.dram_tensor("PF", [NF * 128, L], BF16, kind=sk)
    PTz = nc.dram_tensor("PTz", [L, 2048], BF16, kind=sk)
    PTdt = nc.dram_tensor("PTdt", [L, 64], F32, kind=sk)
    PTv = nc.dram_tensor("PTv", [L, 1024], BF16, kind=sk)
    PTg = nc.dram_tensor("PTg", [L, 2048], BF16, kind=sk)
    XC = nc.dram_tensor("XC", [L, 2048], BF16, kind=sk)
    BC = nc.dram_tensor("BC", [L, 512], BF16, kind=sk)
    BCT = nc.dram_tensor("BCT", [512, L], BF16, kind=sk)
    CCT = nc.dram_tensor("CCT", [512, L], BF16, kind=sk)
    YF = nc.dram_tensor("YF", [L, 2048], F32, kind=sk)
    YTOK = nc.dram_tensor("YTOK", [L, 3072], BF16, kind=sk)
    EUb = nc.dram_tensor("EUb", [NEXP, D], BF16)
    EVb = nc.dram_tensor("EVb", [NEXP, D], BF16)
    XM = nc.dram_tensor("XM", [NSH * 128, D], F32, kind=sk)
    DBGF = nc.dram_tensor("DBGF", [128, 4096], F32, kind=sk)
    QN = nc.dram_tensor("QN", [1024, L], BF16, kind=sk)
    KN = nc.dram_tensor("KN", [1024, L], BF16, kind=sk)
    DBGB = nc.dram_tensor("DBGB", [128, 4096], BF16, kind=sk)

    S = Sched(nc)
    with contextlib.ExitStack() as st:
        sb = lambda n, s, d: st.enter_context(nc.sbuf_tensor(n, s, d))
        pst = lambda n, s, d: st.enter_context(nc.psum_tensor(n, s, d))
        identB = sb("identB", [128, 128], BF16)
        identF = sb("identF", [128, 128], F32)
        triU = sb("triU", [128, 128], F32)
        triLo = sb("triLo", [128, 128], F32)
        triLs = sb("triLs", [128, 128], F32)
        triUs = sb("triUs", [128, 128], F32)
        onesF = sb("onesF", [128, 128], F32)
        onesB = sb("onesB", [128, 128], BF16)
        gmix = sb("gmix_s", [128, 8], F32)
        wbuf = sb("wbuf", [128, 41472], BF16)
        fbuf = sb("fbuf", [128, 16384], F32)
        scr = sb("scr", [128, 6144], F32)
        scrB = scr[:, :].bitcast(BF16)
        psF = pst("psF", [128, 6, 512], F32)
        psB = pst("psB", [128, 2, 1024], BF16)
        B_const = Buf("const")
        B_w = Buf("wbuf")
        B_f = Buf("fbuf")
        B_ps = [Buf("ps%d" % i) for i in range(6)]
        B_psb = [Buf("psb%d" % i) for i in range(2)]

        def tri(t, pattern, cm, cmp):
            S.op("pool", lambda e: e.memset(t[:], 1.0), writes=[B_const])
            S.op("pool", lambda e: e.affine_select(out=t[:], in_=t[:], pattern=pattern, compare_op=cmp,
                                                   fill=0.0, base=0, channel_multiplier=cm),
                 reads=[B_const], writes=[B_const])
        S.op("pool", lambda e: e.memset(identF[:], 0.0), writes=[B_const])
        S.op("pool", lambda e: e.affine_select(out=identF[:], in_=identF[:], pattern=[[-1, 128]],
                                               compare_op=ALU.not_equal, fill=1.0, base=0, channel_multiplier=1),
             reads=[B_const], writes=[B_const])
        S.op("pool", lambda e: e.tensor_copy(out=identB[:], in_=identF[:]), reads=[B_const], writes=[B_const])
        tri(triU, [[1, 128]], -1, ALU.is_ge)
        tri(triLo, [[-1, 128]], 1, ALU.is_ge)
        tri(triLs, [[-1, 128]], 1, ALU.is_gt)
        tri(triUs, [[1, 128]], -1, ALU.is_gt)
        S.op("pool", lambda e: e.memset(onesF[:], 1.0), writes=[B_const])
        S.op("pool", lambda e: e.memset(onesB[:], 1.0), writes=[B_const])
        S.dma("sp", lambda e: e.dma_start(out=gmix[:], in_=gmix_d[:, :]), writes=[B_const])
        S.barrier()

        xt = [fbuf[:, i * 1024:(i + 1) * 1024] for i in range(2)]
        B_xt = [Buf() for _ in range(2)]
        sq = fbuf[:, 2048:4096]
        B_sq = Buf()
        ss = sb("ss", [128, 2], F32)
        B_ss = Buf()
        xs = scrB[:, 4096:5120]
        B_xs = Buf()
        hT = scrB[:, 0:4096].rearrange("p (k t) -> p k t", k=8)
        B_hT = Buf()
        fe_n = [0]

        def rms_rstd(src, Bsrc, width):
            sc = 1.0 / np.sqrt(width)
            S.op("act", lambda e: e.activation(out=sq[:, 0:width], in_=src, func=AF.Square, scale=float(sc),
                                               accum_out=ss[:, 0:1]),
                 reads=[Bsrc], writes=[B_sq, B_ss])
            S.op("dve", lambda e: e.tensor_scalar_add(out=ss[:, 0:1], in0=ss[:, 0:1], scalar1=1e-6),
                 reads=[B_ss], writes=[B_ss])
            S.op("act", lambda e: e.activation(out=ss[:, 0:1], in_=ss[:, 0:1], func=AF.Ln), reads=[B_ss], writes=[B_ss])
            S.op("act", lambda e: e.activation(out=ss[:, 0:1], in_=ss[:, 0:1], func=AF.Exp, scale=-0.5),
                 reads=[B_ss], writes=[B_ss])

        def front_end(i, tt):
            j = fe_n[0] % 2
            fe_n[0] += 1
            S.dma("sp", lambda e: e.dma_start(out=xt[j], in_=x_d[i * 128:(i + 1) * 128, :]), writes=[B_xt[j]])
            rms_rstd(xt[j], B_xt[j], D)
            S.op("dve", lambda e: e.tensor_scalar(out=xs, in0=xt[j], scalar1=ss[:, 0:1], scalar2=None,
                                                  op0=ALU.mult),
                 reads=[B_xt[j], B_ss], writes=[B_xs])
            for k in range(8):
                S.op("pe", lambda e, k=k: e.transpose(out=psB[:, 0, k * 128:(k + 1) * 128],
                                                      in_=xs[:, k * 128:(k + 1) * 128], identity=identB[:]),
                     reads=[B_xs, B_const], writes=[B_psb[0]])
            for k in range(8):
                if k % 2 == 0:
                    S.op("act", lambda e, k=k: e.activation(out=hT[:, k, tt * 128:(tt + 1) * 128],
                                                            in_=psB[:, 0, k * 128:(k + 1) * 128],
                                                            func=AF.Copy, scale=gmix[:, k:k + 1]),
                         reads=[B_psb[0], B_const], writes=[B_hT])
                else:
                    S.op("dve", lambda e, k=k: e.tensor_scalar(out=hT[:, k, tt * 128:(tt + 1) * 128],
                                                               in0=psB[:, 0, k * 128:(k + 1) * 128],
                                                               scalar1=gmix[:, k:k + 1], scalar2=None, op0=ALU.mult),
                         reads=[B_psb[0], B_const], writes=[B_hT])

        wst = [fbuf[:, 4096 + i * 1024:4096 + (i + 1) * 1024] for i in range(2)]
        B_wst = [Buf() for _ in range(2)]
        WF = wbuf[:, 0:NF * 1024].rearrange("p (c k n) -> p c k n", c=NF, k=8)
        for ct in range(NF):
            j = ct % 2
            S.dma("sp", lambda e, ct=ct, j=j: e.dma_start(out=wst[j], in_=wf_d[ct].rearrange("p k n -> p (k n)")),
                  writes=[B_wst[j]])
            eng = "dve" if ct % 2 == 0 else "pool"
            S.op(eng, lambda e, ct=ct, j=j: e.tensor_copy(out=wbuf[:, ct * 1024:(ct + 1) * 1024], in_=wst[j]),
                 reads=[B_wst[j]], writes=[B_w])
        stg = [scrB[:, 5120 + i * 512:5120 + (i + 1) * 512] for i in range(4)]
        B_stg = [Buf() for _ in range(4)]
        n_ev = 0
        for b in range(L // 512):
            for tt in range(4):
                front_end(b * 4 + tt, tt)
            for ct in range(NF):
                bank = ct % 4
                for k in range(8):
                    S.op("pe", lambda e, ct=ct, k=k, bank=bank: e.matmul(psF[:, bank, :], lhsT=WF[:, ct, k, :],
                                                                          rhs=hT[:, k, :], start=(k == 0), stop=(k == 7)),
                         reads=[B_w, B_hT], writes=[B_ps[bank]])
                sj = n_ev % 4
                n_ev += 1
                if sj % 2 == 0:
                    S.op("act", lambda e, bank=bank, sj=sj: e.copy(out=stg[sj], in_=psF[:, bank, :]),
                         reads=[B_ps[bank]], writes=[B_stg[sj]])
                else:
                    S.op("dve", lambda e, bank=bank, sj=sj: e.tensor_copy(out=stg[sj], in_=psF[:, bank, :]),
                         reads=[B_ps[bank]], writes=[B_stg[sj]])
                S.dma("sp", lambda e, ct=ct, b=b, sj=sj: e.dma_start(
                    out=PF[ct * 128:(ct + 1) * 128, b * 512:(b + 1) * 512], in_=stg[sj]),
                    reads=[B_stg[sj]])
        S.barrier()

        WT = wbuf[:, 0:8 * NTC].rearrange("p (k n) -> p k n", k=8)
        for k in range(8):
            for c0 in range(0, NTC, 1024):
                c1 = min(NTC, c0 + 1024)
                j = (c0 // 1024) % 2
                S.dma("sp", lambda e, k=k, c0=c0, c1=c1, j=j: e.dma_start(out=wst[j][:, 0:c1 - c0], in_=wt_d[:, k, c0:c1]),
                      writes=[B_wst[j]])
                S.op("dve", lambda e, k=k, c0=c0, c1=c1, j=j: e.tensor_copy(out=WT[:, k, c0:c1], in_=wst[j][:, 0:c1 - c0]),
                     reads=[B_wst[j]], writes=[B_w])
        stz = scrB[:, 7168:9216]
        stv = scrB[:, 9216:10240]
        stgt = scrB[:, 10240:12288]
        stdt = sb("stdt", [128, 64], F32)
        B_stz, B_stv, B_stgt, B_stdt = Buf(), Buf(), Buf(), Buf()
        nb = 0
        for i in range(NT):
            front_end(i, 0)

            def tgroup(c0, n, func, dst, Bdst, dcol):
                nonlocal nb
                bank = nb % 4
                nb += 1
                for k in range(8):
                    S.op("pe", lambda e, k=k: e.matmul(psF[:, bank, 0:n], lhsT=hT[:, k, 0:128], rhs=WT[:, k, c0:c0 + n],
                                                       start=(k == 0), stop=(k == 7)),
                         reads=[B_w, B_hT], writes=[B_ps[bank]])
                S.op("act", lambda e: e.activation(out=dst[:, dcol:dcol + n], in_=psF[:, bank, 0:n], func=func),
                     reads=[B_ps[bank]], writes=[Bdst])
            for q in range(4):
                tgroup(q * 512, 512, AF.Silu, stz, B_stz, q * 512)
            tgroup(2048, 64, AF.Copy, stdt, B_stdt, 0)
            for q in range(2):
                tgroup(2112 + q * 512, 512, AF.Copy, stv, B_stv, q * 512)
            for q in range(4):
                tgroup(3136 + q * 512, 512, AF.Sigmoid, stgt, B_stgt, q * 512)
            r0, r1 = i * 128, (i + 1) * 128
            S.dma("sp", lambda e, r0=r0, r1=r1: e.dma_start(out=PTz[r0:r1, :], in_=stz), reads=[B_stz])
            S.dma("sp", lambda e, r0=r0, r1=r1: e.dma_start(out=PTdt[r0:r1, :], in_=stdt[:]), reads=[B_stdt])
            S.dma("sp", lambda e, r0=r0, r1=r1: e.dma_start(out=PTv[r0:r1, :], in_=stv), reads=[B_stv])
            S.dma("sp", lambda e, r0=r0, r1=r1: e.dma_start(out=PTg[r0:r1, :], in_=stgt), reads=[B_stgt])
        S.barrier()

        convw_s = sb("convw_s", [128, 120], F32)
        convbT_s = sb("convbT_s", [128, 24], F32)
        convbrow_s = sb("convbrow_s", [1, 3072], F32)
        dtb_s = sb("dtb_s", [128, 64], F32)
        aneg = sb("aneg", [128, 64], F32)
        dskip_s = sb("dskip_s", [128, 2048], F32)
        sng_s = sb("sng_s", [128, 2048], F32)
        for dst, src in ((convw_s, convw_d), (convbT_s, convbT_d), (convbrow_s, convbrow_d), (dtb_s, dtb_d),
                         (aneg, alog_d), (dskip_s, dskip_d), (sng_s, sng_d)):
            S.dma("sp", lambda e, dst=dst, src=src: e.dma_start(out=dst[:], in_=src[:, :]), writes=[B_const])
        S.op("act", lambda e: e.activation(out=aneg[:], in_=aneg[:], func=AF.Exp), reads=[B_const], writes=[B_const])
        S.op("dve", lambda e: e.tensor_scalar_mul(out=aneg[:], in0=aneg[:], scalar1=-1.0), reads=[B_const], writes=[B_const])
        DG = wbuf[:, 0:120 * 128].rearrange("p (c j n) -> p c j n", c=24, j=5)
        for i in range(120):
            S.op("dve" if i % 2 == 0 else "pool",
                 lambda e, i=i: e.tensor_scalar(out=DG[:, i // 5, i % 5, :], in0=identF[:], scalar1=convw_s[:, i:i + 1],
                                                scalar2=None, op0=ALU.mult),
                 reads=[B_const], writes=[B_w])
        WB0 = 120 * 128

        def wv(off, n):
            return wbuf[:, WB0 + off:WB0 + off + n]
        xin = wv(0, 24 * 132).rearrange("p (c t) -> p c t", c=24)
        xcst = wv(3200, 2048)
        bcst = wv(5248, 512)
        tst = wv(5760, 1024).rearrange("p (g t) -> p g t", g=8)
        B_xin, B_xcst, B_bcst, B_tst = Buf(), Buf(), Buf(), Buf()
        nb = 0
        for c in range(NT):
            lo, hi = c * 128 - 2, c * 128 + 130
            s0, s1 = max(lo, 0), min(hi, L)
            if lo < 0 or hi > L:
                S.op("pool", lambda e: e.memset(xin, 0.0), writes=[B_xin])
            S.dma("sp", lambda e, s0=s0, s1=s1, lo=lo: e.dma_start(
                out=xin[:, :, s0 - lo:s1 - lo], in_=PF[0:3072, s0:s1].rearrange("(c p) t -> p c t", p=128)),
                writes=[B_xin])

            def conv_T(ct, bank, slot):
                o = psF[:, bank, slot * 128:(slot + 1) * 128]
                for j in range(5):
                    S.op("pe", lambda e, j=j: e.matmul(o, lhsT=xin[:, ct, j:j + 128], rhs=DG[:, ct, j, :],
                                                       start=(j == 0), stop=False),
                         reads=[B_xin, B_w], writes=[B_ps[bank]])
                S.op("pe", lambda e: e.matmul(o, lhsT=onesF[0:1, 0:128], rhs=convbrow_s[0:1, ct * 128:(ct + 1) * 128],
                                              start=False, stop=True),
                     reads=[B_const], writes=[B_ps[bank]])
            for cg in range(5):
                bank = nb % 4
                nb += 1
                for slot in range(4):
                    conv_T(cg * 4 + slot, bank, slot)
                if cg < 4:
                    S.op("act", lambda e, bank=bank, cg=cg: e.activation(out=xcst[:, cg * 512:(cg + 1) * 512],
                                                                         in_=psF[:, bank, :], func=AF.Silu),
                         reads=[B_ps[bank]], writes=[B_xcst])
                else:
                    S.op("act", lambda e, bank=bank: e.activation(out=bcst, in_=psF[:, bank, :], func=AF.Silu),
                         reads=[B_ps[bank]], writes=[B_bcst])
            S.dma("sp", lambda e, c=c: e.dma_start(out=XC[c * 128:(c + 1) * 128, :], in_=xcst), reads=[B_xcst])
            S.dma("sp", lambda e, c=c: e.dma_start(out=BC[c * 128:(c + 1) * 128, :], in_=bcst), reads=[B_bcst])
            for half in range(2):
                bank = nb % 4
                nb += 1
                for slot in range(4):
                    ct = 16 + half * 4 + slot
                    o = psF[:, bank, slot * 128:(slot + 1) * 128]
                    for j in range(5):
                        S.op("pe", lambda e, j=j, ct=ct, o=o: e.matmul(o, lhsT=DG[:, ct, j, :], rhs=xin[:, ct, j:j + 128],
                                                                        start=(j == 0), stop=(j == 4)),
                             reads=[B_xin, B_w], writes=[B_ps[bank]])
                    S.op("act", lambda e, ct=ct, o=o: e.activation(out=tst[:, ct - 16, :], in_=o, func=AF.Silu,
                                                                   bias=convbT_s[:, ct:ct + 1]),
                         reads=[B_ps[bank], B_const], writes=[B_tst])
            S.dma("sp", lambda e, c=c: e.dma_start(out=BCT[:, c * 128:(c + 1) * 128].rearrange("(g p) t -> p g t", p=128),
                                                   in_=tst[:, 0:4, :]), reads=[B_tst])
            S.dma("sp", lambda e, c=c: e.dma_start(out=CCT[:, c * 128:(c + 1) * 128].rearrange("(g p) t -> p g t", p=128),
                                                   in_=tst[:, 4:8, :]), reads=[B_tst])
        S.barrier()

        def fv(off, n):
            return fbuf[:, off:off + n]
        LA = fv(0, 4096).rearrange("p (h s) -> p h s", h=32)
        yacc = fv(4096, 2048)
        Hs = fv(6144, 2048)
        yf = fv(8192, 2048)
        yz = fv(10240, 2048)
        Eg = fv(12288, 1024)
        tmpf = fv(13312, 512)
        MTm = fv(13824, 512).rearrange("p (g l) -> p g l", g=4)
        sm = fv(14336, 256)
        sm2 = fv(14592, 96)
        sqw = fv(14700, 1600)
        xc = wv(0, 2048)
        bc = wv(2048, 512)
        bct = wv(2560, 512).rearrange("p (g t) -> p g t", g=4)
        cct = wv(3072, 512).rearrange("p (g t) -> p g t", g=4)
        xdt = wv(3584, 2048).rearrange("p (h d) -> p h d", h=32)
        MD = wv(5632, 1024).rearrange("p (h l) -> p h l", h=8)
        Hb = wv(6656, 2048)
        xdd = wv(8704, 512).rearrange("p (h d) -> p h d", h=8)
        zs = wv(9216, 2048)
        yn = wv(11264, 2048)
        ynT = wv(13312, 2048).rearrange("p (c t) -> p c t", c=16)
        dtr = sb("dtr", [128, 64], F32)
        Bn = {n: Buf(n) for n in "LA yacc Hs yf yz Eg tmpf MTm sm sm2 xc bc bct cct xdt MD Hb xdd zs yn ynT dtr".split()}
        def sweep(d):
            S.op("pool", lambda e: e.memset(Hs, 0.0), writes=[Bn["Hs"]])
            S.op("pool", lambda e: e.memset(Hb, 0.0), writes=[Bn["Hb"]])
            T1 = triLs if d == 0 else triUs
            T2 = triU if d == 0 else triLo
            Tds = triLs if d == 0 else triUs
            order = range(NT) if d == 0 else range(NT - 1, -1, -1)
            for c in order:
                r0, r1 = c * 128, (c + 1) * 128
                S.dma("sp", lambda e, r0=r0, r1=r1: e.dma_start(out=xc, in_=XC[r0:r1, :]), writes=[Bn["xc"]])
                S.dma("sp", lambda e, r0=r0, r1=r1: e.dma_start(out=bc, in_=BC[r0:r1, :]), writes=[Bn["bc"]])
                S.dma("sp", lambda e, r0=r0, r1=r1: e.dma_start(out=bct, in_=BCT[:, r0:r1].rearrange("(g p) t -> p g t", p=128)),
                      writes=[Bn["bct"]])
                S.dma("sp", lambda e, r0=r0, r1=r1: e.dma_start(out=cct, in_=CCT[:, r0:r1].rearrange("(g p) t -> p g t", p=128)),
                      writes=[Bn["cct"]])
                S.dma("sp", lambda e, r0=r0, r1=r1: e.dma_start(out=dtr[:], in_=PTdt[r0:r1, :]), writes=[Bn["dtr"]])
                if d == 1:
                    S.dma("sp", lambda e, r0=r0, r1=r1: e.dma_start(out=yf, in_=YF[r0:r1, :]), writes=[Bn["yf"]])
                    S.dma("sp", lambda e, r0=r0, r1=r1: e.dma_start(out=zs, in_=PTz[r0:r1, :]), writes=[Bn["zs"]])
                o32 = slice(d * 32, d * 32 + 32)
                smB = [Bn["sm"]]
                S.op("dve", lambda e: e.tensor_tensor(out=sm[:, 0:32], in0=dtr[:, o32], in1=dtb_s[:, o32], op=ALU.add),
                     reads=[Bn["dtr"], B_const], writes=smB)
                S.op("act", lambda e: e.activation(out=sm[:, 32:64], in_=sm[:, 0:32], func=AF.Abs),
                     reads=smB, writes=smB)
                S.op("act", lambda e: e.activation(out=sm[:, 64:96], in_=sm[:, 32:64], func=AF.Exp, scale=-1.0),
                     reads=smB, writes=smB)
                S.op("act", lambda e: e.activation(out=sm[:, 96:128], in_=sm[:, 64:96], func=AF.Ln, bias=1.0),
                     reads=smB, writes=smB)
                S.op("dve", lambda e: e.tensor_scalar_max(out=sm[:, 128:160], in0=sm[:, 0:32], scalar1=0.0),
                     reads=smB, writes=smB)
                S.op("dve", lambda e: e.tensor_tensor(out=sm[:, 160:192], in0=sm[:, 128:160], in1=sm[:, 96:128], op=ALU.add),
                     reads=smB, writes=smB)
                S.op("dve", lambda e: e.tensor_tensor(out=sm[:, 192:224], in0=sm[:, 160:192], in1=aneg[:, o32], op=ALU.mult),
                     reads=smB + [B_const], writes=smB)
                dtv = sm[:, 160:192]
                adt = sm[:, 192:224]
                S.op("dve", lambda e: e.tensor_tensor(out=xdt, in0=xc.rearrange("p (h d) -> p h d", h=32),
                                                      in1=dtv.unsqueeze(2).to_broadcast([128, 32, 64]), op=ALU.mult),
                     reads=[Bn["xc"]] + smB, writes=[Bn["xdt"]])
                for g in range(4):
                    S.op("pe", lambda e, g=g: e.matmul(psF[:, 0, g * 128:(g + 1) * 128], lhsT=bct[:, g, :], rhs=cct[:, g, :],
                                                       start=True, stop=True),
                         reads=[Bn["bct"], Bn["cct"]], writes=[B_ps[0]])
                msk = triU if d == 0 else triLo
                S.op("dve", lambda e: e.tensor_tensor(out=MTm, in0=psF[:, 0, :].rearrange("p (g l) -> p g l", g=4),
                                                      in1=msk[:].unsqueeze(1).to_broadcast([128, 4, 128]), op=ALU.mult),
                     reads=[B_ps[0], B_const], writes=[Bn["MTm"]])
                S.op("pool", lambda e: e.tensor_tensor(out=LA, in0=T1[:].unsqueeze(1).to_broadcast([128, 32, 128]),
                                                       in1=adt.unsqueeze(2).to_broadcast([128, 32, 128]), op=ALU.mult),
                     reads=smB + [B_const], writes=[Bn["LA"]])
                S.op("pe", lambda e: e.matmul(psF[:, 0, 0:32], lhsT=T2[:], rhs=adt, start=True, stop=True),
                     reads=smB + [B_const, Bn["MTm"]], writes=[B_ps[0]])
                S.op("pe", lambda e: e.matmul(psF[:, 0, 32:64], lhsT=Tds[:], rhs=adt, start=True, stop=True),
                     reads=smB + [B_const], writes=[B_ps[0]])
                S.op("pe", lambda e: e.matmul(psF[:, 0, 64:96], lhsT=onesF[:], rhs=adt, start=True, stop=True),
                     reads=smB + [B_const], writes=[B_ps[0]])
                S.op("act", lambda e: e.activation(out=sm2, in_=psF[:, 0, 0:96], func=AF.Exp),
                     reads=[B_ps[0]], writes=[Bn["sm2"]])
                EA, dsv, cdv = sm2[:, 0:32], sm2[:, 32:64], sm2[:, 64:96]
                for g in range(4):
                    for hh in range(8):
                        h = g * 8 + hh
                        bk = 1 + hh // 4
                        S.op("pe", lambda e, h=h, hh=hh, bk=bk: e.matmul(psF[:, bk, (hh % 4) * 128:(hh % 4 + 1) * 128],
                                                                          lhsT=LA[:, h, :], rhs=T2[:], start=True, stop=True),
                             reads=[Bn["LA"], B_const], writes=[B_ps[bk]])
                    S.op("act", lambda e: e.activation(out=Eg, in_=psF[:, 1:3, :].rearrange("p b n -> p (b n)"), func=AF.Exp),
                         reads=[B_ps[1], B_ps[2]], writes=[Bn["Eg"]])
                    S.op("dve", lambda e, g=g: e.tensor_tensor(out=MD, in0=Eg.rearrange("p (h l) -> p h l", h=8),
                                                               in1=MTm[:, g, :].unsqueeze(1).to_broadcast([128, 8, 128]),
                                                               op=ALU.mult),
                         reads=[Bn["Eg"], Bn["MTm"]], writes=[Bn["MD"]])
                    if dbg and d == 0 and c == 0 and g == 0:
                        S.dma("sp", lambda e: e.dma_start(out=DBGF[:, 0:256], in_=sm), reads=[Bn["sm"]])
                        S.dma("sp", lambda e: e.dma_start(out=DBGF[:, 256:352], in_=sm2), reads=[Bn["sm2"]])
                        S.dma("sp", lambda e: e.dma_start(out=DBGF[:, 512:1024], in_=MTm.rearrange("p g l -> p (g l)")), reads=[Bn["MTm"]])
                        S.dma("sp", lambda e: e.dma_start(out=DBGF[:, 1024:2048], in_=Eg), reads=[Bn["Eg"]])
                        S.dma("sp", lambda e: e.dma_start(out=DBGB[:, 0:2048], in_=xdt.rearrange("p h d -> p (h d)")), reads=[Bn["xdt"]])
                        S.dma("sp", lambda e: e.dma_start(out=DBGB[:, 2048:3072], in_=MD.rearrange("p h l -> p (h l)")), reads=[Bn["MD"]])
                    for hh in range(8):
                        h = g * 8 + hh
                        S.op("pe", lambda e, h=h, hh=hh: e.matmul(psF[:, 3, hh * 64:(hh + 1) * 64], lhsT=MD[:, hh, :],
                                                                  rhs=xdt[:, h, :], start=True, stop=True),
                             reads=[Bn["MD"], Bn["xdt"]], writes=[B_ps[3]])
                    S.op("pe", lambda e, g=g: e.matmul(psF[:, 4, :], lhsT=cct[:, g, :], rhs=Hb[:, g * 512:(g + 1) * 512],
                                                       start=True, stop=True),
                         reads=[Bn["cct"], Bn["Hb"]], writes=[B_ps[4]])
                    S.op("dve", lambda e, g=g: e.tensor_tensor(out=tmpf.rearrange("p (h d) -> p h d", h=8),
                                                               in0=psF[:, 4, :].rearrange("p (h d) -> p h d", h=8),
                                                               in1=EA[:, g * 8:(g + 1) * 8].unsqueeze(2).to_broadcast([128, 8, 64]),
                                                               op=ALU.mult),
                         reads=[B_ps[4], Bn["sm2"]], writes=[Bn["tmpf"]])
                    S.op("dve", lambda e, g=g: e.tensor_tensor(out=yacc[:, g * 512:(g + 1) * 512], in0=psF[:, 3, :], in1=tmpf,
                                                               op=ALU.add),
                         reads=[B_ps[3], Bn["tmpf"]], writes=[Bn["yacc"]])
                    S.op("pool", lambda e, g=g: e.tensor_tensor(out=xdd, in0=xdt[:, g * 8:(g + 1) * 8, :],
                                                                in1=dsv[:, g * 8:(g + 1) * 8].unsqueeze(2).to_broadcast([128, 8, 64]),
                                                                op=ALU.mult),
                         reads=[Bn["xdt"], Bn["sm2"]], writes=[Bn["xdd"]])
                    S.op("pe", lambda e, g=g: e.matmul(psF[:, 5, :], lhsT=bc[:, g * 128:(g + 1) * 128],
                                                       rhs=xdd.rearrange("p h d -> p (h d)"), start=True, stop=True),
                         reads=[Bn["bc"], Bn["xdd"]], writes=[B_ps[5]])
                    Hg = Hs[:, g * 512:(g + 1) * 512]
                    S.op("dve", lambda e, g=g, Hg=Hg: e.tensor_tensor(out=Hg.rearrange("p (h d) -> p h d", h=8),
                                                                      in0=Hg.rearrange("p (h d) -> p h d", h=8),
                                                                      in1=cdv[:, g * 8:(g + 1) * 8].unsqueeze(2).to_broadcast([128, 8, 64]),
                                                                      op=ALU.mult),
                         reads=[Bn["sm2"], Bn["Hs"]], writes=[Bn["Hs"]])
                    S.op("dve", lambda e, Hg=Hg: e.tensor_tensor(out=Hg, in0=psF[:, 5, :], in1=Hg, op=ALU.add),
                         reads=[B_ps[5], Bn["Hs"]], writes=[Bn["Hs"]])
                    S.op("act", lambda e, g=g, Hg=Hg: e.copy(out=Hb[:, g * 512:(g + 1) * 512], in_=Hg),
                         reads=[Bn["Hs"]], writes=[Bn["Hb"]])
                if d == 0:
                    S.dma("sp", lambda e, r0=r0, r1=r1: e.dma_start(out=YF[r0:r1, :], in_=yacc), reads=[Bn["yacc"]])
                else:
                    S.op("pool", lambda e: e.tensor_tensor(out=yacc, in0=yacc, in1=yf, op=ALU.add),
                         reads=[Bn["yf"], Bn["yacc"]], writes=[Bn["yacc"]])
                    S.op("dve", lambda e: e.tensor_tensor(out=yf, in0=xc, in1=dskip_s[:], op=ALU.mult),
                         reads=[Bn["xc"], B_const, Bn["yacc"]], writes=[Bn["yf"]])
                    S.op("pool", lambda e: e.tensor_tensor(out=yacc, in0=yacc, in1=yf, op=ALU.add),
                         reads=[Bn["yf"], Bn["yacc"]], writes=[Bn["yacc"]])
                    S.op("dve", lambda e: e.tensor_tensor(out=yz, in0=yacc, in1=zs, op=ALU.mult),
                         reads=[Bn["yacc"], Bn["zs"]], writes=[Bn["yz"]])
                    S.op("act", lambda e: e.activation(out=yf, in_=yz, func=AF.Square, scale=float(1.0 / np.sqrt(2048.0)),
                                                       accum_out=ss[:, 0:1]),
                         reads=[Bn["yz"]], writes=[Bn["yf"], B_ss])
                    S.op("dve", lambda e: e.tensor_scalar_add(out=ss[:, 0:1], in0=ss[:, 0:1], scalar1=1e-6),
                         reads=[B_ss], writes=[B_ss])
                    S.op("act", lambda e: e.activation(out=ss[:, 0:1], in_=ss[:, 0:1], func=AF.Ln), reads=[B_ss], writes=[B_ss])
                    S.op("act", lambda e: e.activation(out=ss[:, 0:1], in_=ss[:, 0:1], func=AF.Exp, scale=-0.5),
                         reads=[B_ss], writes=[B_ss])
                    S.op("dve", lambda e: e.scalar_tensor_tensor(out=yn, in0=yz, scalar=ss[:, 0:1], in1=sng_s[:],
                                                                 op0=ALU.mult, op1=ALU.mult),
                         reads=[Bn["yz"], B_ss, B_const], writes=[Bn["yn"]])
                    S.dma("sp", lambda e, r0=r0, r1=r1: e.dma_start(out=YTOK[r0:r1, 0:2048], in_=yn), reads=[Bn["yn"]])
        sweep(0)
        sweep(1)
        S.barrier()

        qkg = sb("qkg", [128, 2], F32)
        S.dma("sp", lambda e: e.dma_start(out=qkg[:, 0:1], in_=qg_d[:, :]), writes=[B_const])
        S.dma("sp", lambda e: e.dma_start(out=qkg[:, 1:2], in_=kg_d[:, :]), writes=[B_const])
        BD = sb("BD", [128, 128], BF16)
        S.op("pool", lambda e: e.memset(BD[:], 0.0), writes=[B_const])
        S.op("pool", lambda e: e.memset(BD[0:64, 0:64], 1.0), reads=[B_const], writes=[B_const])
        S.op("pool", lambda e: e.memset(BD[64:128, 64:128], 1.0), reads=[B_const], writes=[B_const])
        qraw = wbuf[:, 0:512]
        qsq = wbuf[:, 512:1024]
        qno = wbuf[:, 1024:1536]
        rstd_q = fbuf[:, 0:512]
        B_qraw, B_qsq, B_qno, B_rq = Buf(), Buf(), Buf(), Buf()
        for b in range(L // 512):
            for ct in range(16):
                c0, c1 = b * 512, (b + 1) * 512
                S.dma("sp", lambda e, ct=ct, c0=c0, c1=c1: e.dma_start(out=qraw, in_=PF[(24 + ct) * 128:(25 + ct) * 128, c0:c1]),
                      writes=[B_qraw])
                S.op("dve", lambda e: e.tensor_tensor(out=qsq, in0=qraw, in1=qraw, op=ALU.mult), reads=[B_qraw], writes=[B_qsq])
                bank = ct % 2
                S.op("pe", lambda e, bank=bank: e.matmul(psF[:, bank, :], lhsT=BD[:], rhs=qsq, start=True, stop=True),
                     reads=[B_qsq, B_const], writes=[B_ps[bank]])
                S.op("act", lambda e, bank=bank: e.activation(out=rstd_q, in_=psF[:, bank, :], func=AF.Ln, scale=1.0 / 64, bias=1e-6),
                     reads=[B_ps[bank]], writes=[B_rq])
                S.op("act", lambda e: e.activation(out=rstd_q, in_=rstd_q, func=AF.Exp, scale=-0.5), reads=[B_rq], writes=[B_rq])
                gi = 0 if ct < 8 else 1
                S.op("dve", lambda e, gi=gi: e.scalar_tensor_tensor(out=qno, in0=qraw, scalar=qkg[:, gi:gi + 1], in1=rstd_q,
                                                                   op0=ALU.mult, op1=ALU.mult),
                     reads=[B_qraw, B_rq, B_const], writes=[B_qno])
                dstT = QN if ct < 8 else KN
                cc = ct % 8
                S.dma("sp", lambda e, dstT=dstT, cc=cc, c0=c0, c1=c1: e.dma_start(out=dstT[cc * 128:(cc + 1) * 128, c0:c1], in_=qno),
                      reads=[B_qno])
        S.barrier()
        NE = 5 * 16 * 128
        Egen = wbuf[:, 0:NE].rearrange("p (k h q) -> p k h q", k=5, h=16)
        Espec = wbuf[:, NE:2 * NE].rearrange("p (k h q) -> p k h q", k=5, h=16)
        o0 = 2 * NE
        qn = wbuf[:, o0:o0 + 1024].rearrange("p (c t) -> p c t", c=8)
        kn = wbuf[:, o0 + 1024:o0 + 1024 + 5120].rearrange("p (c t) -> p c t", c=8)
        o1 = o0 + 6144
        vaug = wbuf[:, o1:o1 + 5200].rearrange("p (k h d) -> p k h d", k=5, h=16)
        o2 = o1 + 5200
        pbuf = [wbuf[:, o2 + i * 512:o2 + (i + 1) * 512] for i in range(2)]
        pm = [[wbuf[:, o2 + 1024 + (i * 5 + k) * 512:o2 + 1024 + (i * 5 + k + 1) * 512] for k in range(5)] for i in range(2)]
        yna = wbuf[:, o2 + 6144:o2 + 7168]
        ynaT = wbuf[:, o2 + 7168:o2 + 8192].rearrange("p (c t) -> p c t", c=8)
        tstage = fbuf[:, 0:NE]
        den = fbuf[:, NE:NE + 16]
        B_E, B_Es, B_qn, B_kn, B_va, B_yna, B_ynaT, B_ts, B_den = (Buf() for _ in range(9))
        B_pb = [Buf(), Buf()]
        B_pm = [Buf(), Buf()]

        def load_table(pat, dst, Bdst):
            S.dma("sp", lambda e: e.dma_start(out=tstage, in_=bt_d[pat]), writes=[B_ts])
            S.op("act", lambda e: e.activation(out=dst.rearrange("p k h q -> p (k h q)"), in_=tstage, func=AF.Exp),
                 reads=[B_ts], writes=[Bdst])
        load_table(2, Egen, B_E)
        S.op("pool", lambda e: e.memset(vaug.rearrange("p k h d -> p (k h d)"), 1.0), writes=[B_va])
        npp = 0
        for t in range(NT):
            kt0 = min(max(t - 2, 0), NT - 5)
            rel = t - kt0
            if rel == 2:
                Et, BEt = Egen, B_E
            else:
                load_table(rel, Espec, B_Es)
                Et, BEt = Espec, B_Es
            r0, r1 = t * 128, (t + 1) * 128
            k0, k1 = kt0 * 128, (kt0 + 5) * 128
            S.dma("sp", lambda e, r0=r0, r1=r1: e.dma_start(out=qn, in_=QN[:, r0:r1].rearrange("(c p) t -> p c t", p=128)),
                  writes=[B_qn])
            S.dma("sp", lambda e, k0=k0, k1=k1: e.dma_start(out=kn, in_=KN[:, k0:k1].rearrange("(c p) t -> p c t", p=128)),
                  writes=[B_kn])
            for kb in range(5):
                S.dma("sp", lambda e, kb=kb, k0=k0: e.dma_start(
                    out=vaug[:, kb, :, 0:64], in_=PTv[k0 + kb * 128:k0 + (kb + 1) * 128, :].rearrange("p (h d) -> p h d", h=16)),
                    writes=[B_va])
            for hq in range(4):
                jq = hq % 2
                for kb in range(5):
                    j = npp % 2
                    npp += 1
                    bank = j
                    for hh in range(4):
                        h = hq * 4 + hh
                        ct, po = h // 2, (h % 2) * 64
                        S.op("pe", lambda e, ct=ct, po=po, hh=hh, kb=kb, bank=bank: e.matmul(
                            psF[:, bank, hh * 128:(hh + 1) * 128], lhsT=kn[po:po + 64, ct, kb * 128:(kb + 1) * 128],
                            rhs=qn[po:po + 64, ct, :], start=True, stop=True),
                            reads=[B_kn, B_qn], writes=[B_ps[bank]])
                    S.op("act", lambda e, bank=bank, j=j: e.activation(out=pbuf[j], in_=psF[:, bank, :], func=AF.Exp, scale=0.125),
                         reads=[B_ps[bank]], writes=[B_pb[j]])
                    S.op("pool" if j == 0 else "dve", lambda e, j=j, kb=kb, hq=hq, Et=Et, jq=jq: e.tensor_tensor(
                        out=pm[jq][kb], in0=pbuf[j], in1=Et[:, kb, hq * 4:(hq + 1) * 4, :].rearrange("p h q -> p (h q)"), op=ALU.mult),
                        reads=[B_pb[j], BEt], writes=[B_pm[jq]])
                for hh in range(4):
                    h = hq * 4 + hh
                    ob, oc = 2 + h // 7, (h % 7) * 65
                    for kb in range(5):
                        S.op("pe", lambda e, h=h, hh=hh, kb=kb, jq=jq, ob=ob, oc=oc: e.matmul(
                            psF[:, ob, oc:oc + 65], lhsT=pm[jq][kb][:, hh * 128:(hh + 1) * 128], rhs=vaug[:, kb, h, :],
                            start=(kb == 0), stop=(kb == 4)),
                            reads=[B_pm[jq], B_va], writes=[B_ps[ob]])
            for ob, h0, nh in ((2, 0, 7), (3, 7, 7), (4, 14, 2)):
                pv = psF[:, ob, 0:nh * 65].rearrange("p (h d) -> p h d", h=nh)
                S.op("dve", lambda e, pv=pv, h0=h0, nh=nh: e.tensor_copy(out=den[:, h0:h0 + nh].unsqueeze(2), in_=pv[:, :, 64:65]),
                     reads=[B_ps[ob]], writes=[B_den])
            S.op("dve", lambda e: e.reciprocal(out=den, in_=den), reads=[B_den], writes=[B_den])
            for ob, h0, nh in ((2, 0, 7), (3, 7, 7), (4, 14, 2)):
                pv = psF[:, ob, 0:nh * 65].rearrange("p (h d) -> p h d", h=nh)
                S.op("dve", lambda e, pv=pv, h0=h0, nh=nh: e.tensor_tensor(
                    out=yna[:, h0 * 64:(h0 + nh) * 64].rearrange("p (h d) -> p h d", h=nh), in0=pv[:, :, 0:64],
                    in1=den[:, h0:h0 + nh].unsqueeze(2).to_broadcast([128, nh, 64]), op=ALU.mult),
                    reads=[B_ps[ob], B_den], writes=[B_yna])
            S.dma("sp", lambda e, r0=r0, r1=r1: e.dma_start(out=YTOK[r0:r1, 2048:3072], in_=yna), reads=[B_yna])
        S.barrier()

        WO = wbuf[:, 0:24 * 1024].rearrange("p (c n) -> p c n", c=24)
        for ct in range(24):
            j = ct % 2
            S.dma("sp", lambda e, ct=ct, j=j: e.dma_start(out=wst[j], in_=wo_d[:, ct, :]), writes=[B_wst[j]])
            S.op("dve" if j == 0 else "pool", lambda e, ct=ct, j=j: e.tensor_copy(out=WO[:, ct, :], in_=wst[j]),
                 reads=[B_wst[j]], writes=[B_w])
        _o = [0]

        def sv(n, shape=None, dt=None):
            v = scr[:, _o[0]:_o[0] + n]
            _o[0] += n
            if dt is not None:
                v = v.bitcast(dt)
            return v
        gffn_s = sv(1024)
        skT_s = sv(256).rearrange("p (k n) -> p k n", k=2)
        iota_i = sv(256, dt=I32)
        iota_f = sv(256)
        S.dma("sp", lambda e: e.dma_start(out=gffn_s, in_=gffn_d[:, :]), writes=[B_const])
        S.dma("sp", lambda e: e.dma_start(out=skT_s, in_=skT_d[:, :, :]), writes=[B_const])
        S.op("pool", lambda e: e.iota(iota_i, pattern=[[1, 256]], base=0, channel_multiplier=0), writes=[B_const])
        S.op("pool", lambda e: e.tensor_copy(out=iota_f, in_=iota_i), reads=[B_const], writes=[B_const])
        yT_s = wbuf[:, 24576:24576 + 3072].rearrange("p (c t) -> p c t", c=24)
        g12 = wbuf[:, 27648:27648 + 2048]
        yrow = wbuf[:, 29696:29696 + 3072]
        gbb = [wbuf[:, 32768 + i * 1024:32768 + (i + 1) * 1024] for i in range(4)]
        cst = [wbuf[:, 36864 + i * 1024:36864 + (i + 1) * 1024] for i in range(2)]
        B_cst = [Buf(), Buf()]
        B_yrow = Buf()
        rowidx_s = sv(NSH, dt=I32)
        S.dma("sp", lambda e: e.dma_start(out=rowidx_s, in_=rowidx_d[:, :]), writes=[B_const])
        nce = 0
        for src, dstE in ((eu_d, EUb), (ev_d, EVb)):
            for rt in range(NEXP // 128):
                j = nce % 2
                nce += 1
                S.dma("sp", lambda e, src=src, rt=rt, j=j: e.dma_start(out=wst[j], in_=src[rt * 128:(rt + 1) * 128, :]), writes=[B_wst[j]])
                if nce % 3 == 0:
                    S.op("act", lambda e, j=j: e.copy(out=cst[j], in_=wst[j]), reads=[B_wst[j]], writes=[B_cst[j]])
                elif nce % 3 == 1:
                    S.op("dve", lambda e, j=j: e.tensor_copy(out=cst[j], in_=wst[j]), reads=[B_wst[j]], writes=[B_cst[j]])
                else:
                    S.op("pool", lambda e, j=j: e.tensor_copy(out=cst[j], in_=wst[j]), reads=[B_wst[j]], writes=[B_cst[j]])
                S.dma("sp", lambda e, dstE=dstE, rt=rt, j=j: e.dma_start(out=dstE[rt * 128:(rt + 1) * 128, :], in_=cst[j]), reads=[B_cst[j]])
        S.barrier()
        xin_f = fbuf[:, 0:1024]
        xm = fbuf[:, 1024:2048]
        tmpo = fbuf[:, 2048:3072]
        tt = fbuf[:, 3072:4096]
        tT_s = fbuf[:, 4096:5120]
        qT_s = fbuf[:, 5120:7168]
        s_s = fbuf[:, 7168:9216].rearrange("p (k n) -> p k n", k=16)
        gbuf = [fbuf[:, 9216 + i * 1024:9216 + (i + 1) * 1024] for i in range(4)]
        junk = fbuf[:, 13312:14336]
        wpst = [fbuf[:, 14336 + i * 1024:14336 + (i + 1) * 1024] for i in range(2)]
        top = sv(256).rearrange("p (a b) -> p a b", a=16)
        idxu = sv(256, dt=U32).rearrange("p (a b) -> p a b", a=16)
        idxf = sv(256).rearrange("p (a b) -> p a b", a=16)
        work = sv(128)
        cand = sv(256).rearrange("p (a b) -> p a b", a=16)
        cidx = sv(256).rearrange("p (a b) -> p a b", a=16)
        work2 = sv(256)
        best = sv(16)
        posu = sv(16, dt=U32)
        posf = sv(16)
        ef = sv(128)
        ei = sv(128, dt=I32)
        gts = sv(128)
        actv = sv(128)
        wgt = sv(128)
        smz = sv(4)
        Bq = {n: Buf(n) for n in "yT g12 xin xm tmpo tt tT qT s junk top idxu idxf work cand cidx work2 best posu posf ef ei gts actv wgt smz".split()}
        B_gb = [Buf() for _ in range(4)]
        B_wp = [Buf(), Buf()]
        tt2 = fbuf[:, 9216:10240]
        xm2 = fbuf[:, 10240:11264]
        junk2 = fbuf[:, 11264:12288]
        gts2 = sv(128)
        ei2 = sv(128, dt=I32)
        gring = gbb + cst + [wbuf[:, 38912:39936], wbuf[:, 39936:40960]]
        B_gr = [Buf() for _ in gring]
        NG = len(gring)
        for _n in ("tt", "xm", "ei", "gts"):
            Bq[_n + "0"] = Bq[_n]
            Bq[_n + "1"] = Buf(_n + "1")
        Bq["junk2"] = Buf("junk2")
        PARV = {"tt": (tt, tt2), "xm": (xm, xm2), "ei": (ei, ei2), "gts": (gts, gts2)}
        ngbc = [0]
        dgs = [sv(64, dt=BF16) for _ in range(4)]
        B_dg = [Buf() for _ in range(4)]

        def stage1(i):
            par = i % 2
            tt_p, xm_p, ei_p, gts_p = PARV["tt"][par], PARV["xm"][par], PARV["ei"][par], PARV["gts"][par]
            r0, r1 = i * 128, (i + 1) * 128
            ridx = rowidx_s[:, i:i + 1]
            S.dma("pool", lambda e, ridx=ridx: e.indirect_dma_start(out=yrow, out_offset=None, in_=YTOK[:, :],
                                                                    in_offset=bass.IndirectOffsetOnAxis(ap=ridx, axis=0)),
                  reads=[B_const], writes=[B_yrow])
            S.dma("pool", lambda e, ridx=ridx: e.indirect_dma_start(out=g12, out_offset=None, in_=PTg[:, :],
                                                                    in_offset=bass.IndirectOffsetOnAxis(ap=ridx, axis=0)),
                  reads=[B_const], writes=[Bq["g12"]])
            S.dma("pool", lambda e, ridx=ridx: e.indirect_dma_start(out=xin_f, out_offset=None, in_=x_d[:, :],
                                                                    in_offset=bass.IndirectOffsetOnAxis(ap=ridx, axis=0)),
                  reads=[B_const], writes=[Bq["xin"]])
            for rnd, (c0, nct) in enumerate(((0, 16), (16, 8))):
                for cc in range(nct):
                    ct = c0 + cc
                    S.op("pe", lambda e, ct=ct, cc=cc: e.transpose(out=psB[:, cc // 8, (cc % 8) * 128:(cc % 8 + 1) * 128],
                                                                  in_=yrow[:, ct * 128:(ct + 1) * 128], identity=identB[:]),
                         reads=[B_yrow, B_const], writes=[B_psb[cc // 8]])
                S.op("act", lambda e, c0=c0: e.copy(out=yT_s[:, c0:c0 + 8, :].rearrange("p c t -> p (c t)"), in_=psB[:, 0, :]),
                     reads=[B_psb[0]], writes=[Bq["yT"]])
                if nct == 16:
                    S.op("dve", lambda e: e.tensor_copy(out=yT_s[:, 8:16, :].rearrange("p c t -> p (c t)"), in_=psB[:, 1, :]),
                         reads=[B_psb[1]], writes=[Bq["yT"]])
            for half in range(2):
                yield
                hs = slice(half * 512, (half + 1) * 512)
                hs2 = slice(1024 + half * 512, 1024 + (half + 1) * 512)
                for ct in range(16):
                    S.op("pe", lambda e, ct=ct, hs=hs: e.matmul(psF[:, 0, :], lhsT=yT_s[:, ct, :], rhs=WO[:, ct, hs],
                                                                start=(ct == 0), stop=(ct == 15)),
                         reads=[Bq["yT"], B_w], writes=[B_ps[0]])
                for ct in range(16, 24):
                    S.op("pe", lambda e, ct=ct, hs=hs: e.matmul(psF[:, 1, :], lhsT=yT_s[:, ct, :], rhs=WO[:, ct, hs],
                                                                start=(ct == 16), stop=(ct == 23)),
                         reads=[Bq["yT"], B_w], writes=[B_ps[1]])
                S.op("dve", lambda e, hs=hs: e.tensor_tensor(out=tmpo[:, hs], in0=psF[:, 0, :], in1=g12[:, hs], op=ALU.mult),
                     reads=[B_ps[0], Bq["g12"]], writes=[Bq["tmpo"]])
                S.op("dve", lambda e, hs=hs, hs2=hs2: e.tensor_tensor(out=junk[:, hs], in0=psF[:, 1, :], in1=g12[:, hs2], op=ALU.mult),
                     reads=[B_ps[1], Bq["g12"]], writes=[Bq["junk"]])
                S.op("pool", lambda e, hs=hs: e.tensor_tensor(out=tmpo[:, hs], in0=tmpo[:, hs], in1=junk[:, hs], op=ALU.add),
                     reads=[Bq["junk"], Bq["tmpo"]], writes=[Bq["tmpo"]])
                S.op("pool", lambda e, hs=hs: e.tensor_tensor(out=xm_p[:, hs], in0=tmpo[:, hs], in1=xin_f[:, hs], op=ALU.add),
                     reads=[Bq["tmpo"], Bq["xin"]], writes=[Bq["xm%d" % par]])
            if dbg:
                S.dma("sp", lambda e, r0=r0, r1=r1: e.dma_start(out=XM[r0:r1, :], in_=xm_p), reads=[Bq["xm%d" % par]])
            S.op("act", lambda e: e.activation(out=junk, in_=xm_p, func=AF.Square, scale=1.0 / 32, accum_out=ss[:, 0:1]),
                 reads=[Bq["xm%d" % par]], writes=[Bq["junk"], B_ss])
            S.op("dve", lambda e: e.tensor_scalar_add(out=ss[:, 0:1], in0=ss[:, 0:1], scalar1=1e-6), reads=[B_ss], writes=[B_ss])
            S.op("act", lambda e: e.activation(out=ss[:, 0:1], in_=ss[:, 0:1], func=AF.Ln), reads=[B_ss], writes=[B_ss])
            S.op("act", lambda e: e.activation(out=ss[:, 0:1], in_=ss[:, 0:1], func=AF.Exp, scale=-0.5), reads=[B_ss], writes=[B_ss])
            S.op("dve", lambda e: e.scalar_tensor_tensor(out=tt_p, in0=xm_p, scalar=ss[:, 0:1], in1=gffn_s, op0=ALU.mult, op1=ALU.mult),
                 reads=[Bq["xm%d" % par], B_ss, B_const], writes=[Bq["tt%d" % par]])
            for k in range(8):
                S.op("pe", lambda e, k=k: e.transpose(out=psF[:, 2 + k // 4, (k % 4) * 128:(k % 4 + 1) * 128],
                                                      in_=tt_p[:, k * 128:(k + 1) * 128], identity=identF[:]),
                     reads=[Bq["tt%d" % par], B_const], writes=[B_ps[2 + k // 4]])
            S.op("act", lambda e: e.copy(out=tT_s[:, 0:512], in_=psF[:, 2, :]), reads=[B_ps[2]], writes=[Bq["tT"]])
            S.op("dve", lambda e: e.tensor_copy(out=tT_s[:, 512:1024], in_=psF[:, 3, :]), reads=[B_ps[3]], writes=[Bq["tT"]])
            for cq in range(16):
                yield
                j = cq % 2
                bank = 2 + (cq // 4) % 2
                S.dma("sp", lambda e, cq=cq, j=j: e.dma_start(out=wpst[j], in_=wpq_d[cq].rearrange("p k n -> p (k n)")), writes=[B_wp[j]])
                for k in range(8):
                    S.op("pe", lambda e, cq=cq, k=k, j=j, bank=bank: e.matmul(
                        psF[:, bank, (cq % 4) * 128:(cq % 4 + 1) * 128], lhsT=wpst[j][:, k * 128:(k + 1) * 128],
                        rhs=tT_s[:, k * 128:(k + 1) * 128], start=(k == 0), stop=(k == 7)),
                        reads=[B_wp[j], Bq["tT"]], writes=[B_ps[bank]])
                if cq % 4 == 3:
                    q0 = (cq // 4) * 512
                    S.op("act" if (cq // 4) % 2 == 0 else "dve",
                         (lambda e, bank=bank, q0=q0: e.copy(out=qT_s[:, q0:q0 + 512], in_=psF[:, bank, :])) if (cq // 4) % 2 == 0 else
                         (lambda e, bank=bank, q0=q0: e.tensor_copy(out=qT_s[:, q0:q0 + 512], in_=psF[:, bank, :])),
                         reads=[B_ps[bank]], writes=[Bq["qT"]])
            for hk in range(16):
                yield
                bank = 2 + (hk // 4) % 2
                S.op("pe", lambda e, hk=hk, bank=bank: e.matmul(psF[:, bank, (hk % 4) * 128:(hk % 4 + 1) * 128],
                                                                lhsT=qT_s[:, hk * 128:(hk + 1) * 128], rhs=skT_s[:, hk % 2, :],
                                                                start=True, stop=True),
                     reads=[Bq["qT"], B_const], writes=[B_ps[bank]])
                if hk % 4 == 3:
                    k0 = hk - 3
                    S.op("act", lambda e, bank=bank, k0=k0: e.copy(out=s_s[:, k0:k0 + 4, :].rearrange("p k n -> p (k n)"), in_=psF[:, bank, :]),
                         reads=[B_ps[bank]], writes=[Bq["s"]])
            for hk in range(16):
                yield
                S.op("dve", lambda e, hk=hk: e.max(out=top[:, hk, 0:8], in_=s_s[:, hk, :]), reads=[Bq["s"]], writes=[Bq["top"]])
                S.op("dve", lambda e, hk=hk: e.max_index(out=idxu[:, hk, 0:8], in_max=top[:, hk, 0:8], in_values=s_s[:, hk, :]),
                     reads=[Bq["s"], Bq["top"]], writes=[Bq["idxu"]])
                S.op("dve", lambda e, hk=hk: e.match_replace(out=work, in_to_replace=top[:, hk, 0:8], in_values=s_s[:, hk, :],
                                                             imm_value=-1e30),
                     reads=[Bq["s"], Bq["top"]], writes=[Bq["work"]])
                S.op("dve", lambda e, hk=hk: e.max(out=top[:, hk, 8:16], in_=work), reads=[Bq["work"]], writes=[Bq["top"]])
                S.op("dve", lambda e, hk=hk: e.max_index(out=idxu[:, hk, 8:16], in_max=top[:, hk, 8:16], in_values=work),
                     reads=[Bq["work"], Bq["top"]], writes=[Bq["idxu"]])
            S.op("dve", lambda e: e.tensor_copy(out=idxf, in_=idxu), reads=[Bq["idxu"]], writes=[Bq["idxf"]])
            for h in range(8):
                yield
                S.op("dve", lambda e, h=h: e.tensor_tensor(out=cand, in0=top[:, 2 * h, :].unsqueeze(2).to_broadcast([128, 16, 16]),
                                                           in1=top[:, 2 * h + 1, :].unsqueeze(1).to_broadcast([128, 16, 16]), op=ALU.add),
                     reads=[Bq["top"]], writes=[Bq["cand"]])
                S.op("pool", lambda e, h=h: e.tensor_scalar_mul(out=posf, in0=idxf[:, 2 * h, :], scalar1=128.0),
                     reads=[Bq["idxf"]], writes=[Bq["posf"]])
                S.op("pool", lambda e, h=h: e.tensor_tensor(out=cidx, in0=posf.unsqueeze(2).to_broadcast([128, 16, 16]),
                                                            in1=idxf[:, 2 * h + 1, :].unsqueeze(1).to_broadcast([128, 16, 16]), op=ALU.add),
                     reads=[Bq["idxf"], Bq["posf"]], writes=[Bq["cidx"]])
                candf = cand.rearrange("p a b -> p (a b)")
                cidxf = cidx.rearrange("p a b -> p (a b)")
                S.op("dve", lambda e: e.max(out=best[:, 0:8], in_=candf), reads=[Bq["cand"]], writes=[Bq["best"]])
                S.op("dve", lambda e: e.max_index(out=posu[:, 0:8], in_max=best[:, 0:8], in_values=candf),
                     reads=[Bq["cand"], Bq["best"]], writes=[Bq["posu"]])
                S.op("dve", lambda e: e.match_replace(out=work2, in_to_replace=best[:, 0:8], in_values=candf, imm_value=-1e30),
                     reads=[Bq["cand"], Bq["best"]], writes=[Bq["work2"]])
                S.op("dve", lambda e: e.max(out=best[:, 8:16], in_=work2), reads=[Bq["work2"]], writes=[Bq["best"]])
                S.op("dve", lambda e: e.max_index(out=posu[:, 8:16], in_max=best[:, 8:16], in_values=work2),
                     reads=[Bq["work2"], Bq["best"]], writes=[Bq["posu"]])
                S.op("dve", lambda e: e.tensor_copy(out=posf, in_=posu), reads=[Bq["posu"], Bq["cidx"]], writes=[Bq["posf"]])
                for k in range(16):
                    yield
                    S.op("dve", lambda e, h=h, k=k: e.scalar_tensor_tensor(out=work2, in0=iota_f, scalar=posf[:, k:k + 1],
                                                                         in1=cidxf, op0=ALU.is_equal, op1=ALU.mult,
                                                                         accum_out=ef[:, h * 16 + k:h * 16 + k + 1]),
                         reads=[Bq["posf"], Bq["cidx"], B_const, Bq["work2"]], writes=[Bq["work2"], Bq["ef"]])
                S.op("dve", lambda e: e.tensor_scalar_mul(out=smz[:, 0:1], in0=best[:, 0:1], scalar1=-1.0),
                     reads=[Bq["best"]], writes=[Bq["smz"]])
                S.op("act", lambda e, h=h: e.activation(out=gts_p[:, h * 16:(h + 1) * 16], in_=best, func=AF.Exp, bias=smz[:, 0:1],
                                                        accum_out=smz[:, 1:2]),
                     reads=[Bq["best"], Bq["smz"]], writes=[Bq["gts%d" % par], Bq["smz"]])
                S.op("dve", lambda e: e.reciprocal(out=smz[:, 2:3], in_=smz[:, 1:2]), reads=[Bq["smz"]], writes=[Bq["smz"]])
                S.op("dve", lambda e, h=h: e.tensor_scalar(out=gts_p[:, h * 16:(h + 1) * 16], in0=gts_p[:, h * 16:(h + 1) * 16],
                                                           scalar1=smz[:, 2:3], scalar2=None, op0=ALU.mult),
                     reads=[Bq["gts%d" % par], Bq["smz"]], writes=[Bq["gts%d" % par]])
            S.op("dve", lambda e: e.tensor_copy(out=ei_p, in_=ef), reads=[Bq["ef"]], writes=[Bq["ei%d" % par]])
            yield

        def stage2(i):
            par = i % 2
            r0, r1 = i * 128, (i + 1) * 128
            tt_p, xm_p, ei_p, gts_p = PARV["tt"][par], PARV["xm"][par], PARV["ei"][par], PARV["gts"][par]
            for m in range(128):
                j = ngbc[0] % NG
                ngbc[0] += 1
                yield
                S.dma("pool", lambda e, m=m, j=j: e.indirect_dma_start(
                    out=gring[j], out_offset=None, in_=EUb[:, :],
                    in_offset=bass.IndirectOffsetOnAxis(ap=ei_p[:, m:m + 1], axis=0)),
                    reads=[Bq["ei%d" % par]], writes=[B_gr[j]])
                S.op("dve", lambda e, m=m, j=j: e.scalar_tensor_tensor(out=junk2, in0=gring[j], scalar=1.0, in1=tt_p, op0=ALU.mult,
                                                                      op1=ALU.mult, accum_out=actv[:, m:m + 1]),
                     reads=[B_gr[j], Bq["tt%d" % par]], writes=[Bq["junk2"], Bq["actv"]])
            S.op("act", lambda e: e.activation(out=wgt, in_=actv, func=AF.Gelu), reads=[Bq["actv"]], writes=[Bq["wgt"]])
            S.op("dve", lambda e: e.tensor_tensor(out=wgt, in0=wgt, in1=gts_p, op=ALU.mult),
                 reads=[Bq["wgt"], Bq["gts%d" % par]], writes=[Bq["wgt"]])
            for m in range(128):
                j = ngbc[0] % NG
                ngbc[0] += 1
                jd = m % 4
                yield
                S.dma("pool", lambda e, m=m, j=j: e.indirect_dma_start(
                    out=gring[j], out_offset=None, in_=EVb[:, :],
                    in_offset=bass.IndirectOffsetOnAxis(ap=ei_p[:, m:m + 1], axis=0)),
                    reads=[Bq["ei%d" % par]], writes=[B_gr[j]])
                S.op("act", lambda e, m=m, jd=jd: e.activation(out=dgs[jd], in_=identF[:], func=AF.Copy, scale=wgt[:, m:m + 1]),
                     reads=[Bq["wgt"], B_const], writes=[B_dg[jd]])
                for half in range(2):
                    S.op("pe", lambda e, m=m, j=j, jd=jd, half=half: e.matmul(
                        psF[:, 4 + half, :], lhsT=dgs[jd], rhs=gring[j][:, half * 512:(half + 1) * 512],
                        start=(m == 0), stop=(m == 127)),
                        reads=[B_dg[jd], B_gr[j]], writes=[B_ps[4 + half]])
            S.op("dve", lambda e: e.tensor_tensor(out=xm_p[:, 0:512], in0=psF[:, 4, :], in1=xm_p[:, 0:512], op=ALU.add),
                 reads=[B_ps[4], Bq["xm%d" % par]], writes=[Bq["xm%d" % par]])
            S.op("dve", lambda e: e.tensor_tensor(out=xm_p[:, 512:1024], in0=psF[:, 5, :], in1=xm_p[:, 512:1024], op=ALU.add),
                 reads=[B_ps[5], Bq["xm%d" % par]], writes=[Bq["xm%d" % par]])
            S.dma("sp", lambda e, r0=r0, r1=r1: e.dma_start(out=out_d[r0:r1, :], in_=xm_p), reads=[Bq["xm%d" % par]])
            yield

        g1 = stage1(0)
        for _ in g1:
            pass
        for i in range(NSH):
            g2 = stage2(i)
            g1 = stage1(i + 1) if i + 1 < NSH else None
            while True:
                alive = False
                for _k in range(3):
                    try:
                        next(g2)
                        alive = True
                    except StopIteration:
                        break
                if g1 is not None:
                    try:
                        next(g1)
                        alive = True
                    except StopIteration:
                        g1 = None
                if not alive:
                    break
        S.barrier()
        S.emit()
    return nc


OFF_XBC, OFF_DT, OFF_QKV, OFF_GATE = 2048, 5120, 5184, 8256


def na_bias_tables(rpb, rows):
    NTl = rows // 2
    pats = [0, 1, 2, NTl - 2, NTl - 1]
    out = np.full((5, 128, 5, 16, 128), -30000.0, np.float32)
    for pi, t in enumerate(pats):
        kt0 = min(max(t - 2, 0), NTl - 5)
        for qi in range(128):
            r = 2 * t + qi // 64
            c = qi % 64
            r0 = min(max(r - 4, 0), rows - 8)
            c0 = min(max(c - 8, 0), 64 - 16)
            for kr in range(r0, r0 + 8):
                kb = kr // 2 - kt0
                assert 0 <= kb < 5
                kp0 = (kr % 2) * 64
                cols = np.arange(c0, c0 + 16)
                out[pi, kp0 + cols, kb, :, qi] = rpb[:, kr - r + 7, cols - c + 15].T
    return out.reshape(5, 128, 5 * 16 * 128)


def prep_common(inp, L):
    w_in = inp["w_in"][0]
    m = {}
    m["gmix"] = np.ascontiguousarray(inp["g_mix"][0].reshape(8, 128).T)
    colsF = np.concatenate([np.arange(OFF_XBC, OFF_XBC + 3072), np.arange(OFF_QKV, OFF_QKV + 2048)])
    wf = w_in[:, colsF].reshape(8, 128, NF, 128)
    m["wf"] = np.ascontiguousarray(wf.transpose(2, 1, 0, 3))
    colsT = np.concatenate([np.arange(0, 2048), np.arange(OFF_DT, OFF_DT + 64),
                            np.arange(OFF_QKV + 2048, OFF_QKV + 3072), np.arange(OFF_GATE, OFF_GATE + 2048)])
    wt = w_in[:, colsT].reshape(8, 128, NTC)
    m["wt"] = np.ascontiguousarray(wt.transpose(1, 0, 2))
    cw = inp["conv_w"][0]
    m["convw"] = np.ascontiguousarray(cw.reshape(5, 24, 128).transpose(2, 1, 0).reshape(128, 120))
    cb = inp["conv_b"][0]
    m["convbT"] = np.ascontiguousarray(cb.reshape(24, 128).T)
    m["convbrow"] = np.ascontiguousarray(cb.reshape(1, 3072))
    m["dtb"] = np.ascontiguousarray(np.broadcast_to(inp["dt_bias"][0].reshape(1, 64), (128, 64)))
    m["alog"] = np.ascontiguousarray(np.broadcast_to(inp["a_log"][0].reshape(1, 64), (128, 64)))
    m["dskip"] = np.ascontiguousarray(np.broadcast_to(np.repeat(inp["d_skip"][0], 64).reshape(1, 2048), (128, 2048)))
    m["sng"] = np.ascontiguousarray(np.broadcast_to(inp["ssd_norm_g"][0].reshape(1, 2048), (128, 2048)))
    m["qg"] = np.ascontiguousarray(np.tile(inp["q_norm_g"][0], 2).reshape(128, 1))
    m["kg"] = np.ascontiguousarray(np.tile(inp["k_norm_g"][0], 2).reshape(128, 1))
    m["bt"] = na_bias_tables(inp["rpb"][0], L // 64)
    m["wo"] = np.ascontiguousarray(inp["w_out"][0].reshape(24, 128, 1024).transpose(1, 0, 2))
    m["gffn"] = np.ascontiguousarray(np.broadcast_to(inp["g_ffn"][0].reshape(1, 1024), (128, 1024)))
    m["wpq"] = np.ascontiguousarray(inp["w_pq"][0].reshape(8, 128, 16, 128).transpose(2, 1, 0, 3))
    m["skT"] = np.ascontiguousarray(inp["sub_keys"][0].transpose(2, 0, 1))
    m["eu"] = np.ascontiguousarray(inp["expert_u"][0])
    m["ev"] = np.ascontiguousarray(inp["expert_v"][0])
    return {k: np.asarray(v, np.float32) for k, v in m.items()}


_NC_CACHE = {}


def core_shares(NT):
    groups = {0: [0, 3, 6], 1: [1, 4, 7], 2: [2, 5]}
    shares = {}
    for sq, cores in groups.items():
        parts = np.array_split(np.arange(NT), len(cores))
        for c, p in zip(cores, parts):
            shares[c] = [int(v) for v in p]
    NSH = (NT + 1) // 2
    return shares, NSH


def share_rowidx(tiles, NSH):
    tl = list(tiles) + [tiles[-1]] * (NSH - len(tiles))
    idx = np.array(tl, np.int32)[None, :] * 128 + np.arange(128, dtype=np.int32)[:, None]
    return np.ascontiguousarray(idx.astype(np.int32))


def kernel(**inp):
    L = inp["x_prompt"].shape[1]
    seqs = [inp["x_prompt"][0], inp["x_prompt"][1], inp["x_sample"][0]]
    common = prep_common(inp, L)
    if L not in _NC_CACHE:
        _NC_CACHE[L] = build(L)
    nc = _NC_CACHE[L]
    shares, NSH = core_shares(L // 128)
    in_maps = []
    for c in range(8):
        m = dict(common)
        m["x"] = np.ascontiguousarray(seqs[c % 3], dtype=np.float32)
        m["rowidx"] = share_rowidx(shares[c], NSH)
        in_maps.append(m)
    res = run_bass_kernel_spmd(nc, in_maps, core_ids=list(range(8)))
    outs = [np.empty((L, D), np.float32) for _ in range(3)]
    for c in range(8):
        o = np.asarray(res.results[c]["out"], np.float32)
        for i, t in enumerate(shares[c]):
            outs[c % 3][t * 128:(t + 1) * 128] = o[i * 128:(i + 1) * 128]
    return (np.stack(outs[0:2], 0), outs[2][None])
```

```python
import contextlib
import numpy as np
import concourse.bass as bass
import concourse.mybir as mybir
from concourse.bass_utils import run_bass_kernel_spmd

F32 = mybir.dt.float32
BF16 = mybir.dt.bfloat16
I32 = mybir.dt.int32
U32 = mybir.dt.uint32
AF = mybir.ActivationFunctionType
ALU = mybir.AluOpType

ENGS = ("pe", "act", "dve", "pool", "sp")
NDMASEM = 12


class Buf:
    __slots__ = ("name", "w", "r")

    def __init__(self, name=""):
        self.name = name
        self.w = None
        self.r = []


class Sched:
    def __init__(self, nc):
        self.nc = nc
        self.q = {e: [] for e in ENGS}
        self.nop = {e: 0 for e in ENGS}
        self.opent = {e: [] for e in ENGS}
        self.seen = {e: {} for e in ENGS}
        self.dma_cnt = {(q, j): 0 for q in ("sp", "pool", "act") for j in range(NDMASEM)}
        self.dma_rr = {"sp": 0, "pool": 0, "act": 0}

    def _deps(self, eng, reads, writes):
        need = {}

        def add(tok):
            if tok is None:
                return
            k, v = tok
            if need.get(k, 0) < v:
                need[k] = v
        for b in reads:
            add(b.w)
        for b in writes:
            add(b.w)
            for t in b.r:
                add(t)
        waits = []
        seen = self.seen[eng]
        for k, v in need.items():
            if seen.get(k, 0) < v:
                seen[k] = v
                waits.append((k, v))
                if k[0] == "e":
                    self.opent[k[1]][v - 1][3] = True
        return waits

    def _commit(self, tok, reads, writes):
        for b in reads:
            b.r.append(tok)
            if len(b.r) > 16:
                m = {}
                for k, v in b.r:
                    if m.get(k, 0) < v:
                        m[k] = v
                b.r = list(m.items())
        for b in writes:
            b.w = tok
            b.r = []

    def op(self, eng, fn, reads=(), writes=()):
        waits = self._deps(eng, reads, writes)
        self.nop[eng] += 1
        tok = (("e", eng), self.nop[eng])
        ent = [waits, fn, "e", False]
        self.q[eng].append(ent)
        self.opent[eng].append(ent)
        self._commit(tok, reads, writes)
        return tok

    def dma(self, eng, fn, reads=(), writes=()):
        waits = self._deps(eng, reads, writes)
        j = self.dma_rr[eng]
        self.dma_rr[eng] = (j + 1) % NDMASEM
        self.dma_cnt[(eng, j)] += 16
        tok = (("d", eng, j), self.dma_cnt[(eng, j)])
        self.q[eng].append([waits, fn, ("d", eng, j), True])
        self._commit(tok, reads, writes)
        return tok

    def barrier(self):
        allw = [(("e", e), self.nop[e]) for e in ENGS if self.nop[e] > 0]
        allw += [(("d", q, j), v) for (q, j), v in self.dma_cnt.items() if v > 0]
        for e in ENGS:
            waits = []
            for k, v in allw:
                if self.seen[e].get(k, 0) < v:
                    self.seen[e][k] = v
                    waits.append((k, v))
                    if k[0] == "e":
                        self.opent[k[1]][v - 1][3] = True
            self.q[e].append([waits, None, None, False])

    def emit(self):
        nc = self.nc
        real = {}
        for e in ENGS:
            c = 0
            arr = [0]
            for ent in self.opent[e]:
                if ent[3]:
                    c += 1
                arr.append(c)
            real[e] = arr
        with contextlib.ExitStack() as st:
            sems = {}
            for e in ENGS:
                sems[("e", e)] = st.enter_context(nc.semaphore("s_" + e))
            for (q, j), v in self.dma_cnt.items():
                if v > 0:
                    sems[("d", q, j)] = st.enter_context(nc.semaphore("d_%s_%d" % (q, j)))
            block = st.enter_context(nc.Block())

            def run(ename):
                def body(eng):
                    for waits, fn, kind, marked in self.q[ename]:
                        for k, v in waits:
                            if k[0] == "e":
                                eng.wait_ge(sems[k], real[k[1]][v])
                            else:
                                eng.wait_ge(sems[k], v)
                        if fn is not None:
                            ins = fn(eng)
                            if kind == "e":
                                if marked:
                                    ins.then_inc(sems[("e", ename)], 1)
                            else:
                                ins.then_inc(sems[kind], 16)
                return body
            block.tensor(run("pe"))
            block.scalar(run("act"))
            block.vector(run("dve"))
            block.gpsimd(run("pool"))
            block.sync(run("sp"))


D = 1024
NF = 40
NTC = 5184
NEXP = 16384


def build(L, dbg=False, NSH=None):
    NT = L // 128
    if NSH is None:
        NSH = (NT + 1) // 2
    ROWS = L // 64
    nc = bass.Bass("TRN2", target_bir_lowering=False)
    dt_in = lambda n, s, d=F32: nc.dram_tensor(n, s, d, kind="ExternalInput")
    x_d = dt_in("x", [L, D])
    gmix_d = dt_in("gmix", [128, 8])
    wf_d = dt_in("wf", [NF, 128, 8, 128])
    wt_d = dt_in("wt", [128, 8, NTC])
    convw_d = dt_in("convw", [128, 120])
    convbT_d = dt_in("convbT", [128, 24])
    convbrow_d = dt_in("convbrow", [1, 3072])
    dtb_d = dt_in("dtb", [128, 64])
    alog_d = dt_in("alog", [128, 64])
    dskip_d = dt_in("dskip", [128, 2048])
    sng_d = dt_in("sng", [128, 2048])
    qg_d = dt_in("qg", [128, 1])
    kg_d = dt_in("kg", [128, 1])
    bt_d = dt_in("bt", [5, 128, 5 * 16 * 128])
    wo_d = dt_in("wo", [128, 24, 1024])
    gffn_d = dt_in("gffn", [128, 1024])
    wpq_d = dt_in("wpq", [16, 128, 8, 128])
    skT_d = dt_in("skT", [128, 2, 128])
    eu_d = dt_in("eu", [NEXP, D])
    ev_d = dt_in("ev", [NEXP, D])
    rowidx_d = nc.dram_tensor("rowidx", [128, NSH], I32, kind="ExternalInput")
    out_d = nc.dram_tensor("out", [NSH * 128, D], F32, kind="ExternalOutput")
    sk = "ExternalOutput" if dbg else "Internal"
    PF = nc.dram_tensor("PF", [NF * 128, L], BF16, kind=sk)
    PTz = nc.dram_tensor("PTz", [L, 2048], BF16, kind=sk)
    PTdt = nc.dram_tensor("PTdt", [L, 64], F32, kind=sk)
    PTv = nc.dram_tensor("PTv", [L, 1024], BF16, kind=sk)
    PTg = nc.dram_tensor("PTg", [L, 2048], BF16, kind=sk)
    XC = nc.dram_tensor("XC", [L, 2048], BF16, kind=sk)
    BC = nc.dram_tensor("BC", [L, 512], BF16, kind=sk)
    BCT = nc.dram_tensor("BCT", [512, L], BF16, kind=sk)
    CCT = nc.dram_tensor("CCT", [512, L], BF16, kind=sk)
    YF = nc.dram_tensor("YF", [L, 2048], F32, kind=sk)
    YTOK = nc.dram_tensor("YTOK", [L, 3072], BF16, kind=sk)
    EUb = nc.dram_tensor("EUb", [NEXP, D], BF16)
    EVb = nc.dram_tensor("EVb", [NEXP, D], BF16)
    XM = nc.dram_tensor("XM", [NSH * 128, D], F32, kind=sk)
    DBGF = nc.dram_tensor("DBGF", [128, 4096], F32, kind=sk)
    QN = nc.dram_tensor("QN", [1024, L], BF16, kind=sk)
    KN = nc.dram_tensor("KN", [1024, L], BF16, kind=sk)
    DBGB = nc.dram_tensor("DBGB", [128, 4096], BF16, kind=sk)

    S = Sched(nc)
    with contextlib.ExitStack() as st:
        sb = lambda n, s, d: st.enter_context(nc.sbuf_tensor(n, s, d))
        pst = lambda n, s, d: st.enter_context(nc.psum_tensor(n, s, d))
        identB = sb("identB", [128, 128], BF16)
        identF = sb("identF", [128, 128], F32)
        triU = sb("triU", [128, 128], F32)
        triLo = sb("triLo", [128, 128], F32)
        triLs = sb("triLs", [128, 128], F32)
        triUs = sb("triUs", [128, 128], F32)
        onesF = sb("onesF", [128, 128], F32)
        onesB = sb("onesB", [128, 128], BF16)
        gmix = sb("gmix_s", [128, 8], F32)
        wbuf = sb("wbuf", [128, 41472], BF16)
        fbuf = sb("fbuf", [128, 16384], F32)
        scr = sb("scr", [128, 6144], F32)
        scrB = scr[:, :].bitcast(BF16)
        psF = pst("psF", [128, 6, 512], F32)
        psB = pst("psB", [128, 2, 1024], BF16)
        B_const = Buf("const")
        B_w = Buf("wbuf")
        B_f = Buf("fbuf")
        B_ps = [Buf("ps%d" % i) for i in range(6)]
        B_psb = [Buf("psb%d" % i) for i in range(2)]

        def tri(t, pattern, cm, cmp):
            S.op("pool", lambda e: e.memset(t[:], 1.0), writes=[B_const])
            S.op("pool", lambda e: e.affine_select(out=t[:], in_=t[:], pattern=pattern, compare_op=cmp,
                                                   fill=0.0, base=0, channel_multiplier=cm),
                 reads=[B_const], writes=[B_const])
        S.op("pool", lambda e: e.memset(identF[:], 0.0), writes=[B_const])
        S.op("pool", lambda e: e.affine_select(out=identF[:], in_=identF[:], pattern=[[-1, 128]],
                                               compare_op=ALU.not_equal, fill=1.0, base=0, channel_multiplier=1),
             reads=[B_const], writes=[B_const])
        S.op("pool", lambda e: e.tensor_copy(out=identB[:], in_=identF[:]), reads=[B_const], writes=[B_const])
        tri(triU, [[1, 128]], -1, ALU.is_ge)
        tri(triLo, [[-1, 128]], 1, ALU.is_ge)
        tri(triLs, [[-1, 128]], 1, ALU.is_gt)
        tri(triUs, [[1, 128]], -1, ALU.is_gt)
        S.op("pool", lambda e: e.memset(onesF[:], 1.0), writes=[B_const])
        S.op("pool", lambda e: e.memset(onesB[:], 1.0), writes=[B_const])
        S.dma("sp", lambda e: e.dma_start(out=gmix[:], in_=gmix_d[:, :]), writes=[B_const])
        S.barrier()

        xt = [fbuf[:, i * 1024:(i + 1) * 1024] for i in range(2)]
        B_xt = [Buf() for _ in range(2)]
        sq = fbuf[:, 2048:4096]
        B_sq = Buf()
        ss = sb("ss", [128, 2], F32)
        B_ss = Buf()
        xs = scrB[:, 4096:5120]
        B_xs = Buf()
        hT = scrB[:, 0:4096].rearrange("p (k t) -> p k t", k=8)
        B_hT = Buf()
        fe_n = [0]

        def rms_rstd(src, Bsrc, width):
            sc = 1.0 / np.sqrt(width)
            S.op("act", lambda e: e.activation(out=sq[:, 0:width], in_=src, func=AF.Square, scale=float(sc),
                                               accum_out=ss[:, 0:1]),
                 reads=[Bsrc], writes=[B_sq, B_ss])
            S.op("dve", lambda e: e.tensor_scalar_add(out=ss[:, 0:1], in0=ss[:, 0:1], scalar1=1e-6),
                 reads=[B_ss], writes=[B_ss])
            S.op("act", lambda e: e.activation(out=ss[:, 0:1], in_=ss[:, 0:1], func=AF.Ln), reads=[B_ss], writes=[B_ss])
            S.op("act", lambda e: e.activation(out=ss[:, 0:1], in_=ss[:, 0:1], func=AF.Exp, scale=-0.5),
                 reads=[B_ss], writes=[B_ss])

        def front_end(i, tt):
            j = fe_n[0] % 2
            fe_n[0] += 1
            S.dma("sp", lambda e: e.dma_start(out=xt[j], in_=x_d[i * 128:(i + 1) * 128, :]), writes=[B_xt[j]])
            rms_rstd(xt[j], B_xt[j], D)
            S.op("dve", lambda e: e.tensor_scalar(out=xs, in0=xt[j], scalar1=ss[:, 0:1], scalar2=None,
                                                  op0=ALU.mult),
                 reads=[B_xt[j], B_ss], writes=[B_xs])
            for k in range(8):
                S.op("pe", lambda e, k=k: e.transpose(out=psB[:, 0, k * 128:(k + 1) * 128],
                                                      in_=xs[:, k * 128:(k + 1) * 128], identity=identB[:]),
                     reads=[B_xs, B_const], writes=[B_psb[0]])
            for k in range(8):
                if k % 2 == 0:
                    S.op("act", lambda e, k=k: e.activation(out=hT[:, k, tt * 128:(tt + 1) * 128],
                                                            in_=psB[:, 0, k * 128:(k + 1) * 128],
                                                            func=AF.Copy, scale=gmix[:, k:k + 1]),
                         reads=[B_psb[0], B_const], writes=[B_hT])
                else:
                    S.op("dve", lambda e, k=k: e.tensor_scalar(out=hT[:, k, tt * 128:(tt + 1) * 128],
                                                               in0=psB[:, 0, k * 128:(k + 1) * 128],
                                                               scalar1=gmix[:, k:k + 1], scalar2=None, op0=ALU.mult),
                         reads=[B_psb[0], B_const], writes=[B_hT])

        wst = [fbuf[:, 4096 + i * 1024:4096 + (i + 1) * 1024] for i in range(2)]
        B_wst = [Buf() for _ in range(2)]
        WF = wbuf[:, 0:NF * 1024].rearrange("p (c k n) -> p c k n", c=NF, k=8)
        for ct in range(NF):
            j = ct % 2
            S.dma("sp", lambda e, ct=ct, j=j: e.dma_start(out=wst[j], in_=wf_d[ct].rearrange("p k n -> p (k n)")),
                  writes=[B_wst[j]])
            eng = "dve" if ct % 2 == 0 else "pool"
            S.op(eng, lambda e, ct=ct, j=j: e.tensor_copy(out=wbuf[:, ct * 1024:(ct + 1) * 1024], in_=wst[j]),
                 reads=[B_wst[j]], writes=[B_w])
        stg = [scrB[:, 5120 + i * 512:5120 + (i + 1) * 512] for i in range(4)]
        B_stg = [Buf() for _ in range(4)]
        n_ev = 0
        for b in range(L // 512):
            for tt in range(4):
                front_end(b * 4 + tt, tt)
            for ct in range(NF):
                bank = ct % 4
                for k in range(8):
                    S.op("pe", lambda e, ct=ct, k=k, bank=bank: e.matmul(psF[:, bank, :], lhsT=WF[:, ct, k, :],
                                                                          rhs=hT[:, k, :], start=(k == 0), stop=(k == 7)),
                         reads=[B_w, B_hT], writes=([B_ps[bank]] if k in (0, 7) else []))
                sj = n_ev % 4
                n_ev += 1
                if sj % 2 == 0:
                    S.op("act", lambda e, bank=bank, sj=sj: e.copy(out=stg[sj], in_=psF[:, bank, :]),
                         reads=[B_ps[bank]], writes=[B_stg[sj]])
                else:
                    S.op("dve", lambda e, bank=bank, sj=sj: e.tensor_copy(out=stg[sj], in_=psF[:, bank, :]),
                         reads=[B_ps[bank]], writes=[B_stg[sj]])
                S.dma("sp", lambda e, ct=ct, b=b, sj=sj: e.dma_start(
                    out=PF[ct * 128:(ct + 1) * 128, b * 512:(b + 1) * 512], in_=stg[sj]),
                    reads=[B_stg[sj]])
        S.barrier()

        WT = wbuf[:, 0:8 * NTC].rearrange("p (k n) -> p k n", k=8)
        for k in range(8):
            for c0 in range(0, NTC, 1024):
                c1 = min(NTC, c0 + 1024)
                j = (c0 // 1024) % 2
                S.dma("sp", lambda e, k=k, c0=c0, c1=c1, j=j: e.dma_start(out=wst[j][:, 0:c1 - c0], in_=wt_d[:, k, c0:c1]),
                      writes=[B_wst[j]])
                S.op("dve", lambda e, k=k, c0=c0, c1=c1, j=j: e.tensor_copy(out=WT[:, k, c0:c1], in_=wst[j][:, 0:c1 - c0]),
                     reads=[B_wst[j]], writes=[B_w])
        stz = scrB[:, 7168:9216]
        stv = scrB[:, 9216:10240]
        stgt = scrB[:, 10240:12288]
        stdt = sb("stdt", [128, 64], F32)
        B_stz, B_stv, B_stgt, B_stdt = Buf(), Buf(), Buf(), Buf()
        nb = 0
        for i in range(NT):
            front_end(i, 0)

            def tgroup(c0, n, func, dst, Bdst, dcol):
                nonlocal nb
                bank = nb % 4
                nb += 1
                for k in range(8):
                    S.op("pe", lambda e, k=k: e.matmul(psF[:, bank, 0:n], lhsT=hT[:, k, 0:128], rhs=WT[:, k, c0:c0 + n],
                                                       start=(k == 0), stop=(k == 7)),
                         reads=[B_w, B_hT], writes=[B_ps[bank]])
                S.op("act", lambda e: e.activation(out=dst[:, dcol:dcol + n], in_=psF[:, bank, 0:n], func=func),
                     reads=[B_ps[bank]], writes=[Bdst])
            for q in range(4):
                tgroup(q * 512, 512, AF.Silu, stz, B_stz, q * 512)
            tgroup(2048, 64, AF.Copy, stdt, B_stdt, 0)
            for q in range(2):
                tgroup(2112 + q * 512, 512, AF.Copy, stv, B_stv, q * 512)
            for q in range(4):
                tgroup(3136 + q * 512, 512, AF.Sigmoid, stgt, B_stgt, q * 512)
            r0, r1 = i * 128, (i + 1) * 128
            S.dma("sp", lambda e, r0=r0, r1=r1: e.dma_start(out=PTz[r0:r1, :], in_=stz), reads=[B_stz])
            S.dma("sp", lambda e, r0=r0, r1=r1: e.dma_start(out=PTdt[r0:r1, :], in_=stdt[:]), reads=[B_stdt])
            S.dma("sp", lambda e, r0=r0, r1=r1: e.dma_start(out=PTv[r0:r1, :], in_=stv), reads=[B_stv])
            S.dma("sp", lambda e, r0=r0, r1=r1: e.dma_start(out=PTg[r0:r1, :], in_=stgt), reads=[B_stgt])
        S.barrier()

        convw_s = sb("convw_s", [128, 120], F32)
        convbT_s = sb("convbT_s", [128, 24], F32)
        convbrow_s = sb("convbrow_s", [1, 3072], F32)
        dtb_s = sb("dtb_s", [128, 64], F32)
        aneg = sb("aneg", [128, 64], F32)
        dskip_s = sb("dskip_s", [128, 2048], F32)
        sng_s = sb("sng_s", [128, 2048], F32)
        for dst, src in ((convw_s, convw_d), (convbT_s, convbT_d), (convbrow_s, convbrow_d), (dtb_s, dtb_d),
                         (aneg, alog_d), (dskip_s, dskip_d), (sng_s, sng_d)):
            S.dma("sp", lambda e, dst=dst, src=src: e.dma_start(out=dst[:], in_=src[:, :]), writes=[B_const])
        S.op("act", lambda e: e.activation(out=aneg[:], in_=aneg[:], func=AF.Exp), reads=[B_const], writes=[B_const])
        S.op("dve", lambda e: e.tensor_scalar_mul(out=aneg[:], in0=aneg[:], scalar1=-1.0), reads=[B_const], writes=[B_const])
        DG = wbuf[:, 0:120 * 128].rearrange("p (c j n) -> p c j n", c=24, j=5)
        for i in range(120):
            S.op("dve" if i % 2 == 0 else "pool",
                 lambda e, i=i: e.tensor_scalar(out=DG[:, i // 5, i % 5, :], in0=identF[:], scalar1=convw_s[:, i:i + 1],
                                                scalar2=None, op0=ALU.mult),
                 reads=[B_const], writes=[B_w])
        WB0 = 120 * 128

        def wv(off, n):
            return wbuf[:, WB0 + off:WB0 + off + n]
        xin = wv(0, 24 * 132).rearrange("p (c t) -> p c t", c=24)
        xcst = wv(3200, 2048)
        bcst = wv(5248, 512)
        tst = wv(5760, 1024).rearrange("p (g t) -> p g t", g=8)
        B_xin, B_xcst, B_bcst, B_tst = Buf(), Buf(), Buf(), Buf()
        nb = 0
        for c in range(NT):
            lo, hi = c * 128 - 2, c * 128 + 130
            s0, s1 = max(lo, 0), min(hi, L)
            if lo < 0 or hi > L:
                S.op("pool", lambda e: e.memset(xin, 0.0), writes=[B_xin])
            S.dma("sp", lambda e, s0=s0, s1=s1, lo=lo: e.dma_start(
                out=xin[:, :, s0 - lo:s1 - lo], in_=PF[0:3072, s0:s1].rearrange("(c p) t -> p c t", p=128)),
                writes=[B_xin])

            def conv_T(ct, bank, slot):
                o = psF[:, bank, slot * 128:(slot + 1) * 128]
                for j in range(5):
                    S.op("pe", lambda e, j=j: e.matmul(o, lhsT=xin[:, ct, j:j + 128], rhs=DG[:, ct, j, :],
                                                       start=(j == 0), stop=False),
                         reads=[B_xin, B_w], writes=[B_ps[bank]])
                S.op("pe", lambda e: e.matmul(o, lhsT=onesF[0:1, 0:128], rhs=convbrow_s[0:1, ct * 128:(ct + 1) * 128],
                                              start=False, stop=True),
                     reads=[B_const], writes=[B_ps[bank]])
            for cg in range(5):
                bank = nb % 4
                nb += 1
                for slot in range(4):
                    conv_T(cg * 4 + slot, bank, slot)
                if cg < 4:
                    S.op("act", lambda e, bank=bank, cg=cg: e.activation(out=xcst[:, cg * 512:(cg + 1) * 512],
                                                                         in_=psF[:, bank, :], func=AF.Silu),
                         reads=[B_ps[bank]], writes=[B_xcst])
                else:
                    S.op("act", lambda e, bank=bank: e.activation(out=bcst, in_=psF[:, bank, :], func=AF.Silu),
                         reads=[B_ps[bank]], writes=[B_bcst])
            S.dma("sp", lambda e, c=c: e.dma_start(out=XC[c * 128:(c + 1) * 128, :], in_=xcst), reads=[B_xcst])
            S.dma("sp", lambda e, c=c: e.dma_start(out=BC[c * 128:(c + 1) * 128, :], in_=bcst), reads=[B_bcst])
            for half in range(2):
                bank = nb % 4
                nb += 1
                for slot in range(4):
                    ct = 16 + half * 4 + slot
                    o = psF[:, bank, slot * 128:(slot + 1) * 128]
                    for j in range(5):
                        S.op("pe", lambda e, j=j, ct=ct, o=o: e.matmul(o, lhsT=DG[:, ct, j, :], rhs=xin[:, ct, j:j + 128],
                                                                        start=(j == 0), stop=(j == 4)),
                             reads=[B_xin, B_w], writes=[B_ps[bank]])
                    S.op("act", lambda e, ct=ct, o=o: e.activation(out=tst[:, ct - 16, :], in_=o, func=AF.Silu,
                                                                   bias=convbT_s[:, ct:ct + 1]),
                         reads=[B_ps[bank], B_const], writes=[B_tst])
            S.dma("sp", lambda e, c=c: e.dma_start(out=BCT[:, c * 128:(c + 1) * 128].rearrange("(g p) t -> p g t", p=128),
                                                   in_=tst[:, 0:4, :]), reads=[B_tst])
            S.dma("sp", lambda e, c=c: e.dma_start(out=CCT[:, c * 128:(c + 1) * 128].rearrange("(g p) t -> p g t", p=128),
                                                   in_=tst[:, 4:8, :]), reads=[B_tst])
        S.barrier()

        def fv(off, n):
            return fbuf[:, off:off + n]
        LA = fv(0, 4096).rearrange("p (h s) -> p h s", h=32)
        yacc = fv(4096, 2048)
        Hs = fv(6144, 2048)
        yf = fv(8192, 2048)
        yz = fv(10240, 2048)
        Eg = fv(12288, 1024)
        tmpf = fv(13312, 512)
        MTm = fv(13824, 512).rearrange("p (g l) -> p g l", g=4)
        sm = fv(14336, 256)
        sm2 = fv(14592, 96)
        sqw = fv(14700, 1600)
        xc = wv(0, 2048)
        bc = wv(2048, 512)
        bct = wv(2560, 512).rearrange("p (g t) -> p g t", g=4)
        cct = wv(3072, 512).rearrange("p (g t) -> p g t", g=4)
        xdt = wv(3584, 2048).rearrange("p (h d) -> p h d", h=32)
        MD = wv(5632, 1024).rearrange("p (h l) -> p h l", h=8)
        Hb = wv(6656, 2048)
        xdd = wv(8704, 512).rearrange("p (h d) -> p h d", h=8)
        zs = wv(9216, 2048)
        yn = wv(11264, 2048)
        ynT = wv(13312, 2048).rearrange("p (c t) -> p c t", c=16)
        dtr = sb("dtr", [128, 64], F32)
        Bn = {n: Buf(n) for n in "LA yacc Hs yf yz Eg tmpf MTm sm sm2 xc bc bct cct xdt MD Hb xdd zs yn ynT dtr".split()}
        def sweep(d):
            S.op("pool", lambda e: e.memset(Hs, 0.0), writes=[Bn["Hs"]])
            S.op("pool", lambda e: e.memset(Hb, 0.0), writes=[Bn["Hb"]])
            T1 = triLs if d == 0 else triUs
            T2 = triU if d == 0 else triLo
            Tds = triLs if d == 0 else triUs
            order = range(NT) if d == 0 else range(NT - 1, -1, -1)
            for c in order:
                r0, r1 = c * 128, (c + 1) * 128
                S.dma("sp", lambda e, r0=r0, r1=r1: e.dma_start(out=xc, in_=XC[r0:r1, :]), writes=[Bn["xc"]])
                S.dma("sp", lambda e, r0=r0, r1=r1: e.dma_start(out=bc, in_=BC[r0:r1, :]), writes=[Bn["bc"]])
                S.dma("sp", lambda e, r0=r0, r1=r1: e.dma_start(out=bct, in_=BCT[:, r0:r1].rearrange("(g p) t -> p g t", p=128)),
                      writes=[Bn["bct"]])
                S.dma("sp", lambda e, r0=r0, r1=r1: e.dma_start(out=cct, in_=CCT[:, r0:r1].rearrange("(g p) t -> p g t", p=128)),
                      writes=[Bn["cct"]])
                S.dma("sp", lambda e, r0=r0, r1=r1: e.dma_start(out=dtr[:], in_=PTdt[r0:r1, :]), writes=[Bn["dtr"]])
                if d == 1:
                    S.dma("sp", lambda e, r0=r0, r1=r1: e.dma_start(out=yf, in_=YF[r0:r1, :]), writes=[Bn["yf"]])
                    S.dma("sp", lambda e, r0=r0, r1=r1: e.dma_start(out=zs, in_=PTz[r0:r1, :]), writes=[Bn["zs"]])
                o32 = slice(d * 32, d * 32 + 32)
                smB = [Bn["sm"]]
                S.op("dve", lambda e: e.tensor_tensor(out=sm[:, 0:32], in0=dtr[:, o32], in1=dtb_s[:, o32], op=ALU.add),
                     reads=[Bn["dtr"], B_const], writes=smB)
                S.op("act", lambda e: e.activation(out=sm[:, 32:64], in_=sm[:, 0:32], func=AF.Abs),
                     reads=smB, writes=smB)
                S.op("act", lambda e: e.activation(out=sm[:, 64:96], in_=sm[:, 32:64], func=AF.Exp, scale=-1.0),
                     reads=smB, writes=smB)
                S.op("act", lambda e: e.activation(out=sm[:, 96:128], in_=sm[:, 64:96], func=AF.Ln, bias=1.0),
                     reads=smB, writes=smB)
                S.op("dve", lambda e: e.tensor_scalar_max(out=sm[:, 128:160], in0=sm[:, 0:32], scalar1=0.0),
                     reads=smB, writes=smB)
                S.op("dve", lambda e: e.tensor_tensor(out=sm[:, 160:192], in0=sm[:, 128:160], in1=sm[:, 96:128], op=ALU.add),
                     reads=smB, writes=smB)
                S.op("dve", lambda e: e.tensor_tensor(out=sm[:, 192:224], in0=sm[:, 160:192], in1=aneg[:, o32], op=ALU.mult),
                     reads=smB + [B_const], writes=smB)
                dtv = sm[:, 160:192]
                adt = sm[:, 192:224]
                S.op("dve", lambda e: e.tensor_tensor(out=xdt, in0=xc.rearrange("p (h d) -> p h d", h=32),
                                                      in1=dtv.unsqueeze(2).to_broadcast([128, 32, 64]), op=ALU.mult),
                     reads=[Bn["xc"]] + smB, writes=[Bn["xdt"]])
                for g in range(4):
                    S.op("pe", lambda e, g=g: e.matmul(psF[:, 0, g * 128:(g + 1) * 128], lhsT=bct[:, g, :], rhs=cct[:, g, :],
                                                       start=True, stop=True),
                         reads=[Bn["bct"], Bn["cct"]], writes=[B_ps[0]])
                msk = triU if d == 0 else triLo
                S.op("dve", lambda e: e.tensor_tensor(out=MTm, in0=psF[:, 0, :].rearrange("p (g l) -> p g l", g=4),
                                                      in1=msk[:].unsqueeze(1).to_broadcast([128, 4, 128]), op=ALU.mult),
                     reads=[B_ps[0], B_const], writes=[Bn["MTm"]])
                S.op("pool", lambda e: e.tensor_tensor(out=LA, in0=T1[:].unsqueeze(1).to_broadcast([128, 32, 128]),
                                                       in1=adt.unsqueeze(2).to_broadcast([128, 32, 128]), op=ALU.mult),
                     reads=smB + [B_const], writes=[Bn["LA"]])
                S.op("pe", lambda e: e.matmul(psF[:, 0, 0:32], lhsT=T2[:], rhs=adt, start=True, stop=True),
                     reads=smB + [B_const, Bn["MTm"]], writes=[B_ps[0]])
                S.op("pe", lambda e: e.matmul(psF[:, 0, 32:64], lhsT=Tds[:], rhs=adt, start=True, stop=True),
                     reads=smB + [B_const], writes=[B_ps[0]])
                S.op("pe", lambda e: e.matmul(psF[:, 0, 64:96], lhsT=onesF[:], rhs=adt, start=True, stop=True),
                     reads=smB + [B_const], writes=[B_ps[0]])
                S.op("act", lambda e: e.activation(out=sm2, in_=psF[:, 0, 0:96], func=AF.Exp),
                     reads=[B_ps[0]], writes=[Bn["sm2"]])
                EA, dsv, cdv = sm2[:, 0:32], sm2[:, 32:64], sm2[:, 64:96]
                for g in range(4):
                    for hh in range(8):
                        h = g * 8 + hh
                        bk = 1 + hh // 4
                        S.op("pe", lambda e, h=h, hh=hh, bk=bk: e.matmul(psF[:, bk, (hh % 4) * 128:(hh % 4 + 1) * 128],
                                                                          lhsT=LA[:, h, :], rhs=T2[:], start=True, stop=True),
                             reads=[Bn["LA"], B_const], writes=[B_ps[bk]])
                    S.op("act", lambda e: e.activation(out=Eg, in_=psF[:, 1:3, :].rearrange("p b n -> p (b n)"), func=AF.Exp),
                         reads=[B_ps[1], B_ps[2]], writes=[Bn["Eg"]])
                    S.op("dve", lambda e, g=g: e.tensor_tensor(out=MD, in0=Eg.rearrange("p (h l) -> p h l", h=8),
                                                               in1=MTm[:, g, :].unsqueeze(1).to_broadcast([128, 8, 128]),
                                                               op=ALU.mult),
                         reads=[Bn["Eg"], Bn["MTm"]], writes=[Bn["MD"]])
                    if dbg and d == 0 and c == 0 and g == 0:
                        S.dma("sp", lambda e: e.dma_start(out=DBGF[:, 0:256], in_=sm), reads=[Bn["sm"]])
                        S.dma("sp", lambda e: e.dma_start(out=DBGF[:, 256:352], in_=sm2), reads=[Bn["sm2"]])
                        S.dma("sp", lambda e: e.dma_start(out=DBGF[:, 512:1024], in_=MTm.rearrange("p g l -> p (g l)")), reads=[Bn["MTm"]])
                        S.dma("sp", lambda e: e.dma_start(out=DBGF[:, 1024:2048], in_=Eg), reads=[Bn["Eg"]])
                        S.dma("sp", lambda e: e.dma_start(out=DBGB[:, 0:2048], in_=xdt.rearrange("p h d -> p (h d)")), reads=[Bn["xdt"]])
                        S.dma("sp", lambda e: e.dma_start(out=DBGB[:, 2048:3072], in_=MD.rearrange("p h l -> p (h l)")), reads=[Bn["MD"]])
                    for hh in range(8):
                        h = g * 8 + hh
                        S.op("pe", lambda e, h=h, hh=hh: e.matmul(psF[:, 3, hh * 64:(hh + 1) * 64], lhsT=MD[:, hh, :],
                                                                  rhs=xdt[:, h, :], start=True, stop=True),
                             reads=[Bn["MD"], Bn["xdt"]], writes=[B_ps[3]])
                    S.op("pe", lambda e, g=g: e.matmul(psF[:, 4, :], lhsT=cct[:, g, :], rhs=Hb[:, g * 512:(g + 1) * 512],
                                                       start=True, stop=True),
                         reads=[Bn["cct"], Bn["Hb"]], writes=[B_ps[4]])
                    S.op("dve", lambda e, g=g: e.tensor_tensor(out=tmpf.rearrange("p (h d) -> p h d", h=8),
                                                               in0=psF[:, 4, :].rearrange("p (h d) -> p h d", h=8),
                                                               in1=EA[:, g * 8:(g + 1) * 8].unsqueeze(2).to_broadcast([128, 8, 64]),
                                                               op=ALU.mult),
                         reads=[B_ps[4], Bn["sm2"]], writes=[Bn["tmpf"]])
                    S.op("dve", lambda e, g=g: e.tensor_tensor(out=yacc[:, g * 512:(g + 1) * 512], in0=psF[:, 3, :], in1=tmpf,
                                                               op=ALU.add),
                         reads=[B_ps[3], Bn["tmpf"]], writes=[Bn["yacc"]])
                    S.op("pool", lambda e, g=g: e.tensor_tensor(out=xdd, in0=xdt[:, g * 8:(g + 1) * 8, :],
                                                                in1=dsv[:, g * 8:(g + 1) * 8].unsqueeze(2).to_broadcast([128, 8, 64]),
                                                                op=ALU.mult),
                         reads=[Bn["xdt"], Bn["sm2"]], writes=[Bn["xdd"]])
                    S.op("pe", lambda e, g=g: e.matmul(psF[:, 5, :], lhsT=bc[:, g * 128:(g + 1) * 128],
                                                       rhs=xdd.rearrange("p h d -> p (h d)"), start=True, stop=True),
                         reads=[Bn["bc"], Bn["xdd"]], writes=[B_ps[5]])
                    Hg = Hs[:, g * 512:(g + 1) * 512]
                    S.op("dve", lambda e, g=g, Hg=Hg: e.tensor_tensor(out=Hg.rearrange("p (h d) -> p h d", h=8),
                                                                      in0=Hg.rearrange("p (h d) -> p h d", h=8),
                                                                      in1=cdv[:, g * 8:(g + 1) * 8].unsqueeze(2).to_broadcast([128, 8, 64]),
                                                                      op=ALU.mult),
                         reads=[Bn["sm2"], Bn["Hs"]], writes=[Bn["Hs"]])
                    S.op("dve", lambda e, Hg=Hg: e.tensor_tensor(out=Hg, in0=psF[:, 5, :], in1=Hg, op=ALU.add),
                         reads=[B_ps[5], Bn["Hs"]], writes=[Bn["Hs"]])
                    S.op("act", lambda e, g=g, Hg=Hg: e.copy(out=Hb[:, g * 512:(g + 1) * 512], in_=Hg),
                         reads=[Bn["Hs"]], writes=[Bn["Hb"]])
                if d == 0:
                    S.dma("sp", lambda e, r0=r0, r1=r1: e.dma_start(out=YF[r0:r1, :], in_=yacc), reads=[Bn["yacc"]])
                else:
                    S.op("pool", lambda e: e.tensor_tensor(out=yacc, in0=yacc, in1=yf, op=ALU.add),
                         reads=[Bn["yf"], Bn["yacc"]], writes=[Bn["yacc"]])
                    S.op("dve", lambda e: e.tensor_tensor(out=yf, in0=xc, in1=dskip_s[:], op=ALU.mult),
                         reads=[Bn["xc"], B_const, Bn["yacc"]], writes=[Bn["yf"]])
                    S.op("pool", lambda e: e.tensor_tensor(out=yacc, in0=yacc, in1=yf, op=ALU.add),
                         reads=[Bn["yf"], Bn["yacc"]], writes=[Bn["yacc"]])
                    S.op("dve", lambda e: e.tensor_tensor(out=yz, in0=yacc, in1=zs, op=ALU.mult),
                         reads=[Bn["yacc"], Bn["zs"]], writes=[Bn["yz"]])
                    S.op("act", lambda e: e.activation(out=yf, in_=yz, func=AF.Square, scale=float(1.0 / np.sqrt(2048.0)),
                                                       accum_out=ss[:, 0:1]),
                         reads=[Bn["yz"]], writes=[Bn["yf"], B_ss])
                    S.op("dve", lambda e: e.tensor_scalar_add(out=ss[:, 0:1], in0=ss[:, 0:1], scalar1=1e-6),
                         reads=[B_ss], writes=[B_ss])
                    S.op("act", lambda e: e.activation(out=ss[:, 0:1], in_=ss[:, 0:1], func=AF.Ln), reads=[B_ss], writes=[B_ss])
                    S.op("act", lambda e: e.activation(out=ss[:, 0:1], in_=ss[:, 0:1], func=AF.Exp, scale=-0.5),
                         reads=[B_ss], writes=[B_ss])
                    S.op("dve", lambda e: e.scalar_tensor_tensor(out=yn, in0=yz, scalar=ss[:, 0:1], in1=sng_s[:],
                                                                 op0=ALU.mult, op1=ALU.mult),
                         reads=[Bn["yz"], B_ss, B_const], writes=[Bn["yn"]])
                    S.dma("sp", lambda e, r0=r0, r1=r1: e.dma_start(out=YTOK[r0:r1, 0:2048], in_=yn), reads=[Bn["yn"]])
        sweep(0)
        sweep(1)
        S.barrier()

        qkg = sb("qkg", [128, 2], F32)
        S.dma("sp", lambda e: e.dma_start(out=qkg[:, 0:1], in_=qg_d[:, :]), writes=[B_const])
        S.dma("sp", lambda e: e.dma_start(out=qkg[:, 1:2], in_=kg_d[:, :]), writes=[B_const])
        BD = sb("BD", [128, 128], BF16)
        S.op("pool", lambda e: e.memset(BD[:], 0.0), writes=[B_const])
        S.op("pool", lambda e: e.memset(BD[0:64, 0:64], 1.0), reads=[B_const], writes=[B_const])
        S.op("pool", lambda e: e.memset(BD[64:128, 64:128], 1.0), reads=[B_const], writes=[B_const])
        qraw = wbuf[:, 0:512]
        qsq = wbuf[:, 512:1024]
        qno = wbuf[:, 1024:1536]
        rstd_q = fbuf[:, 0:512]
        B_qraw, B_qsq, B_qno, B_rq = Buf(), Buf(), Buf(), Buf()
        for b in range(L // 512):
            for ct in range(16):
                c0, c1 = b * 512, (b + 1) * 512
                S.dma("sp", lambda e, ct=ct, c0=c0, c1=c1: e.dma_start(out=qraw, in_=PF[(24 + ct) * 128:(25 + ct) * 128, c0:c1]),
                      writes=[B_qraw])
                S.op("dve", lambda e: e.tensor_tensor(out=qsq, in0=qraw, in1=qraw, op=ALU.mult), reads=[B_qraw], writes=[B_qsq])
                bank = ct % 2
                S.op("pe", lambda e, bank=bank: e.matmul(psF[:, bank, :], lhsT=BD[:], rhs=qsq, start=True, stop=True),
                     reads=[B_qsq, B_const], writes=[B_ps[bank]])
                S.op("act", lambda e, bank=bank: e.activation(out=rstd_q, in_=psF[:, bank, :], func=AF.Ln, scale=1.0 / 64, bias=1e-6),
                     reads=[B_ps[bank]], writes=[B_rq])
                S.op("act", lambda e: e.activation(out=rstd_q, in_=rstd_q, func=AF.Exp, scale=-0.5), reads=[B_rq], writes=[B_rq])
                gi = 0 if ct < 8 else 1
                S.op("dve", lambda e, gi=gi: e.scalar_tensor_tensor(out=qno, in0=qraw, scalar=qkg[:, gi:gi + 1], in1=rstd_q,
                                                                   op0=ALU.mult, op1=ALU.mult),
                     reads=[B_qraw, B_rq, B_const], writes=[B_qno])
                dstT = QN if ct < 8 else KN
                cc = ct % 8
                S.dma("sp", lambda e, dstT=dstT, cc=cc, c0=c0, c1=c1: e.dma_start(out=dstT[cc * 128:(cc + 1) * 128, c0:c1], in_=qno),
                      reads=[B_qno])
        S.barrier()
        NE = 5 * 16 * 128
        Egen = wbuf[:, 0:NE].rearrange("p (k h q) -> p k h q", k=5, h=16)
        Espec = wbuf[:, NE:2 * NE].rearrange("p (k h q) -> p k h q", k=5, h=16)
        o0 = 2 * NE
        qn = wbuf[:, o0:o0 + 1024].rearrange("p (c t) -> p c t", c=8)
        kn = wbuf[:, o0 + 1024:o0 + 1024 + 5120].rearrange("p (c t) -> p c t", c=8)
        o1 = o0 + 6144
        vaug = wbuf[:, o1:o1 + 5200].rearrange("p (k h d) -> p k h d", k=5, h=16)
        o2 = o1 + 5200
        pbuf = [wbuf[:, o2 + i * 512:o2 + (i + 1) * 512] for i in range(2)]
        pm = [[wbuf[:, o2 + 1024 + (i * 5 + k) * 512:o2 + 1024 + (i * 5 + k + 1) * 512] for k in range(5)] for i in range(2)]
        yna = wbuf[:, o2 + 6144:o2 + 7168]
        ynaT = wbuf[:, o2 + 7168:o2 + 8192].rearrange("p (c t) -> p c t", c=8)
        tstage = fbuf[:, 0:NE]
        den = fbuf[:, NE:NE + 16]
        B_E, B_Es, B_qn, B_kn, B_va, B_yna, B_ynaT, B_ts, B_den = (Buf() for _ in range(9))
        B_pb = [Buf(), Buf()]
        B_pm = [Buf(), Buf()]

        def load_table(pat, dst, Bdst):
            S.dma("sp", lambda e: e.dma_start(out=tstage, in_=bt_d[pat]), writes=[B_ts])
            S.op("act", lambda e: e.activation(out=dst.rearrange("p k h q -> p (k h q)"), in_=tstage, func=AF.Exp),
                 reads=[B_ts], writes=[Bdst])
        load_table(2, Egen, B_E)
        S.op("pool", lambda e: e.memset(vaug.rearrange("p k h d -> p (k h d)"), 1.0), writes=[B_va])
        npp = 0
        for t in range(NT):
            kt0 = min(max(t - 2, 0), NT - 5)
            rel = t - kt0
            if rel == 2:
                Et, BEt = Egen, B_E
            else:
                load_table(rel, Espec, B_Es)
                Et, BEt = Espec, B_Es
            r0, r1 = t * 128, (t + 1) * 128
            k0, k1 = kt0 * 128, (kt0 + 5) * 128
            S.dma("sp", lambda e, r0=r0, r1=r1: e.dma_start(out=qn, in_=QN[:, r0:r1].rearrange("(c p) t -> p c t", p=128)),
                  writes=[B_qn])
            S.dma("sp", lambda e, k0=k0, k1=k1: e.dma_start(out=kn, in_=KN[:, k0:k1].rearrange("(c p) t -> p c t", p=128)),
                  writes=[B_kn])
            for kb in range(5):
                S.dma("sp", lambda e, kb=kb, k0=k0: e.dma_start(
                    out=vaug[:, kb, :, 0:64], in_=PTv[k0 + kb * 128:k0 + (kb + 1) * 128, :].rearrange("p (h d) -> p h d", h=16)),
                    writes=[B_va])
            for hq in range(4):
                jq = hq % 2
                for kb in range(5):
                    j = npp % 2
                    npp += 1
                    bank = j
                    for hh in range(4):
                        h = hq * 4 + hh
                        ct, po = h // 2, (h % 2) * 64
                        S.op("pe", lambda e, ct=ct, po=po, hh=hh, kb=kb, bank=bank: e.matmul(
                            psF[:, bank, hh * 128:(hh + 1) * 128], lhsT=kn[po:po + 64, ct, kb * 128:(kb + 1) * 128],
                            rhs=qn[po:po + 64, ct, :], start=True, stop=True),
                            reads=[B_kn, B_qn], writes=[B_ps[bank]])
                    S.op("act", lambda e, bank=bank, j=j: e.activation(out=pbuf[j], in_=psF[:, bank, :], func=AF.Exp, scale=0.125),
                         reads=[B_ps[bank]], writes=[B_pb[j]])
                    S.op("pool" if j == 0 else "dve", lambda e, j=j, kb=kb, hq=hq, Et=Et, jq=jq: e.tensor_tensor(
                        out=pm[jq][kb], in0=pbuf[j], in1=Et[:, kb, hq * 4:(hq + 1) * 4, :].rearrange("p h q -> p (h q)"), op=ALU.mult),
                        reads=[B_pb[j], BEt], writes=[B_pm[jq]])
                for hh in range(4):
                    h = hq * 4 + hh
                    ob, oc = 2 + h // 7, (h % 7) * 65
                    for kb in range(5):
                        S.op("pe", lambda e, h=h, hh=hh, kb=kb, jq=jq, ob=ob, oc=oc: e.matmul(
                            psF[:, ob, oc:oc + 65], lhsT=pm[jq][kb][:, hh * 128:(hh + 1) * 128], rhs=vaug[:, kb, h, :],
                            start=(kb == 0), stop=(kb == 4)),
                            reads=[B_pm[jq], B_va], writes=[B_ps[ob]])
            for ob, h0, nh in ((2, 0, 7), (3, 7, 7), (4, 14, 2)):
                pv = psF[:, ob, 0:nh * 65].rearrange("p (h d) -> p h d", h=nh)
                S.op("dve", lambda e, pv=pv, h0=h0, nh=nh: e.tensor_copy(out=den[:, h0:h0 + nh].unsqueeze(2), in_=pv[:, :, 64:65]),
                     reads=[B_ps[ob]], writes=[B_den])
            S.op("dve", lambda e: e.reciprocal(out=den, in_=den), reads=[B_den], writes=[B_den])
            for ob, h0, nh in ((2, 0, 7), (3, 7, 7), (4, 14, 2)):
                pv = psF[:, ob, 0:nh * 65].rearrange("p (h d) -> p h d", h=nh)
                S.op("dve", lambda e, pv=pv, h0=h0, nh=nh: e.tensor_tensor(
                    out=yna[:, h0 * 64:(h0 + nh) * 64].rearrange("p (h d) -> p h d", h=nh), in0=pv[:, :, 0:64],
                    in1=den[:, h0:h0 + nh].unsqueeze(2).to_broadcast([128, nh, 64]), op=ALU.mult),
                    reads=[B_ps[ob], B_den], writes=[B_yna])
            S.dma("sp", lambda e, r0=r0, r1=r1: e.dma_start(out=YTOK[r0:r1, 2048:3072], in_=yna), reads=[B_yna])
        S.barrier()

        WO = wbuf[:, 0:24 * 1024].rearrange("p (c n) -> p c n", c=24)
        for ct in range(24):
            j = ct % 2
            S.dma("sp", lambda e, ct=ct, j=j: e.dma_start(out=wst[j], in_=wo_d[:, ct, :]), writes=[B_wst[j]])
            S.op("dve" if j == 0 else "pool", lambda e, ct=ct, j=j: e.tensor_copy(out=WO[:, ct, :], in_=wst[j]),
                 reads=[B_wst[j]], writes=[B_w])
        _o = [0]

        def sv(n, shape=None, dt=None):
            v = scr[:, _o[0]:_o[0] + n]
            _o[0] += n
            if dt is not None:
                v = v.bitcast(dt)
            return v
        gffn_s = sv(1024)
        skT_s = sv(256).rearrange("p (k n) -> p k n", k=2)
        iota_i = sv(256, dt=I32)
        iota_f = sv(256)
        S.dma("sp", lambda e: e.dma_start(out=gffn_s, in_=gffn_d[:, :]), writes=[B_const])
        S.dma("sp", lambda e: e.dma_start(out=skT_s, in_=skT_d[:, :, :]), writes=[B_const])
        S.op("pool", lambda e: e.iota(iota_i, pattern=[[1, 256]], base=0, channel_multiplier=0), writes=[B_const])
        S.op("pool", lambda e: e.tensor_copy(out=iota_f, in_=iota_i), reads=[B_const], writes=[B_const])
        yT_s = wbuf[:, 24576:24576 + 3072].rearrange("p (c t) -> p c t", c=24)
        g12 = wbuf[:, 27648:27648 + 2048]
        yrow = wbuf[:, 29696:29696 + 3072]
        gbb = [wbuf[:, 32768 + i * 1024:32768 + (i + 1) * 1024] for i in range(4)]
        cst = [wbuf[:, 36864 + i * 1024:36864 + (i + 1) * 1024] for i in range(2)]
        B_cst = [Buf(), Buf()]
        B_yrow = Buf()
        rowidx_s = sv(NSH, dt=I32)
        S.dma("sp", lambda e: e.dma_start(out=rowidx_s, in_=rowidx_d[:, :]), writes=[B_const])
        nce = 0
        for src, dstE in ((eu_d, EUb), (ev_d, EVb)):
            for rt in range(NEXP // 128):
                j = nce % 2
                nce += 1
                S.dma("sp", lambda e, src=src, rt=rt, j=j: e.dma_start(out=wst[j], in_=src[rt * 128:(rt + 1) * 128, :]), writes=[B_wst[j]])
                if nce % 3 == 0:
                    S.op("act", lambda e, j=j: e.copy(out=cst[j], in_=wst[j]), reads=[B_wst[j]], writes=[B_cst[j]])
                elif nce % 3 == 1:
                    S.op("dve", lambda e, j=j: e.tensor_copy(out=cst[j], in_=wst[j]), reads=[B_wst[j]], writes=[B_cst[j]])
                else:
                    S.op("pool", lambda e, j=j: e.tensor_copy(out=cst[j], in_=wst[j]), reads=[B_wst[j]], writes=[B_cst[j]])
                S.dma("sp", lambda e, dstE=dstE, rt=rt, j=j: e.dma_start(out=dstE[rt * 128:(rt + 1) * 128, :], in_=cst[j]), reads=[B_cst[j]])
        S.barrier()
        xin_f = fbuf[:, 0:1024]
        xm = fbuf[:, 1024:2048]
        tmpo = fbuf[:, 2048:3072]
        tt = fbuf[:, 3072:4096]
        tT_s = fbuf[:, 4096:5120]
        qT_s = fbuf[:, 5120:7168]
        s_s = fbuf[:, 7168:9216].rearrange("p (k n) -> p k n", k=16)
        gbuf = [fbuf[:, 9216 + i * 1024:9216 + (i + 1) * 1024] for i in range(4)]
        junk = fbuf[:, 13312:14336]
        wpst = [fbuf[:, 14336 + i * 1024:14336 + (i + 1) * 1024] for i in range(2)]
        top = sv(256).rearrange("p (a b) -> p a b", a=16)
        idxu = sv(256, dt=U32).rearrange("p (a b) -> p a b", a=16)
        idxf = sv(256).rearrange("p (a b) -> p a b", a=16)
        work = sv(128)
        cand = sv(256).rearrange("p (a b) -> p a b", a=16)
        cidx = sv(256).rearrange("p (a b) -> p a b", a=16)
        work2 = sv(256)
        best = sv(16)
        posu = sv(16, dt=U32)
        posf = sv(16)
        ef = sv(128)
        ei = sv(128, dt=I32)
        gts = sv(128)
        actv = sv(128)
        wgt = sv(128)
        smz = sv(4)
        Bq = {n: Buf(n) for n in "yT g12 xin xm tmpo tt tT qT s junk top idxu idxf work cand cidx work2 best posu posf ef ei gts actv wgt smz".split()}
        B_gb = [Buf() for _ in range(4)]
        B_wp = [Buf(), Buf()]
        tt2 = fbuf[:, 9216:10240]
        xm2 = fbuf[:, 10240:11264]
        junk2 = fbuf[:, 11264:12288]
        gts2 = sv(128)
        ei2 = sv(128, dt=I32)
        gring = gbb + cst + [wbuf[:, 38912:39936], wbuf[:, 39936:40960]]
        B_gr = [Buf() for _ in gring]
        NG = len(gring)
        for _n in ("tt", "xm", "ei", "gts"):
            Bq[_n + "0"] = Bq[_n]
            Bq[_n + "1"] = Buf(_n + "1")
        Bq["junk2"] = Buf("junk2")
        PARV = {"tt": (tt, tt2), "xm": (xm, xm2), "ei": (ei, ei2), "gts": (gts, gts2)}
        ngbc = [0]
        dgs = [sv(64, dt=BF16) for _ in range(4)]
        B_dg = [Buf() for _ in range(4)]

        def stage1(i):
            par = i % 2
            tt_p, xm_p, ei_p, gts_p = PARV["tt"][par], PARV["xm"][par], PARV["ei"][par], PARV["gts"][par]
            r0, r1 = i * 128, (i + 1) * 128
            ridx = rowidx_s[:, i:i + 1]
            S.dma("pool", lambda e, ridx=ridx: e.indirect_dma_start(out=yrow, out_offset=None, in_=YTOK[:, :],
                                                                    in_offset=bass.IndirectOffsetOnAxis(ap=ridx, axis=0)),
                  reads=[B_const], writes=[B_yrow])
            S.dma("pool", lambda e, ridx=ridx: e.indirect_dma_start(out=g12, out_offset=None, in_=PTg[:, :],
                                                                    in_offset=bass.IndirectOffsetOnAxis(ap=ridx, axis=0)),
                  reads=[B_const], writes=[Bq["g12"]])
            S.dma("pool", lambda e, ridx=ridx: e.indirect_dma_start(out=xin_f, out_offset=None, in_=x_d[:, :],
                                                                    in_offset=bass.IndirectOffsetOnAxis(ap=ridx, axis=0)),
                  reads=[B_const], writes=[Bq["xin"]])
            for rnd, (c0, nct) in enumerate(((0, 16), (16, 8))):
                for cc in range(nct):
                    ct = c0 + cc
                    S.op("pe", lambda e, ct=ct, cc=cc: e.transpose(out=psB[:, cc // 8, (cc % 8) * 128:(cc % 8 + 1) * 128],
                                                                  in_=yrow[:, ct * 128:(ct + 1) * 128], identity=identB[:]),
                         reads=[B_yrow, B_const], writes=[B_psb[cc // 8]])
                S.op("act", lambda e, c0=c0: e.copy(out=yT_s[:, c0:c0 + 8, :].rearrange("p c t -> p (c t)"), in_=psB[:, 0, :]),
                     reads=[B_psb[0]], writes=[Bq["yT"]])
                if nct == 16:
                    S.op("dve", lambda e: e.tensor_copy(out=yT_s[:, 8:16, :].rearrange("p c t -> p (c t)"), in_=psB[:, 1, :]),
                         reads=[B_psb[1]], writes=[Bq["yT"]])
            for half in range(2):
                yield
                hs = slice(half * 512, (half + 1) * 512)
                hs2 = slice(1024 + half * 512, 1024 + (half + 1) * 512)
                for ct in range(16):
                    S.op("pe", lambda e, ct=ct, hs=hs: e.matmul(psF[:, 0, :], lhsT=yT_s[:, ct, :], rhs=WO[:, ct, hs],
                                                                start=(ct == 0), stop=(ct == 15)),
                         reads=[Bq["yT"], B_w], writes=[B_ps[0]])
                for ct in range(16, 24):
                    S.op("pe", lambda e, ct=ct, hs=hs: e.matmul(psF[:, 1, :], lhsT=yT_s[:, ct, :], rhs=WO[:, ct, hs],
                                                                start=(ct == 16), stop=(ct == 23)),
                         reads=[Bq["yT"], B_w], writes=[B_ps[1]])
                S.op("dve", lambda e, hs=hs: e.tensor_tensor(out=tmpo[:, hs], in0=psF[:, 0, :], in1=g12[:, hs], op=ALU.mult),
                     reads=[B_ps[0], Bq["g12"]], writes=[Bq["tmpo"]])
                S.op("dve", lambda e, hs=hs, hs2=hs2: e.tensor_tensor(out=junk[:, hs], in0=psF[:, 1, :], in1=g12[:, hs2], op=ALU.mult),
                     reads=[B_ps[1], Bq["g12"]], writes=[Bq["junk"]])
                S.op("pool", lambda e, hs=hs: e.tensor_tensor(out=tmpo[:, hs], in0=tmpo[:, hs], in1=junk[:, hs], op=ALU.add),
                     reads=[Bq["junk"], Bq["tmpo"]], writes=[Bq["tmpo"]])
                S.op("pool", lambda e, hs=hs: e.tensor_tensor(out=xm_p[:, hs], in0=tmpo[:, hs], in1=xin_f[:, hs], op=ALU.add),
                     reads=[Bq["tmpo"], Bq["xin"]], writes=[Bq["xm%d" % par]])
            if dbg:
                S.dma("sp", lambda e, r0=r0, r1=r1: e.dma_start(out=XM[r0:r1, :], in_=xm_p), reads=[Bq["xm%d" % par]])
            S.op("act", lambda e: e.activation(out=junk, in_=xm_p, func=AF.Square, scale=1.0 / 32, accum_out=ss[:, 0:1]),
                 reads=[Bq["xm%d" % par]], writes=[Bq["junk"], B_ss])
            S.op("dve", lambda e: e.tensor_scalar_add(out=ss[:, 0:1], in0=ss[:, 0:1], scalar1=1e-6), reads=[B_ss], writes=[B_ss])
            S.op("act", lambda e: e.activation(out=ss[:, 0:1], in_=ss[:, 0:1], func=AF.Ln), reads=[B_ss], writes=[B_ss])
            S.op("act", lambda e: e.activation(out=ss[:, 0:1], in_=ss[:, 0:1], func=AF.Exp, scale=-0.5), reads=[B_ss], writes=[B_ss])
            S.op("dve", lambda e: e.scalar_tensor_tensor(out=tt_p, in0=xm_p, scalar=ss[:, 0:1], in1=gffn_s, op0=ALU.mult, op1=ALU.mult),
                 reads=[Bq["xm%d" % par], B_ss, B_const], writes=[Bq["tt%d" % par]])
            for k in range(8):
                S.op("pe", lambda e, k=k: e.transpose(out=psF[:, 2 + k // 4, (k % 4) * 128:(k % 4 + 1) * 128],
                                                      in_=tt_p[:, k * 128:(k + 1) * 128], identity=identF[:]),
                     reads=[Bq["tt%d" % par], B_const], writes=[B_ps[2 + k // 4]])
            S.op("act", lambda e: e.copy(out=tT_s[:, 0:512], in_=psF[:, 2, :]), reads=[B_ps[2]], writes=[Bq["tT"]])
            S.op("dve", lambda e: e.tensor_copy(out=tT_s[:, 512:1024], in_=psF[:, 3, :]), reads=[B_ps[3]], writes=[Bq["tT"]])
            for cq in range(16):
                yield
                j = cq % 2
                bank = 2 + (cq // 4) % 2
                S.dma("sp", lambda e, cq=cq, j=j: e.dma_start(out=wpst[j], in_=wpq_d[cq].rearrange("p k n -> p (k n)")), writes=[B_wp[j]])
                for k in range(8):
                    S.op("pe", lambda e, cq=cq, k=k, j=j, bank=bank: e.matmul(
                        psF[:, bank, (cq % 4) * 128:(cq % 4 + 1) * 128], lhsT=wpst[j][:, k * 128:(k + 1) * 128],
                        rhs=tT_s[:, k * 128:(k + 1) * 128], start=(k == 0), stop=(k == 7)),
                        reads=[B_wp[j], Bq["tT"]], writes=[B_ps[bank]])
                if cq % 4 == 3:
                    q0 = (cq // 4) * 512
                    S.op("act" if (cq // 4) % 2 == 0 else "dve",
                         (lambda e, bank=bank, q0=q0: e.copy(out=qT_s[:, q0:q0 + 512], in_=psF[:, bank, :])) if (cq // 4) % 2 == 0 else
                         (lambda e, bank=bank, q0=q0: e.tensor_copy(out=qT_s[:, q0:q0 + 512], in_=psF[:, bank, :])),
                         reads=[B_ps[bank]], writes=[Bq["qT"]])
            for hk in range(16):
                yield
                bank = 2 + (hk // 4) % 2
                S.op("pe", lambda e, hk=hk, bank=bank: e.matmul(psF[:, bank, (hk % 4) * 128:(hk % 4 + 1) * 128],
                                                                lhsT=qT_s[:, hk * 128:(hk + 1) * 128], rhs=skT_s[:, hk % 2, :],
                                                                start=True, stop=True),
                     reads=[Bq["qT"], B_const], writes=[B_ps[bank]])
                if hk % 4 == 3:
                    k0 = hk - 3
                    S.op("act", lambda e, bank=bank, k0=k0: e.copy(out=s_s[:, k0:k0 + 4, :].rearrange("p k n -> p (k n)"), in_=psF[:, bank, :]),
                         reads=[B_ps[bank]], writes=[Bq["s"]])
            for hk in range(16):
                yield
                S.op("dve", lambda e, hk=hk: e.max(out=top[:, hk, 0:8], in_=s_s[:, hk, :]), reads=[Bq["s"]], writes=[Bq["top"]])
                S.op("dve", lambda e, hk=hk: e.max_index(out=idxu[:, hk, 0:8], in_max=top[:, hk, 0:8], in_values=s_s[:, hk, :]),
                     reads=[Bq["s"], Bq["top"]], writes=[Bq["idxu"]])
                S.op("dve", lambda e, hk=hk: e.match_replace(out=work, in_to_replace=top[:, hk, 0:8], in_values=s_s[:, hk, :],
                                                             imm_value=-1e30),
                     reads=[Bq["s"], Bq["top"]], writes=[Bq["work"]])
                S.op("dve", lambda e, hk=hk: e.max(out=top[:, hk, 8:16], in_=work), reads=[Bq["work"]], writes=[Bq["top"]])
                S.op("dve", lambda e, hk=hk: e.max_index(out=idxu[:, hk, 8:16], in_max=top[:, hk, 8:16], in_values=work),
                     reads=[Bq["work"], Bq["top"]], writes=[Bq["idxu"]])
            S.op("dve", lambda e: e.tensor_copy(out=idxf, in_=idxu), reads=[Bq["idxu"]], writes=[Bq["idxf"]])
            for h in range(8):
                yield
                S.op("dve", lambda e, h=h: e.tensor_tensor(out=cand, in0=top[:, 2 * h, :].unsqueeze(2).to_broadcast([128, 16, 16]),
                                                           in1=top[:, 2 * h + 1, :].unsqueeze(1).to_broadcast([128, 16, 16]), op=ALU.add),
                     reads=[Bq["top"]], writes=[Bq["cand"]])
                S.op("pool", lambda e, h=h: e.tensor_scalar_mul(out=posf, in0=idxf[:, 2 * h, :], scalar1=128.0),
                     reads=[Bq["idxf"]], writes=[Bq["posf"]])
                S.op("pool", lambda e, h=h: e.tensor_tensor(out=cidx, in0=posf.unsqueeze(2).to_broadcast([128, 16, 16]),
                                                            in1=idxf[:, 2 * h + 1, :].unsqueeze(1).to_broadcast([128, 16, 16]), op=ALU.add),
                     reads=[Bq["idxf"], Bq["posf"]], writes=[Bq["cidx"]])
                candf = cand.rearrange("p a b -> p (a b)")
                cidxf = cidx.rearrange("p a b -> p (a b)")
                S.op("dve", lambda e: e.max(out=best[:, 0:8], in_=candf), reads=[Bq["cand"]], writes=[Bq["best"]])
                S.op("dve", lambda e: e.max_index(out=posu[:, 0:8], in_max=best[:, 0:8], in_values=candf),
                     reads=[Bq["cand"], Bq["best"]], writes=[Bq["posu"]])
                S.op("dve", lambda e: e.match_replace(out=work2, in_to_replace=best[:, 0:8], in_values=candf, imm_value=-1e30),
                     reads=[Bq["cand"], Bq["best"]], writes=[Bq["work2"]])
                S.op("dve", lambda e: e.max(out=best[:, 8:16], in_=work2), reads=[Bq["work2"]], writes=[Bq["best"]])
                S.op("dve", lambda e: e.max_index(out=posu[:, 8:16], in_max=best[:, 8:16], in_values=work2),
                     reads=[Bq["work2"], Bq["best"]], writes=[Bq["posu"]])
                S.op("dve", lambda e: e.tensor_copy(out=posf, in_=posu), reads=[Bq["posu"], Bq["cidx"]], writes=[Bq["posf"]])
                for k in range(16):
                    yield
                    S.op("dve", lambda e, h=h, k=k: e.scalar_tensor_tensor(out=work2, in0=iota_f, scalar=posf[:, k:k + 1],
                                                                         in1=cidxf, op0=ALU.is_equal, op1=ALU.mult,
                                                                         accum_out=ef[:, h * 16 + k:h * 16 + k + 1]),
                         reads=[Bq["posf"], Bq["cidx"], B_const, Bq["work2"]], writes=[Bq["work2"], Bq["ef"]])
                S.op("dve", lambda e: e.tensor_scalar_mul(out=smz[:, 0:1], in0=best[:, 0:1], scalar1=-1.0),
                     reads=[Bq["best"]], writes=[Bq["smz"]])
                S.op("act", lambda e, h=h: e.activation(out=gts_p[:, h * 16:(h + 1) * 16], in_=best, func=AF.Exp, bias=smz[:, 0:1],
                                                        accum_out=smz[:, 1:2]),
                     reads=[Bq["best"], Bq["smz"]], writes=[Bq["gts%d" % par], Bq["smz"]])
                S.op("dve", lambda e: e.reciprocal(out=smz[:, 2:3], in_=smz[:, 1:2]), reads=[Bq["smz"]], writes=[Bq["smz"]])
                S.op("dve", lambda e, h=h: e.tensor_scalar(out=gts_p[:, h * 16:(h + 1) * 16], in0=gts_p[:, h * 16:(h + 1) * 16],
                                                           scalar1=smz[:, 2:3], scalar2=None, op0=ALU.mult),
                     reads=[Bq["gts%d" % par], Bq["smz"]], writes=[Bq["gts%d" % par]])
            S.op("dve", lambda e: e.tensor_copy(out=ei_p, in_=ef), reads=[Bq["ef"]], writes=[Bq["ei%d" % par]])
            yield

        def stage2(i):
            par = i % 2
            r0, r1 = i * 128, (i + 1) * 128
            tt_p, xm_p, ei_p, gts_p = PARV["tt"][par], PARV["xm"][par], PARV["ei"][par], PARV["gts"][par]
            for m in range(128):
                j = ngbc[0] % NG
                ngbc[0] += 1
                yield
                S.dma("pool", lambda e, m=m, j=j: e.indirect_dma_start(
                    out=gring[j], out_offset=None, in_=EUb[:, :],
                    in_offset=bass.IndirectOffsetOnAxis(ap=ei_p[:, m:m + 1], axis=0)),
                    reads=[Bq["ei%d" % par]], writes=[B_gr[j]])
                S.op("dve", lambda e, m=m, j=j: e.scalar_tensor_tensor(out=junk2, in0=gring[j], scalar=1.0, in1=tt_p, op0=ALU.mult,
                                                                      op1=ALU.mult, accum_out=actv[:, m:m + 1]),
                     reads=[B_gr[j], Bq["tt%d" % par]], writes=[Bq["junk2"], Bq["actv"]])
            S.op("act", lambda e: e.activation(out=wgt, in_=actv, func=AF.Gelu), reads=[Bq["actv"]], writes=[Bq["wgt"]])
            S.op("dve", lambda e: e.tensor_tensor(out=wgt, in0=wgt, in1=gts_p, op=ALU.mult),
                 reads=[Bq["wgt"], Bq["gts%d" % par]], writes=[Bq["wgt"]])
            for m in range(128):
                j = ngbc[0] % NG
                ngbc[0] += 1
                jd = m % 4
                yield
                S.dma("pool", lambda e, m=m, j=j: e.indirect_dma_start(
                    out=gring[j], out_offset=None, in_=EVb[:, :],
                    in_offset=bass.IndirectOffsetOnAxis(ap=ei_p[:, m:m + 1], axis=0)),
                    reads=[Bq["ei%d" % par]], writes=[B_gr[j]])
                S.op("act", lambda e, m=m, jd=jd: e.activation(out=dgs[jd], in_=identF[:], func=AF.Copy, scale=wgt[:, m:m + 1]),
                     reads=[Bq["wgt"], B_const], writes=[B_dg[jd]])
                for half in range(2):
                    S.op("pe", lambda e, m=m, j=j, jd=jd, half=half: e.matmul(
                        psF[:, 4 + half, :], lhsT=dgs[jd], rhs=gring[j][:, half * 512:(half + 1) * 512],
                        start=(m == 0), stop=(m == 127)),
                        reads=[B_dg[jd], B_gr[j]], writes=[B_ps[4 + half]])
            S.op("dve", lambda e: e.tensor_tensor(out=xm_p[:, 0:512], in0=psF[:, 4, :], in1=xm_p[:, 0:512], op=ALU.add),
                 reads=[B_ps[4], Bq["xm%d" % par]], writes=[Bq["xm%d" % par]])
            S.op("dve", lambda e: e.tensor_tensor(out=xm_p[:, 512:1024], in0=psF[:, 5, :], in1=xm_p[:, 512:1024], op=ALU.add),
                 reads=[B_ps[5], Bq["xm%d" % par]], writes=[Bq["xm%d" % par]])
            S.dma("sp", lambda e, r0=r0, r1=r1: e.dma_start(out=out_d[r0:r1, :], in_=xm_p), reads=[Bq["xm%d" % par]])
            yield

        g1 = stage1(0)
        for _ in g1:
            pass
        for i in range(NSH):
            g2 = stage2(i)
            g1 = stage1(i + 1) if i + 1 < NSH else None
            while True:
                alive = False
                for _k in range(3):
                    try:
                        next(g2)
                        alive = True
                    except StopIteration:
                        break
                if g1 is not None:
                    try:
                        next(g1)
                        alive = True
                    except StopIteration:
                        g1 = None
                if not alive:
                    break
        S.barrier()
        S.emit()
    return nc


OFF_XBC, OFF_DT, OFF_QKV, OFF_GATE = 2048, 5120, 5184, 8256


def na_bias_tables(rpb, rows):
    NTl = rows // 2
    pats = [0, 1, 2, NTl - 2, NTl - 1]
    out = np.full((5, 128, 5, 16, 128), -30000.0, np.float32)
    for pi, t in enumerate(pats):
        kt0 = min(max(t - 2, 0), NTl - 5)
        for qi in range(128):
            r = 2 * t + qi // 64
            c = qi % 64
            r0 = min(max(r - 4, 0), rows - 8)
            c0 = min(max(c - 8, 0), 64 - 16)
            for kr in range(r0, r0 + 8):
                kb = kr // 2 - kt0
                assert 0 <= kb < 5
                kp0 = (kr % 2) * 64
                cols = np.arange(c0, c0 + 16)
                out[pi, kp0 + cols, kb, :, qi] = rpb[:, kr - r + 7, cols - c + 15].T
    return out.reshape(5, 128, 5 * 16 * 128)


def prep_common(inp, L):
    w_in = inp["w_in"][0]
    m = {}
    m["gmix"] = np.ascontiguousarray(inp["g_mix"][0].reshape(8, 128).T)
    colsF = np.concatenate([np.arange(OFF_XBC, OFF_XBC + 3072), np.arange(OFF_QKV, OFF_QKV + 2048)])
    wf = w_in[:, colsF].reshape(8, 128, NF, 128)
    m["wf"] = np.ascontiguousarray(wf.transpose(2, 1, 0, 3))
    colsT = np.concatenate([np.arange(0, 2048), np.arange(OFF_DT, OFF_DT + 64),
                            np.arange(OFF_QKV + 2048, OFF_QKV + 3072), np.arange(OFF_GATE, OFF_GATE + 2048)])
    wt = w_in[:, colsT].reshape(8, 128, NTC)
    m["wt"] = np.ascontiguousarray(wt.transpose(1, 0, 2))
    cw = inp["conv_w"][0]
    m["convw"] = np.ascontiguousarray(cw.reshape(5, 24, 128).transpose(2, 1, 0).reshape(128, 120))
    cb = inp["conv_b"][0]
    m["convbT"] = np.ascontiguousarray(cb.reshape(24, 128).T)
    m["convbrow"] = np.ascontiguousarray(cb.reshape(1, 3072))
    m["dtb"] = np.ascontiguousarray(np.broadcast_to(inp["dt_bias"][0].reshape(1, 64), (128, 64)))
    m["alog"] = np.ascontiguousarray(np.broadcast_to(inp["a_log"][0].reshape(1, 64), (128, 64)))
    m["dskip"] = np.ascontiguousarray(np.broadcast_to(np.repeat(inp["d_skip"][0], 64).reshape(1, 2048), (128, 2048)))
    m["sng"] = np.ascontiguousarray(np.broadcast_to(inp["ssd_norm_g"][0].reshape(1, 2048), (128, 2048)))
    m["qg"] = np.ascontiguousarray(np.tile(inp["q_norm_g"][0], 2).reshape(128, 1))
    m["kg"] = np.ascontiguousarray(np.tile(inp["k_norm_g"][0], 2).reshape(128, 1))
    m["bt"] = na_bias_tables(inp["rpb"][0], L // 64)
    m["wo"] = np.ascontiguousarray(inp["w_out"][0].reshape(24, 128, 1024).transpose(1, 0, 2))
    m["gffn"] = np.ascontiguousarray(np.broadcast_to(inp["g_ffn"][0].reshape(1, 1024), (128, 1024)))
    m["wpq"] = np.ascontiguousarray(inp["w_pq"][0].reshape(8, 128, 16, 128).transpose(2, 1, 0, 3))
    m["skT"] = np.ascontiguousarray(inp["sub_keys"][0].transpose(2, 0, 1))
    m["eu"] = np.ascontiguousarray(inp["expert_u"][0])
    m["ev"] = np.ascontiguousarray(inp["expert_v"][0])
    return {k: np.asarray(v, np.float32) for k, v in m.items()}


_NC_CACHE = {}


def core_shares(NT):
    groups = {0: [0, 3, 6], 1: [1, 4, 7], 2: [2, 5]}
    shares = {}
    for sq, cores in groups.items():
        parts = np.array_split(np.arange(NT), len(cores))
        for c, p in zip(cores, parts):
            shares[c] = [int(v) for v in p]
    NSH = (NT + 1) // 2
    return shares, NSH


def share_rowidx(tiles, NSH):
    tl = list(tiles) + [tiles[-1]] * (NSH - len(tiles))
    idx = np.array(tl, np.int32)[None, :] * 128 + np.arange(128, dtype=np.int32)[:, None]
    return np.ascontiguousarray(idx.astype(np.int32))


def kernel(**inp):
    L = inp["x_prompt"].shape[1]
    seqs = [inp["x_prompt"][0], inp["x_prompt"][1], inp["x_sample"][0]]
    common = prep_common(inp, L)
    if L not in _NC_CACHE:
        _NC_CACHE[L] = build(L)
    nc = _NC_CACHE[L]
    shares, NSH = core_shares(L // 128)
    in_maps = []
    for c in range(8):
        m = dict(common)
        m["x"] = np.ascontiguousarray(seqs[c % 3], dtype=np.float32)
        m["rowidx"] = share_rowidx(shares[c], NSH)
        in_maps.append(m)
    res = run_bass_kernel_spmd(nc, in_maps, core_ids=list(range(8)))
    outs = [np.empty((L, D), np.float32) for _ in range(3)]
    for c in range(8):
        o = np.asarray(res.results[c]["out"], np.float32)
        for i, t in enumerate(shares[c]):
            outs[c % 3][t * 128:(t + 1) * 128] = o[i * 128:(i + 1) * 128]
    return (np.stack(outs[0:2], 0), outs[2][None])
```
